# Optimizing a Trainium2 kernel written in Bass

```python
import math
import jax, jax.numpy as jnp
from jax import lax
import numpy as np

D_MODEL = 1024
BATCH = 8
SEQ = 2048
DEPTH = 2
DEC_BATCH = 128
DEC_SEQ = 8
PAST_LEN = 2048
PAGE_SIZE = 128

N_A_LAYERS = DEPTH // 2
N_B_LAYERS = DEPTH - N_A_LAYERS
SSM_GROUP = 16
N_GROUPS = D_MODEL // SSM_GROUP
STATE_DIM = 64
DT_MIN = 1e-3
DT_MAX = 1e-1
N_HEADS = 16
HEAD_DIM = D_MODEL // N_HEADS
ROT_DIM = HEAD_DIM // 4
ROPE_THETA = 500000.0
MOBA_BLOCK = 256
MOBA_TOPK = 3
Q_CHUNK = 128
D_FF = 4 * D_MODEL
EPS = 1e-6
NEG_INF = -1e30

kernel_name = 'yoco_s5_moba_step'


def rmsnorm(x, g):
    xf = x.astype(jnp.float32)
    y = xf * lax.rsqrt(jnp.mean(xf * xf, axis=-1, keepdims=True) + EPS)
    return (y * g.astype(jnp.float32)).astype(x.dtype)


def partial_rope(x, pos):
    half = ROT_DIM // 2
    inv = ROPE_THETA ** (-jnp.arange(half, dtype=jnp.float32) / half)
    ang = pos.astype(jnp.float32)[:, None] * inv[None, :]
    cos = jnp.cos(ang)[None, :, None, :]
    sin = jnp.sin(ang)[None, :, None, :]
    xr = x[..., :ROT_DIM].astype(jnp.float32)
    x1, x2 = xr[..., :half], xr[..., half:]
    rot = jnp.concatenate([x1 * cos - x2 * sin, x2 * cos + x1 * sin], axis=-1)
    return jnp.concatenate([rot.astype(x.dtype), x[..., ROT_DIM:]], axis=-1)


def s5_ssm(u, h0_re, h0_im, lam_re, lam_im, log_dt, b_re, b_im, c_re, c_im, d_skip):
    f32 = jnp.float32
    Bt, T, _ = u.shape
    lr, li = lam_re.astype(f32), lam_im.astype(f32)
    dt = jnp.exp(log_dt.astype(f32))[:, None]
    mag = jnp.exp(lr * dt)
    ab_re, ab_im = mag * jnp.cos(li * dt), mag * jnp.sin(li * dt)
    nr, ni = ab_re - 1.0, ab_im
    den = lr * lr + li * li
    f_re = (nr * lr + ni * li) / den
    f_im = (ni * lr - nr * li) / den
    br, bi = b_re.astype(f32), b_im.astype(f32)
    bb_re = f_re[..., None] * br - f_im[..., None] * bi
    bb_im = f_re[..., None] * bi + f_im[..., None] * br
    ug = u.astype(f32).reshape(Bt, T, N_GROUPS, SSM_GROUP)
    bu_re = jnp.einsum('btgc,gpc->tbgp', ug, bb_re)
    bu_im = jnp.einsum('btgc,gpc->tbgp', ug, bb_im)
    if h0_re is not None:
        hr, hi = h0_re.astype(f32), h0_im.astype(f32)
        bu_re = bu_re.at[0].add(ab_re * hr - ab_im * hi)
        bu_im = bu_im.at[0].add(ab_re * hi + ab_im * hr)
    a_re = jnp.broadcast_to(ab_re, (T, 1, N_GROUPS, STATE_DIM))
    a_im = jnp.broadcast_to(ab_im, (T, 1, N_GROUPS, STATE_DIM))

    def combine(e1, e2):
        a1r, a1i, b1r, b1i = e1
        a2r, a2i, b2r, b2i = e2
        return (a1r * a2r - a1i * a2i, a1r * a2i + a1i * a2r,
                a2r * b1r - a2i * b1i + b2r, a2r * b1i + a2i * b1r + b2i)

    _, _, xs_re, xs_im = lax.associative_scan(combine, (a_re, a_im, bu_re, bu_im), axis=0)
    y = (jnp.einsum('gcp,tbgp->btgc', c_re.astype(f32), xs_re)
         - jnp.einsum('gcp,tbgp->btgc', c_im.astype(f32), xs_im))
    y = y.reshape(Bt, T, D_MODEL) + d_skip.astype(f32) * u.astype(f32)
    return y, xs_re[-1], xs_im[-1]


def shared_kv(h, pos, kv_norm, w_kv, k_norm):
    Bt, T, _ = h.shape
    kv = rmsnorm(h, kv_norm) @ w_kv
    k = kv[..., :N_HEADS * HEAD_DIM].reshape(Bt, T, N_HEADS, HEAD_DIM)
    v = kv[..., N_HEADS * HEAD_DIM:].reshape(Bt, T, N_HEADS, HEAD_DIM)
    k = partial_rope(rmsnorm(k, k_norm), pos)
    return k, v


def moba_attend_seq(q, qpos, k, v):
    L = k.shape[0]
    nb = -(-L // MOBA_BLOCK)
    pad = nb * MOBA_BLOCK - L
    k = jnp.pad(k, ((0, pad), (0, 0), (0, 0)))
    v = jnp.pad(v, ((0, pad), (0, 0), (0, 0)))
    kb = k.reshape(nb, MOBA_BLOCK, N_HEADS, HEAD_DIM).transpose(2, 0, 1, 3)
    vb = v.reshape(nb, MOBA_BLOCK, N_HEADS, HEAD_DIM).transpose(2, 0, 1, 3)
    kmean = jnp.mean(kb.astype(jnp.float32), axis=2)
    n_sel = min(MOBA_TOPK, nb)
    head_idx = jnp.arange(N_HEADS)[None, :, None]
    scale = HEAD_DIM ** -0.5

    def attend(args):
        qc, pc = args
        c = qc.shape[0]
        qf = qc.astype(jnp.float32)
        own = pc // MOBA_BLOCK
        gate = jnp.einsum('chd,hnd->chn', qf, kmean)
        is_past = jnp.arange(nb)[None, None, :] < own[:, None, None]
        gate = jnp.where(is_past, gate, NEG_INF)
        _, sel = lax.top_k(gate, n_sel)
        blocks = jnp.concatenate(
            [jnp.broadcast_to(own[:, None, None], (c, N_HEADS, 1)), sel], axis=-1)
        slot_ok = jnp.concatenate(
            [jnp.ones((c, N_HEADS, 1), dtype=bool), sel < own[:, None, None]], axis=-1)
        kg = kb[head_idx, blocks].astype(jnp.float32)
        vg = vb[head_idx, blocks].astype(jnp.float32)
        kpos = blocks[..., None] * MOBA_BLOCK + jnp.arange(MOBA_BLOCK)
        valid = slot_ok[..., None] & (kpos <= pc[:, None, None, None])
        s = jnp.einsum('chd,chsnd->chsn', qf, kg) * scale
        s = jnp.where(valid, s, NEG_INF)
        p = jax.nn.softmax(s.reshape(c, N_HEADS, -1), axis=-1).reshape(s.shape)
        o = jnp.einsum('chsn,chsnd->chd', p, vg)
        return o.astype(qc.dtype)

    T = q.shape[0]
    if T > Q_CHUNK and T % Q_CHUNK == 0:
        o = lax.map(attend, (q.reshape(T // Q_CHUNK, Q_CHUNK, N_HEADS, HEAD_DIM),
                             qpos.reshape(T // Q_CHUNK, Q_CHUNK)))
        return o.reshape(T, N_HEADS, HEAD_DIM)
    return attend((q, qpos))


def run_trunk(x, pos, h0_re, h0_im, attend, p):
    h = x
    fin_re, fin_im = [], []
    k_sh, v_sh = None, None
    Bt, T, _ = x.shape
    for layer in range(DEPTH):
        if layer < N_A_LAYERS:
            a = layer
            u = rmsnorm(h, p['ssm_norm'][a])
            y, sr, si = s5_ssm(u,
                               None if h0_re is None else h0_re[a],
                               None if h0_im is None else h0_im[a],
                               p['ssm_lambda_re'][a], p['ssm_lambda_im'][a], p['ssm_log_dt'][a],
                               p['ssm_b_re'][a], p['ssm_b_im'][a], p['ssm_c_re'][a], p['ssm_c_im'][a],
                               p['ssm_d'][a])
            y = jax.nn.gelu(y).astype(h.dtype)
            z = y @ p['ssm_w_glu'][a]
            h = h + z[..., :D_MODEL] * jax.nn.sigmoid(z[..., D_MODEL:])
            fin_re.append(sr)
            fin_im.append(si)
        else:
            b = layer - N_A_LAYERS
            if k_sh is None:
                k_sh, v_sh = shared_kv(h, pos, p['kv_norm'], p['w_kv'], p['k_norm'])
            u = rmsnorm(h, p['attn_norm'][b])
            q = (u @ p['w_q'][b]).reshape(Bt, T, N_HEADS, HEAD_DIM)
            q = partial_rope(rmsnorm(q, p['q_norm'][b]), pos)
            o = attend(q, k_sh, v_sh)
            h = h + o.reshape(Bt, T, N_HEADS * HEAD_DIM) @ p['w_o'][b]
        u = rmsnorm(h, p['mlp_norm'][layer])
        h = h + jnp.square(jax.nn.relu(u @ p['w_up'][layer])) @ p['w_down'][layer]
    return h, k_sh, v_sh, jnp.stack(fin_re), jnp.stack(fin_im)


def setup_inputs(seed: int = 0) -> dict:
    key = jax.random.key(seed)
    ks = jax.random.split(key, 32)
    f32 = jnp.float32
    n_pages = PAST_LEN // PAGE_SIZE
    n_used = DEC_BATCH * n_pages
    n_pool = n_used + n_used // 4
    G, P, C = N_GROUPS, STATE_DIM, SSM_GROUP
    HD = N_HEADS * HEAD_DIM

    def nrm(k, shape, scale):
        return jax.random.normal(k, shape, f32) * scale

    def gain(k, shape):
        return 1.0 + 0.02 * jax.random.normal(k, shape, f32)

    n_idx = jnp.arange(P, dtype=f32)
    page_table = jax.random.permutation(ks[6], n_pool)[:n_used].reshape(DEC_BATCH, n_pages).astype(jnp.int32)
    return {
        'x_prompt': nrm(ks[0], (BATCH, SEQ, D_MODEL), 1.0),
        'x_sample': nrm(ks[1], (DEC_BATCH, DEC_SEQ, D_MODEL), 1.0),
        'state_ssm_re': nrm(ks[2], (N_A_LAYERS, DEC_BATCH, G, P), 0.1),
        'state_ssm_im': nrm(ks[3], (N_A_LAYERS, DEC_BATCH, G, P), 0.1),
        'cache_k': nrm(ks[4], (n_pool, PAGE_SIZE, N_HEADS, HEAD_DIM), 1.0),
        'cache_v': nrm(ks[5], (n_pool, PAGE_SIZE, N_HEADS, HEAD_DIM), 1.0),
        'page_table': page_table,
        'ssm_norm': gain(ks[7], (N_A_LAYERS, D_MODEL)),
        'ssm_lambda_re': -0.5 + 0.01 * jax.random.normal(ks[8], (N_A_LAYERS, G, P), f32),
        'ssm_lambda_im': math.pi * n_idx + 0.01 * jax.random.normal(ks[9], (N_A_LAYERS, G, P), f32),
        'ssm_log_dt': jax.random.uniform(ks[10], (N_A_LAYERS, G), f32, math.log(DT_MIN), math.log(DT_MAX)),
        'ssm_b_re': nrm(ks[11], (N_A_LAYERS, G, P, C), (2 * C) ** -0.5),
        'ssm_b_im': nrm(ks[12], (N_A_LAYERS, G, P, C), (2 * C) ** -0.5),
        'ssm_c_re': nrm(ks[13], (N_A_LAYERS, G, C, P), P ** -0.5),
        'ssm_c_im': nrm(ks[14], (N_A_LAYERS, G, C, P), P ** -0.5),
        'ssm_d': nrm(ks[15], (N_A_LAYERS, D_MODEL), 1.0),
        'ssm_w_glu': nrm(ks[16], (N_A_LAYERS, D_MODEL, 2 * D_MODEL), D_MODEL ** -0.5),
        'kv_norm': gain(ks[17], (D_MODEL,)),
        'w_kv': nrm(ks[18], (D_MODEL, 2 * HD), D_MODEL ** -0.5),
        'k_norm': gain(ks[19], (HEAD_DIM,)),
        'attn_norm': gain(ks[20], (N_B_LAYERS, D_MODEL)),
        'w_q': nrm(ks[21], (N_B_LAYERS, D_MODEL, HD), D_MODEL ** -0.5),
        'q_norm': gain(ks[22], (N_B_LAYERS, HEAD_DIM)),
        'w_o': nrm(ks[23], (N_B_LAYERS, HD, D_MODEL), HD ** -0.5),
        'mlp_norm': gain(ks[24], (DEPTH, D_MODEL)),
        'w_up': nrm(ks[25], (DEPTH, D_MODEL, D_FF), D_MODEL ** -0.5),
        'w_down': nrm(ks[26], (DEPTH, D_FF, D_MODEL), D_FF ** -0.5),
    }


def reference(x_prompt, x_sample, state_ssm_re, state_ssm_im, cache_k, cache_v, page_table,
              ssm_norm, ssm_lambda_re, ssm_lambda_im, ssm_log_dt, ssm_b_re, ssm_b_im,
              ssm_c_re, ssm_c_im, ssm_d, ssm_w_glu, kv_norm, w_kv, k_norm,
              attn_norm, w_q, q_norm, w_o, mlp_norm, w_up, w_down):
    p = dict(ssm_norm=ssm_norm, ssm_lambda_re=ssm_lambda_re, ssm_lambda_im=ssm_lambda_im,
             ssm_log_dt=ssm_log_dt, ssm_b_re=ssm_b_re, ssm_b_im=ssm_b_im,
             ssm_c_re=ssm_c_re, ssm_c_im=ssm_c_im, ssm_d=ssm_d, ssm_w_glu=ssm_w_glu,
             kv_norm=kv_norm, w_kv=w_kv, k_norm=k_norm, attn_norm=attn_norm, w_q=w_q,
             q_norm=q_norm, w_o=w_o, mlp_norm=mlp_norm, w_up=w_up, w_down=w_down)

    pos_p = jnp.arange(SEQ, dtype=jnp.int32)

    def prompt_attend(q, k, v):
        return lax.map(lambda a: moba_attend_seq(a[0], pos_p, a[1], a[2]), (q, k, v))

    y_prompt, k_prompt, v_prompt, ssm_re_prompt, ssm_im_prompt = run_trunk(
        x_prompt, pos_p, None, None, prompt_attend, p)

    pos_s = PAST_LEN + jnp.arange(DEC_SEQ, dtype=jnp.int32)

    def sample_attend(q, k, v):
        def one(a):
            qb, kn, vn, pt = a
            kp = cache_k[pt].reshape(-1, N_HEADS, HEAD_DIM).astype(kn.dtype)
            vp = cache_v[pt].reshape(-1, N_HEADS, HEAD_DIM).astype(vn.dtype)
            return moba_attend_seq(qb, pos_s, jnp.concatenate([kp, kn], axis=0),
                                   jnp.concatenate([vp, vn], axis=0))
        return lax.map(one, (q, k, v, page_table))

    y_sample, k_sample, v_sample, ssm_re_sample, ssm_im_sample = run_trunk(
        x_sample, pos_s, state_ssm_re, state_ssm_im, sample_attend, p)

    return (y_prompt, y_sample, k_prompt, v_prompt, k_sample, v_sample,
            ssm_re_prompt, ssm_im_prompt, ssm_re_sample, ssm_im_sample)
```

```python
import numpy as np
from contextlib import ExitStack
import concourse.bass as bass
import concourse.mybir as mybir
from concourse.bass_utils import run_bass_kernel_spmd

F32 = mybir.dt.float32
BF16 = mybir.dt.bfloat16
I32 = mybir.dt.int32
ALU = mybir.AluOpType
AF = mybir.ActivationFunctionType
AX = mybir.AxisListType

NCORE = 8
D = 1024
G, PST, CG = 64, 64, 16
NJ = 32
EPS = 1e-6


class Prog:
    def __init__(self, nc):
        self.nc = nc
        self.ops = []
        self.last_w = {}
        self.readers = {}

    def add(self, eng, fn, reads=(), writes=(), dma=None):
        deps = set()
        for r in reads:
            if r in self.last_w:
                deps.add(self.last_w[r])
        for w in writes:
            if w in self.last_w:
                deps.add(self.last_w[w])
            deps |= set(self.readers.get(w, ()))
        i = len(self.ops)
        self.ops.append(dict(eng=eng, fn=fn, deps=deps, dma=dma))
        for r in reads:
            self.readers.setdefault(r, []).append(i)
        for w in writes:
            self.last_w[w] = i
            self.readers[w] = []
        return i

    def _bar_fn(self, e):
        return e.memset(self.bar_ap, 0.0)

    def barrier(self, new_names, dummy_fn=None):
        names = set(self.last_w) | set(self.readers) | set(new_names)
        self.add('dve', self._bar_fn, reads=(), writes=list(names))

    def emit(self, semctx, pool_dummy, counts):
        nc = self.nc
        ops = self.ops
        engs = ['sync', 'act', 'dve', 'pool', 'pe']
        slot_cnt = {}
        dmaval = [None] * len(ops)
        for i, o in enumerate(ops):
            if o['dma'] is not None:
                slot, n = o['dma']
                slot_cnt[slot] = slot_cnt.get(slot, 0) + n
                dmaval[i] = ('dma:' + slot, 16 * slot_cnt[slot])
        semnames = list(engs) + ['dma:' + s for s in slot_cnt]
        sems = {n: semctx(n) for n in semnames}
        per_eng = {e: [] for e in engs}
        for i, o in enumerate(ops):
            per_eng[o['eng']].append(i)
        cum = [0] * len(ops)

        class Proxy:
            def __init__(pself, e, ename):
                pself.e = e
                pself.ename = ename
                pself.count = 0
                pself.selfsync = ename in ('act', 'dve', 'pool')
                pself.nosync = False

            def __getattr__(pself, name):
                f = getattr(pself.e, name)
                if name in ('wait_ge', 'dma_start', 'indirect_dma_start'):
                    return f
                if name in ('nosync',):
                    return object.__getattribute__(pself, name)

                def w(*a, **k):
                    if pself.selfsync and pself.count > 0 and not pself.nosync:
                        pself.e.wait_ge(sems[pself.ename], pself.count)
                    ins = f(*a, **k)
                    ins.then_inc(sems[pself.ename], 1)
                    pself.count += 1
                    return ins
                return w

        def run_engine(ename, e):
            waited = {}
            px = Proxy(e, ename)
            for i in per_eng[ename]:
                o = ops[i]
                nw = 0
                for d in sorted(o['deps']):
                    od = ops[d]
                    if od['dma'] is None and od['eng'] == ename:
                        continue
                    nw += 1
                    if od['dma'] is not None:
                        sn, v = dmaval[d]
                    else:
                        sn, v = od['eng'], (counts[d] if counts is not None else 1)
                    if waited.get(sn, 0) >= v:
                        continue
                    waited[sn] = v
                    e.wait_ge(sems[sn], v)
                if ename == 'pool' and o['dma'] is not None and nw > 0:
                    pool_dummy(px)
                r = o['fn'](px)
                if o['dma'] is not None:
                    sn, v = dmaval[i]
                    assert len(r) == o['dma'][1], (len(r), o['dma'])
                    for ins in r:
                        ins.then_inc(sems[sn], 16)
                cum[i] = px.count
            if ename == 'sync':
                for s, c in slot_cnt.items():
                    e.wait_ge(sems['dma:' + s], 16 * c)

        with nc.Block() as block:
            @block.sync
            def _(e):
                run_engine('sync', e)

            @block.scalar
            def _(e):
                run_engine('act', e)

            @block.vector
            def _(e):
                run_engine('dve', e)

            @block.gpsimd
            def _(e):
                run_engine('pool', e)

            @block.tensor
            def _(e):
                run_engine('pe', e)
        if counts is not None:
            assert counts == cum
        return cum


NPOOL = 2560
STAGE = 5


def build_nc(counts=None):
    nc = bass.Bass("TRN2", target_bir_lowering=False)

    def din(name, shape, dt=F32):
        return nc.dram_tensor(name, list(shape), dt, kind="ExternalInput").ap()

    def dout(name, shape, dt=F32):
        return nc.dram_tensor(name, list(shape), dt, kind="ExternalOutput").ap()

    xp = din("xp", [2048, D])
    xs = din("xs", [128, D])
    wb_d = din("wb", [128, NJ * 2 * 128])
    cx_d = din("cx", [128, NJ * 2 * 128])
    lam_d = din("lam", [128, 3 * NJ])
    h0_d = din("h0", [128, NJ * 2 * 16])
    cols_d = din("cols", [128, 48])
    knb_d = din("knb", [128, 128])
    rope_d = din("rope", [128, 17 * 16])
    wglu_d = din("wglu", [128, 8 * 2048])
    wkv_d = din("wkv", [128, 8 * 2048])
    wq_d = din("wq", [128, 8 * 1024])
    ck_d = din("ck", [NPOOL * 128, 1024])
    cv_d = din("cv", [NPOOL * 128, 1024])
    pt_d = din("pt", [1, 256], I32)
    wo_d = din("wo", [128, 8 * 1024])
    wup_d = [din("wup%d" % l, [128, 8 * 4096]) for l in range(2)]
    wdn_d = [din("wdn%d" % l, [128, 32 * 1024]) for l in range(2)]
    st_p = dout("st_p", [128, NJ * 2])
    st_s = dout("st_s", [128, NJ * 2 * 16])
    k_p = dout("k_p", [2048, D])
    v_p = dout("v_p", [2048, D])
    k_s = dout("k_s", [128, D])
    v_s = dout("v_s", [128, D])
    y_p = dout("y_p", [2048, D])
    y_s = dout("y_s", [128, D])

    with ExitStack() as es:
        def sb(name, shape, dt=F32):
            return es.enter_context(nc.sbuf_tensor("s_" + name, list(shape), dt))

        def ps(name, shape, dt=F32):
            return es.enter_context(nc.psum_tensor("q_" + name, list(shape), dt))

        P = Prog(nc)

        h = sb("h", [128, 17, D])
        arena = sb("arena", [128, 16384], BF16)
        arena2 = sb("arena2", [128, 16896])
        Bk = [ps("bank%d" % i, [128, 512]) for i in range(8)]
        identf = sb("identf", [128, 128])
        ident = sb("ident", [128, 128], BF16)
        dummy = sb("dummy_t", [128, 8])
        P.bar_ap = dummy[:, 1:2]
        lam = sb("lam", [128, 3 * NJ])
        cols = sb("cols", [128, 48])
        knb = sb("knb", [128, 128])
        rope = sb("rope", [128, 17, 16])
        NPW = 9
        apr = sb("apr", [128, NPW, NJ])
        api = sb("api", [128, NPW, NJ])
        apn = sb("apn", [128, NPW, NJ])
        fr = sb("fr", [128, NJ])
        fi = sb("fi", [128, NJ])
        tsm = [sb("tsm%d" % i, [128, NJ]) for i in range(10)]
        xend = sb("xend", [128, NJ, 2])
        scr = sb("scr", [128, 6144])

        def scb(lo, hi):
            return scr[:, lo:hi].bitcast(BF16)
        uT = scb(0, 2048).rearrange("p (k n) -> p k n", k=8)
        xrb = [scb(2048 + 256 * i, 2304 + 256 * i) for i in range(2)]
        xib = [scb(2560 + 256 * i, 2816 + 256 * i) for i in range(2)]
        junk2 = scr[:, 3072:4096]
        tmpc = scr[:, 4096:4608]
        ysb = scr[:, 4608:5120]
        aT = scb(5120, 5632).rearrange("p (f n) -> p f n", f=4)
        rl = scr[:, 5632:5888]
        junk = sb("junk", [128, D])
        xn = sb("xn", [128, D], BF16)
        ss = sb("ss", [128, 4])
        ssh = sb("ssh", [128, 4, 16])
        KTs = scb(5088, 5600).rearrange("p (k n) -> p k n", k=8)
        Vs_bf = scb(5600, 6112)

        def a2b(lo, hi):
            return arena2[:, lo:hi].bitcast(BF16)
        wb = a2b(0, 4096).rearrange("p (a b c) -> p a b c", a=NJ, b=2)
        cx = a2b(4096, 8192).rearrange("p (a b c) -> p a b c", a=NJ, b=2)
        bufs = [[arena2[:, 8192 + (pp * 2 + ri) * 768: 8192 + (pp * 2 + ri + 1) * 768] for ri in range(2)]
                for pp in range(2)]
        sbufS = [[arena2[:, 11264 + (pp * 2 + ri) * 192: 11264 + (pp * 2 + ri + 1) * 192]
                  .rearrange("p (b t) -> p b t", t=12) for ri in range(2)] for pp in range(2)]
        ygT = a2b(12032, 14080).rearrange("p (k n) -> p k n", k=8)
        h0 = arena2[:, 14080:15104].rearrange("p (a b c) -> p a b c", a=NJ, b=2)
        st_s_sb = arena2[:, 15104:16128].rearrange("p (a b c) -> p a b c", a=NJ, b=2)
        uT_all = a2b(0, 8704).rearrange("p (k n) -> p k n", k=8)
        KT = a2b(0, 8192).rearrange("p (k n) -> p k n", k=8)
        Vp = a2b(8192, 16512).rearrange("p (t h d) -> p t h d", t=16, h=16)
        rows_i = arena2[:, 8812:9068].bitcast(I32)
        pt_i = arena2[:, 9068:9324].bitcast(I32)
        S5RES = ['wb', 'cx', 'bufA', 'bufB', 'ygT', 'h0', 'res']
        wglu = arena[:, :].rearrange("p (k n) -> p k n", k=8)
        wkv = wglu

        def wup_v(s):
            return arena[:, s * 8192: s * 8192 + 4096].rearrange("p (k n) -> p k n", k=8)

        def wdn_v(s):
            return arena[:, s * 8192 + 4096: s * 8192 + 8192].rearrange("p (f n) -> p f n", f=4)

        pPr, pPi = Bk[0], Bk[1]
        pT = Bk[2][:, :].bitcast(BF16).rearrange("p (k n) -> p k n", k=8)
        pT2 = Bk[3][:, :].bitcast(BF16).rearrange("p (k n) -> p k n", k=8)
        yT = Bk[3]
        zb = Bk[4:8]

        def mkident(e):
            e.memset(identf[:], 0.0)
            e.memset(dummy[:], 0.0)
            return e.affine_select(identf[:], identf[:], [[-1, 128]], ALU.not_equal, 1.0,
                                   base=0, channel_multiplier=1)
        P.add('pool', mkident, writes=['identf'])
        P.add('dve', lambda e: e.tensor_copy(ident[:], identf[:]), reads=['identf'], writes=['ident'])

        P.add('sync', lambda e: [e.dma_start(out=lam[:], in_=lam_d[:, :]),
                                 e.dma_start(out=cols[:], in_=cols_d[:, :]),
                                 e.dma_start(out=knb[:], in_=knb_d[:, :]),
                                 e.dma_start(out=rope[:].rearrange("p a b -> p (a b)"), in_=rope_d[:, :]),
                                 e.dma_start(out=h0.rearrange("p a b c -> p (a b c)"), in_=h0_d[:, :])],
              writes=['lam', 'cols', 'knb', 'rope', 'h0'], dma=('params', 5))
        P.add('pool', lambda e: [e.dma_start(out=wb[:, 8 * q:8 * q + 8].rearrange("p a b c -> p (a b c)"),
                                             in_=wb_d[:, 2048 * q:2048 * (q + 1)]) for q in range(4)],
              writes=['wb'], dma=('wbl', 4))
        P.add('pool', lambda e: [e.dma_start(out=cx[:, 8 * q:8 * q + 8].rearrange("p a b c -> p (a b c)"),
                                             in_=cx_d[:, 2048 * q:2048 * (q + 1)]) for q in range(4)],
              writes=['cx'], dma=('cx', 4))
        if STAGE >= 2:
            P.add('pool', lambda e: [e.dma_start(out=wglu[:, kc, :], in_=wglu_d[:, 2048 * kc:2048 * (kc + 1)])
                                     for kc in range(8)], writes=['arena'], dma=('arena', 8))
        for t in range(16):
            P.add('sync', (lambda t: lambda e: [e.dma_start(out=h[:, t, :], in_=xp[t * 128:(t + 1) * 128, :])])(t),
                  writes=[('h', t)], dma=('h%d' % t, 1))
        P.add('sync', lambda e: [e.dma_start(out=h[:, 16, :], in_=xs[:, :])], writes=[('h', 16)], dma=('h16', 1))

        lr, li, ld = lam[:, 0:NJ], lam[:, NJ:2 * NJ], lam[:, 2 * NJ:3 * NJ]
        dtv, rho, th, mag, kf, r_, s2, s4, c2, den = [t[:] for t in tsm]
        TWO_PI = float(2 * np.pi)
        MAGIC = 12582912.0
        P.add('act', lambda e: e.activation(dtv, ld, AF.Exp), reads=['lam'], writes=['dtv'])
        P.add('dve', lambda e: e.tensor_tensor(rho, lr, dtv, ALU.mult), reads=['lam', 'dtv'], writes=['rho'])
        P.add('dve', lambda e: e.tensor_tensor(th, li, dtv, ALU.mult), reads=['lam', 'dtv'], writes=['th'])
        P.add('act', lambda e: e.activation(mag, rho, AF.Exp), reads=['rho'], writes=['mag'])

        def rr(e):
            e.tensor_scalar(kf, th, 1.0 / TWO_PI, None, ALU.mult)
            e.tensor_scalar(kf, kf, MAGIC, None, ALU.add)
            e.tensor_scalar(kf, kf, -MAGIC, None, ALU.add)
            return e.scalar_tensor_tensor(r_, kf, -TWO_PI, th, ALU.mult, ALU.add)
        P.add('dve', rr, reads=['th'], writes=['r', 'kf'])
        P.add('act', lambda e: e.activation(s2, r_, AF.Sin, scale=0.5), reads=['r'], writes=['s2'])
        P.add('act', lambda e: e.activation(s4, r_, AF.Sin, scale=0.25), reads=['r'], writes=['s4'])

        def trig(e):
            e.tensor_tensor(c2, s4, s4, ALU.mult)
            e.tensor_scalar(c2, c2, -2.0, 1.0, ALU.mult, ALU.add)
            e.tensor_tensor(den, s2, s2, ALU.mult)
            e.tensor_scalar(den, den, -2.0, 1.0, ALU.mult, ALU.add)
            e.tensor_tensor(apr[:, 0, :], den, mag, ALU.mult)
            e.tensor_tensor(c2, c2, s2, ALU.mult)
            e.scalar_tensor_tensor(api[:, 0, :], c2, 2.0, mag, ALU.mult, ALU.mult)
            nr, tmp1, tmp2 = kf, r_, th
            e.tensor_scalar(nr, apr[:, 0, :], -1.0, None, ALU.add)
            e.tensor_tensor(den, lr, lr, ALU.mult)
            e.tensor_tensor(tmp1, li, li, ALU.mult)
            e.tensor_tensor(den, den, tmp1, ALU.add)
            e.reciprocal(den, den)
            e.tensor_tensor(tmp1, nr, lr, ALU.mult)
            e.tensor_tensor(tmp2, api[:, 0, :], li, ALU.mult)
            e.tensor_tensor(tmp1, tmp1, tmp2, ALU.add)
            e.tensor_tensor(fr[:], tmp1, den, ALU.mult)
            e.tensor_tensor(tmp1, api[:, 0, :], lr, ALU.mult)
            e.tensor_tensor(tmp2, nr, li, ALU.mult)
            e.tensor_tensor(tmp1, tmp1, tmp2, ALU.subtract)
            e.tensor_tensor(fi[:], tmp1, den, ALU.mult)
            for k in range(1, NPW):
                e.tensor_tensor(tmp1, apr[:, k - 1, :], apr[:, k - 1, :], ALU.mult)
                e.tensor_tensor(tmp2, api[:, k - 1, :], api[:, k - 1, :], ALU.mult)
                e.tensor_tensor(apr[:, k, :], tmp1, tmp2, ALU.subtract)
                e.scalar_tensor_tensor(api[:, k, :], apr[:, k - 1, :], 2.0, api[:, k - 1, :], ALU.mult, ALU.mult)
            return e.tensor_scalar(apn[:].rearrange("p a b -> p (a b)"), api[:].rearrange("p a b -> p (a b)"),
                                   -1.0, None, ALU.mult)
        P.add('dve', trig, reads=['s2', 's4', 'mag', 'lam', 'r', 'th', 'kf'], writes=['apw', 'f', 'r', 'th', 'kf'])

        def rms_to_uT(tt, dst, col0, gcol, dst_res, extra_w=()):
            def f1(e):
                e.scalar_tensor_tensor(junk[:], h[:, tt, :], 1.0, h[:, tt, :], ALU.mult, ALU.mult,
                                       accum_out=ss[:, 0:1])
                return e.tensor_scalar(ss[:, 1:2], ss[:, 0:1], 1.0 / D, EPS, ALU.mult, ALU.add)
            P.add('dve', f1, reads=[('h', tt)], writes=['ss', 'junk'])
            P.add('act', lambda e: e.activation(ss[:, 2:3], ss[:, 1:2], AF.Sqrt), reads=['ss'], writes=['ss2'])

            def f2(e):
                e.reciprocal(ss[:, 3:4], ss[:, 2:3])
                return e.tensor_scalar(xn[:], h[:, tt, :], ss[:, 3:4], None, ALU.mult)
            P.add('dve', f2, reads=['ss2', ('h', tt)], writes=['xn', 'ss'])

            def f3(e):
                r = None
                for kc in range(8):
                    r = e.transpose(pT[:, kc, :], xn[:, kc * 128:(kc + 1) * 128], ident[:])
                return r
            P.add('pe', f3, reads=['xn', 'ident'], writes=['B2'])

            def f4(e):
                r = None
                for kc in range(8):
                    r = e.activation(dst[:, kc, col0:col0 + 128], pT[:, kc, :], AF.Copy,
                                     scale=cols[:, gcol * 8 + kc:gcol * 8 + kc + 1])
                return r
            P.add('act', f4, reads=['B2', 'cols'], writes=[dst_res] + list(extra_w))

        LCH = 512
        PAD = 256

        def zero_bufs(e):
            return e.memset(arena2[:, 8192:12032], 0.0)
        P.add('pool', zero_bufs, writes=['bufA', 'bufB'])

        def s5_chunk(ci):
            sample = (ci == 4)
            L = 128 if sample else LCH
            tts = [16] if sample else [ci * 4 + q for q in range(4)]
            for q, tt in enumerate(tts):
                rms_to_uT(tt, uT, q * 128, 0, 'uT')
            for j in range(NJ):
                kc, j4 = j // 4, j % 4

                def bproj(e, j=j, kc=kc):
                    r = None
                    for ri, pp in enumerate((pPr, pPi)):
                        r = e.matmul(pp[:, 0:L], wb[:, j, ri, :], uT[:, kc, 0:L], start=True, stop=True)
                    return r
                P.add('pe', bproj, reads=['uT', 'wb'], writes=['B0', 'B1'])

                if sample:
                    A = [sbufS[0][ri][:, :, 4:12] for ri in range(2)]
                    pv = [pp[:, 0:128].rearrange("p (b t) -> p b t", t=8) for pp in (pPr, pPi)]
                    tv = tmpc[:, 0:128].rearrange("p (b t) -> p b t", t=8)
                else:
                    A = [bufs[0][ri][:, PAD:PAD + L] for ri in range(2)]
                    pv = [pp[:, 0:L] for pp in (pPr, pPi)]
                    tv = tmpc[:, 0:L]

                if sample:
                    tva = junk2[:, 0:128].rearrange("p (b t) -> p b t", t=8)
                    tvb = junk2[:, 512:640].rearrange("p (b t) -> p b t", t=8)
                else:
                    tva, tvb = junk2[:, 0:L], junk2[:, 512:512 + L]

                def evac_a(e, j=j, pv=pv, tva=tva, tvb=tvb):
                    e.activation(tva, pv[1], AF.Copy, scale=fi[:, j:j + 1])
                    return e.activation(tvb, pv[1], AF.Copy, scale=fr[:, j:j + 1])
                P.add('act', evac_a, reads=['B1', 'f'], writes=['tvab'])

                def evac(e, j=j, A=A, pv=pv, tva=tva, tvb=tvb):
                    frj, fij = fr[:, j:j + 1], fi[:, j:j + 1]
                    e.scalar_tensor_tensor(A[0], pv[0], frj, tva, ALU.mult, ALU.subtract)
                    e.nosync = True
                    r = e.scalar_tensor_tensor(A[1], pv[0], fij, tvb, ALU.mult, ALU.add)
                    e.nosync = False
                    a_r, a_i, a_n = apr[:, 0, j:j + 1], api[:, 0, j:j + 1], apn[:, 0, j:j + 1]
                    if sample:
                        c0 = [sbufS[0][ri][:, :, 4:5] for ri in range(2)]
                        er = h0[:, j, 0, :].unsqueeze(2)
                        ei = h0[:, j, 1, :].unsqueeze(2)
                    elif ci > 0:
                        c0 = [bufs[0][ri][:, PAD:PAD + 1] for ri in range(2)]
                        er, ei = xend[:, j, 0:1], xend[:, j, 1:2]
                    else:
                        return r
                    e.scalar_tensor_tensor(c0[0], er, a_r, c0[0], ALU.mult, ALU.add)
                    e.scalar_tensor_tensor(c0[0], ei, a_n, c0[0], ALU.mult, ALU.add)
                    e.scalar_tensor_tensor(c0[1], ei, a_r, c0[1], ALU.mult, ALU.add)
                    r = e.scalar_tensor_tensor(c0[1], er, a_i, c0[1], ALU.mult, ALU.add)
                    return r
                P.add('dve', evac, reads=['B0', 'B1', 'f', 'apw', 'xend', 'h0', 'tvab'], writes=['bufA'])

                nsteps = 3 if sample else 9
                assert nsteps % 2 == 1

                def scan(e, j=j):
                    r = None
                    cur = 0
                    for k in range(nsteps):
                        s = 1 << k
                        a_r, a_i, a_n = apr[:, k, j:j + 1], api[:, k, j:j + 1], apn[:, k, j:j + 1]
                        if sample:
                            X = [sbufS[cur][ri][:, :, 4:12] for ri in range(2)]
                            Xs = [sbufS[cur][ri][:, :, 4 - s:12 - s] for ri in range(2)]
                            Y = [sbufS[1 - cur][ri][:, :, 4:12] for ri in range(2)]
                        else:
                            X = [bufs[cur][ri][:, PAD:PAD + L] for ri in range(2)]
                            Xs = [bufs[cur][ri][:, PAD - s:PAD + L - s] for ri in range(2)]
                            Y = [bufs[1 - cur][ri][:, PAD:PAD + L] for ri in range(2)]
                        e.scalar_tensor_tensor(Y[0], Xs[0], a_r, X[0], ALU.mult, ALU.add)
                        if not sample:
                            e.nosync = True
                        e.scalar_tensor_tensor(Y[0], Xs[1], a_n, Y[0], ALU.mult, ALU.add)
                        e.scalar_tensor_tensor(Y[1], Xs[1], a_r, X[1], ALU.mult, ALU.add)
                        r = e.scalar_tensor_tensor(Y[1], Xs[0], a_i, Y[1], ALU.mult, ALU.add)
                        cur = 1 - cur
                    e.nosync = False
                    assert cur == 1
                    return r
                P.add('dve', scan, reads=['bufA', 'apw'], writes=['bufA', 'bufB'])

                if STAGE < 2:
                    continue
                pb = j % 2
                if sample:
                    R = [sbufS[1][ri][:, :, 4:12] for ri in range(2)]
                    ov = [t[:, 0:128].rearrange("p (b t) -> p b t", t=8) for t in (xrb[pb], xib[pb])]
                else:
                    R = [bufs[1][ri][:, PAD:PAD + L] for ri in range(2)]
                    ov = [xrb[pb][:, 0:L], xib[pb][:, 0:L]]

                def conv(e, R=R, ov=ov, j=j):
                    if sample:
                        for ri in range(2):
                            e.activation(st_s_sb[:, j, ri, :].unsqueeze(2), sbufS[1][ri][:, :, 11:12], AF.Copy)
                    else:
                        for ri in range(2):
                            e.activation(xend[:, j, ri:ri + 1], bufs[1][ri][:, PAD + L - 1:PAD + L], AF.Copy)
                    e.activation(ov[0], R[0], AF.Copy)
                    return e.activation(ov[1], R[1], AF.Copy, scale=-1.0)
                P.add('act', conv, reads=['bufB'], writes=['xb%d' % pb, 'xend', 'res'])

                def cproj(e, j=j, j4=j4, pb=pb):
                    e.matmul(yT[:, 0:L], cx[:, j, 0, :], xrb[pb][:, 0:L], start=(j4 == 0), stop=False)
                    return e.matmul(yT[:, 0:L], cx[:, j, 1, :], xib[pb][:, 0:L], start=False, stop=(j4 == 3))
                P.add('pe', cproj, reads=['xb%d' % pb, 'cx'], writes=['B3'])

                if j4 == 3:
                    def yfin(e, kc=kc):
                        return e.scalar_tensor_tensor(ysb[:, 0:L], uT[:, kc, 0:L], cols[:, 8 + kc:9 + kc],
                                                      yT[:, 0:L], ALU.mult, ALU.add)
                    P.add('dve', yfin, reads=['B3', 'uT', 'cols'], writes=['ysb'])
                    P.add('act', (lambda kc: lambda e: e.activation(ygT[:, kc, 0:L], ysb[:, 0:L],
                                                                    AF.Gelu_apprx_tanh))(kc),
                          reads=['ysb'], writes=['ygT'])

            if STAGE < 2:
                return
            for q, tt in enumerate(tts):
                def glu_mm(e, q=q):
                    r = None
                    for n in range(4):
                        for kc in range(8):
                            r = e.matmul(zb[n][:, :], ygT[:, kc, q * 128:(q + 1) * 128],
                                         wglu[:, kc, n * 512:(n + 1) * 512], start=(kc == 0), stop=(kc == 7))
                    return r
                P.add('pe', glu_mm, reads=['ygT', 'arena'], writes=['B4', 'B5', 'B6', 'B7'])

                def glu_sig(e):
                    e.activation(junk[:, 0:512], zb[2][:, :], AF.Sigmoid)
                    return e.activation(junk[:, 512:1024], zb[3][:, :], AF.Sigmoid)
                P.add('act', glu_sig, reads=['B6', 'B7'], writes=['junk'])

                def glu_fin(e, tt=tt):
                    e.tensor_tensor(junk[:, 0:512], zb[0][:, :], junk[:, 0:512], ALU.mult)
                    e.tensor_tensor(junk[:, 512:1024], zb[1][:, :], junk[:, 512:1024], ALU.mult)
                    return e.tensor_tensor(h[:, tt, :], h[:, tt, :], junk[:], ALU.add)
                P.add('dve', glu_fin, reads=['junk', 'B4', 'B5', ('h', tt)],
                      writes=['junk', ('h', tt), 'B4', 'B5', 'B6', 'B7'])

        for ci in range(5):
            s5_chunk(ci)

        P.add('sync', lambda e: [e.dma_start(out=st_p[:, :], in_=xend[:].rearrange("p a b -> p (a b)")),
                                 e.dma_start(out=st_s[:, :], in_=st_s_sb.rearrange("p a b c -> p (a b c)"))],
              reads=['xend', 'res'], dma=('out', 2))

        def mlp(l, gcol, first_extra):
            for tt in range(17):
                rms_to_uT(tt, uT_all, tt * 128, gcol, 'uT_all', extra_w=(first_extra if tt == 0 else ()))
            groups = [(g * 256, 256, [2 * g, 2 * g + 1]) for g in range(8)] + [(2048, 128, [16])]
            for e8 in range(8):
                s = e8 % 2

                def wload(e, e8=e8, s=s):
                    return [e.dma_start(out=wup_v(s),
                                        in_=wup_d[l].rearrange("p (k n) -> p k n", k=8)[:, :, e8 * 512:(e8 + 1) * 512]),
                            e.dma_start(out=wdn_v(s),
                                        in_=wdn_d[l].rearrange("p (f n) -> p f n", f=32)[:, e8 * 4:(e8 + 1) * 4, :])]
                P.add('pool', wload, writes=['arena%d' % s] + (['arena'] if e8 < 2 else []),
                      dma=('arena%d' % s, 2))
                for (c0, N, tiles) in groups:
                    for f in range(4):
                        ub = Bk[f % 2]

                        def up_mm(e, f=f, ub=ub, c0=c0, N=N, s=s):
                            r = None
                            for kc in range(8):
                                r = e.matmul(ub[:, 0:N], wup_v(s)[:, kc, f * 128:(f + 1) * 128],
                                             uT_all[:, kc, c0:c0 + N], start=(kc == 0), stop=(kc == 7))
                            return r
                        P.add('pe', up_mm, reads=['uT_all', 'arena%d' % s], writes=['B%d' % (f % 2)])

                        def relu2(e, f=f, ub=ub, N=N):
                            e.activation(rl[:, 0:N], ub[:, 0:N], AF.Relu)
                            return e.activation(aT[:, f, 0:N], rl[:, 0:N], AF.Square)
                        P.add('act', relu2, reads=['B%d' % (f % 2)], writes=[('aT', f), 'rl'])
                    for ti, tt in enumerate(tiles):
                        dn = [Bk[4 + 2 * ti], Bk[5 + 2 * ti]]

                        def dn_mm(e, ti=ti, dn=dn, s=s):
                            r = None
                            for hf in range(2):
                                for f in range(4):
                                    r = e.matmul(dn[hf][:, :], aT[:, f, ti * 128:(ti + 1) * 128],
                                                 wdn_v(s)[:, f, hf * 512:(hf + 1) * 512],
                                                 start=(f == 0), stop=(f == 3))
                            return r
                        P.add('pe', dn_mm, reads=[('aT', f) for f in range(4)] + ['arena%d' % s],
                              writes=['B%d' % (4 + 2 * ti), 'B%d' % (5 + 2 * ti)])

                        def dn_add(e, tt=tt, dn=dn):
                            e.tensor_tensor(h[:, tt, 0:512], h[:, tt, 0:512], dn[0][:, :], ALU.add)
                            return e.tensor_tensor(h[:, tt, 512:1024], h[:, tt, 512:1024], dn[1][:, :], ALU.add)
                        P.add('dve', dn_add, reads=['B%d' % (4 + 2 * ti), 'B%d' % (5 + 2 * ti), ('h', tt)],
                              writes=[('h', tt), 'B%d' % (4 + 2 * ti), 'B%d' % (5 + 2 * ti)])

        if STAGE >= 3:
            mlp(0, 2, tuple(S5RES))

            P.add('pool', lambda e: [e.dma_start(out=wkv[:, kc, :], in_=wkv_d[:, 2048 * kc:2048 * (kc + 1)])
                                     for kc in range(8)], writes=['arena', 'arena0', 'arena1'], dma=('arena', 8))

            def ones_col(e):
                return e.memset(Vp[:, :, :, 64:65], 1.0)
            P.add('pool', ones_col, writes=['Vp', 'uT_all'])

            def kv_tile(tt):
                rms_to_uT(tt, uT, 0, 3, 'uT')

                def kv_mm(e):
                    r = None
                    for n in range(4):
                        for kc in range(8):
                            r = e.matmul(zb[n][:, :], uT[:, kc, 0:128], wkv[:, kc, n * 512:(n + 1) * 512],
                                         start=(kc == 0), stop=(kc == 7))
                    return r
                P.add('pe', kv_mm, reads=['uT', 'arena'], writes=['B4', 'B5', 'B6', 'B7'])

                kraw = junk
                vraw = junk2

                def kv_evac(e):
                    e.activation(kraw[:, 0:512], zb[0][:, :], AF.Copy)
                    e.activation(kraw[:, 512:1024], zb[1][:, :], AF.Copy)
                    e.activation(vraw[:, 0:512], zb[2][:, :], AF.Copy)
                    return e.activation(vraw[:, 512:1024], zb[3][:, :], AF.Copy)
                P.add('act', kv_evac, reads=['B4', 'B5', 'B6', 'B7'], writes=['junk', 'junk2', 'B4', 'B5', 'B6', 'B7'])

                def head_norm_rope(e, X, gain_off, tt=tt):
                    X3 = X.rearrange("p (h d) -> p h d", h=16)
                    sq = tmpc[:, :].rearrange("p (h d) -> p h d", h=16)
                    for half in range(2):
                        e.tensor_tensor(sq, X3[:, :, half * 32:(half + 1) * 32], X3[:, :, half * 32:(half + 1) * 32],
                                        ALU.mult)
                        e.tensor_reduce(ssh[:, half, :], sq, AX.X, ALU.add)
                    e.tensor_tensor(ssh[:, 2, :], ssh[:, 0, :], ssh[:, 1, :], ALU.add)
                    return e.tensor_scalar(ssh[:, 2, :], ssh[:, 2, :], 1.0 / 64, EPS, ALU.mult, ALU.add)

                def k_norm1(e):
                    return head_norm_rope(e, kraw[:, :], 0)
                P.add('dve', k_norm1, reads=['junk'], writes=['ssh', 'tmpc'])
                P.add('act', lambda e: e.activation(ssh[:, 3, :], ssh[:, 2, :], AF.Sqrt), reads=['ssh'], writes=['ssh3'])

                def k_norm2(e, tt=tt):
                    X3 = kraw[:, :].rearrange("p (h d) -> p h d", h=16)
                    e.reciprocal(ssh[:, 2, :], ssh[:, 3, :])
                    e.tensor_tensor(X3, X3, ssh[:, 2, :].unsqueeze(2).to_broadcast([128, 16, 64]), ALU.mult)
                    e.tensor_tensor(X3, X3, knb[:, 0:64].unsqueeze(1).to_broadcast([128, 16, 64]), ALU.mult)
                    cs = rope[:, tt, 0:8].unsqueeze(1).to_broadcast([128, 16, 8])
                    sn = rope[:, tt, 8:16].unsqueeze(1).to_broadcast([128, 16, 8])
                    t1 = tmpc[:, 0:128].rearrange("p (h d) -> p h d", h=16)
                    t2 = tmpc[:, 128:256].rearrange("p (h d) -> p h d", h=16)
                    t3 = tmpc[:, 256:384].rearrange("p (h d) -> p h d", h=16)
                    x1, x2 = X3[:, :, 0:8], X3[:, :, 8:16]
                    e.tensor_tensor(t1, x1, sn, ALU.mult)
                    e.tensor_tensor(t2, x2, sn, ALU.mult)
                    e.tensor_tensor(t3, x1, cs, ALU.mult)
                    e.tensor_tensor(x1, t3, t2, ALU.subtract)
                    e.tensor_tensor(t3, x2, cs, ALU.mult)
                    e.tensor_tensor(x2, t3, t1, ALU.add)
                    return e.tensor_copy(xn[:], kraw[:, :])
                P.add('dve', k_norm2, reads=['ssh3', 'junk', 'knb', 'rope'], writes=['junk', 'ssh', 'tmpc', 'xn'])

                if tt < 16:
                    P.add('sync', (lambda tt: lambda e: [
                        e.dma_start(out=k_p[tt * 128:(tt + 1) * 128, :], in_=kraw[:, :]),
                        e.dma_start(out=v_p[tt * 128:(tt + 1) * 128, :], in_=vraw[:, :])])(tt),
                        reads=['junk', 'junk2'], dma=('kvout', 2))

                    def kT_tr(e):
                        r = None
                        for hp in range(8):
                            r = e.transpose(pT2[:, hp, :], xn[:, hp * 128:(hp + 1) * 128], ident[:])
                        return r
                    P.add('pe', kT_tr, reads=['xn', 'ident'], writes=['B3'])
                    P.add('act', (lambda tt: lambda e: e.activation(KT[:, :, tt * 128:(tt + 1) * 128], pT2[:, :, :],
                                                                    AF.Copy))(tt),
                          reads=['B3'], writes=['KT', 'B3'])
                    P.add('pool', (lambda tt: lambda e: e.tensor_copy(
                        Vp[:, tt, :, 0:64], vraw[:, :].rearrange("p (h d) -> p h d", h=16)))(tt),
                        reads=['junk2'], writes=['Vp'])
                else:
                    P.add('sync', lambda e: [e.dma_start(out=k_s[:, :], in_=kraw[:, :]),
                                             e.dma_start(out=v_s[:, :], in_=vraw[:, :])],
                          reads=['junk', 'junk2'], dma=('kvout', 2))

                    def kT_tr_s(e):
                        r = None
                        for hp in range(8):
                            r = e.transpose(pT2[:, hp, :], xn[:, hp * 128:(hp + 1) * 128], ident[:])
                        return r
                    P.add('pe', kT_tr_s, reads=['xn', 'ident'], writes=['B3'])
                    P.add('act', lambda e: e.activation(KTs[:, :, :], pT2[:, :, :], AF.Copy), reads=['B3'],
                          writes=['KTs', 'B3'])
                    P.add('pool', lambda e: e.tensor_copy(Vs_bf[:], vraw[:, :]), reads=['junk2'], writes=['Vs_bf'])
            for tt in range(17):
                kv_tile(tt)

            P.barrier(['arena', 'arena0', 'arena1', 'scr_attn'])
            wq = arena[:, 0:8192].rearrange("p (k n) -> p k n", k=8)
            wo = arena[:, 8192:16384].rearrange("p (k n) -> p k n", k=8)
            P.add('pool', lambda e: [e.dma_start(out=wq[:, kc, :], in_=wq_d[:, 1024 * kc:1024 * (kc + 1)])
                                     for kc in range(8)] +
                                    [e.dma_start(out=wo[:, kc, :], in_=wo_d[:, 1024 * kc:1024 * (kc + 1)])
                                     for kc in range(8)],
                  writes=['arena'], dma=('arena', 16))
            uTq = scb(0, 512).rearrange("p (k n) -> p k n", k=8)
            qT = scb(512, 1024).rearrange("p (k n) -> p k n", k=8)
            PTb = [scb(1024 + 128 * i, 1152 + 128 * i).rearrange("p (k n) -> p k n", k=2) for i in range(2)]
            acc = scr[:, 1280:2320].rearrange("p (h d) -> p h d", h=16)
            Gs = scr[:, 2320:2448].rearrange("p (h n) -> p h n", h=16)
            sel = scr[:, 2448:2576].rearrange("p (h n) -> p h n", h=16)
            cmpb = scr[:, 2576:3600]
            rnk = scr[:, 3600:3728]
            kmT = scr[:, 3728:3792]
            kmBD = scb(3792, 3856).rearrange("p (k a n) -> p k a n", k=8, a=2)
            maskT = scb(3856, 3920)
            rden = scr[:, 3920:3936]
            obf = scb(3936, 4448)
            oT = scb(4448, 4960).rearrange("p (k n) -> p k n", k=8)
            maskf = scr[:, 4960:5088]

            def mk_mask(e):
                e.memset(maskf, 1.0)
                e.affine_select(maskf, maskf, [[1, 128]], ALU.is_ge, 0.0, base=0, channel_multiplier=-1)
                return e.tensor_copy(maskT, maskf)
            P.add('pool', mk_mask, writes=['maskT', 'scr_attn'])

            def mk_kmean(e):
                e.tensor_reduce(kmT, KT.rearrange("p k (n s) -> p (k n) s", s=256), AX.X, ALU.add)
                e.memset(kmBD, 0.0)
                km3 = kmT.rearrange("p (k n) -> p k n", k=8)
                e.tensor_scalar(kmBD[0:64, :, 0, :], km3[0:64], 1.0 / 256, None, ALU.mult)
                return e.tensor_scalar(kmBD[64:128, :, 1, :], km3[64:128], 1.0 / 256, None, ALU.mult)
            P.add('dve', mk_kmean, reads=['KT', 'scr_attn'], writes=['kmBD'])

            def head_norm(Xap, gain_lo, tt, post_scale, tag):
                X3 = Xap.rearrange("p (h d) -> p h d", h=16)

                def n1(e):
                    sq = cmpb[:, 0:512].rearrange("p (h d) -> p h d", h=16)
                    for half in range(2):
                        e.tensor_tensor(sq, X3[:, :, half * 32:(half + 1) * 32], X3[:, :, half * 32:(half + 1) * 32],
                                        ALU.mult)
                        e.tensor_reduce(ssh[:, half, :], sq, AX.X, ALU.add)
                    e.tensor_tensor(ssh[:, 2, :], ssh[:, 0, :], ssh[:, 1, :], ALU.add)
                    return e.tensor_scalar(ssh[:, 2, :], ssh[:, 2, :], 1.0 / 64, EPS, ALU.mult, ALU.add)
                P.add('dve', n1, reads=[tag], writes=['ssh', 'cmpb'])
                P.add('act', lambda e: e.activation(ssh[:, 3, :], ssh[:, 2, :], AF.Sqrt), reads=['ssh'], writes=['ssh3'])

                def n2(e):
                    e.reciprocal(ssh[:, 2, :], ssh[:, 3, :])
                    e.tensor_tensor(X3, X3, ssh[:, 2, :].unsqueeze(2).to_broadcast([128, 16, 64]), ALU.mult)
                    e.tensor_tensor(X3, X3, knb[:, gain_lo:gain_lo + 64].unsqueeze(1).to_broadcast([128, 16, 64]),
                                    ALU.mult)
                    cs = rope[:, tt, 0:8].unsqueeze(1).to_broadcast([128, 16, 8])
                    sn = rope[:, tt, 8:16].unsqueeze(1).to_broadcast([128, 16, 8])
                    t1 = cmpb[:, 0:128].rearrange("p (h d) -> p h d", h=16)
                    t2 = cmpb[:, 128:256].rearrange("p (h d) -> p h d", h=16)
                    t3 = cmpb[:, 256:384].rearrange("p (h d) -> p h d", h=16)
                    x1, x2 = X3[:, :, 0:8], X3[:, :, 8:16]
                    e.tensor_tensor(t1, x1, sn, ALU.mult)
                    e.tensor_tensor(t2, x2, sn, ALU.mult)
                    e.tensor_tensor(t3, x1, cs, ALU.mult)
                    e.tensor_tensor(x1, t3, t2, ALU.subtract)
                    e.tensor_tensor(t3, x2, cs, ALU.mult)
                    e.tensor_tensor(x2, t3, t1, ALU.add)
                    return e.tensor_scalar(xn[:], Xap, post_scale, None, ALU.mult)
                P.add('dve', n2, reads=['ssh3', tag, 'knb', 'rope'], writes=[tag, 'ssh', 'cmpb', 'xn'])

            def q_for_tile(tt, dstT, dst_res):
                rms_to_uT(tt, uTq, 0, 4, 'uTq')

                def q_mm(e):
                    r = None
                    for n in range(2):
                        for kc in range(8):
                            r = e.matmul(Bk[4 + n][:, :], uTq[:, kc, :], wq[:, kc, n * 512:(n + 1) * 512],
                                         start=(kc == 0), stop=(kc == 7))
                    return r
                P.add('pe', q_mm, reads=['uTq', 'arena'], writes=['B4', 'B5'])

                def q_evac(e):
                    e.activation(junk[:, 0:512], Bk[4][:, :], AF.Copy)
                    return e.activation(junk[:, 512:1024], Bk[5][:, :], AF.Copy)
                P.add('act', q_evac, reads=['B4', 'B5'], writes=['junk', 'B4', 'B5'])
                head_norm(junk[:, :], 64, tt, 0.125, 'junk')

                def q_tr(e):
                    r = None
                    for hp in range(8):
                        r = e.transpose(pT[:, hp, :], xn[:, hp * 128:(hp + 1) * 128], ident[:])
                    return r
                P.add('pe', q_tr, reads=['xn', 'ident'], writes=['B2'])
                P.add('act', lambda e: e.activation(dstT, pT[:, :, :], AF.Copy), reads=['B2'], writes=[dst_res, 'B2'])

            def wo_and_residual(tt, oT_ap, oT_res):
                def wo_mm(e):
                    r = None
                    for n in range(2):
                        for hp in range(8):
                            r = e.matmul(Bk[4 + n][:, :], oT_ap[:, hp, :], wo[:, hp, n * 512:(n + 1) * 512],
                                         start=(hp == 0), stop=(hp == 7))
                    return r
                P.add('pe', wo_mm, reads=[oT_res, 'arena'], writes=['B4', 'B5'])

                def wo_add(e):
                    e.tensor_tensor(h[:, tt, 0:512], h[:, tt, 0:512], Bk[4][:, :], ALU.add)
                    return e.tensor_tensor(h[:, tt, 512:1024], h[:, tt, 512:1024], Bk[5][:, :], ALU.add)
                P.add('dve', wo_add, reads=['B4', 'B5', ('h', tt)], writes=[('h', tt), 'B4', 'B5'])

            def attn_prompt_chunk(j):
                own = j // 2
                q_for_tile(j, qT[:, :, :], 'qT')
                if own > 3:
                    def gate_mm(e):
                        r = None
                        for hp in range(8):
                            r = e.matmul(Bk[3][:, hp * 16:(hp + 1) * 16], qT[:, hp, :],
                                         kmBD[:, hp, :, :].rearrange("p a n -> p (a n)"), start=True, stop=True)
                        return r
                    P.add('pe', gate_mm, reads=['qT', 'kmBD'], writes=['B3'])
                    P.add('act', lambda e: e.activation(Gs.rearrange("p h n -> p (h n)"), Bk[3][:, 0:128], AF.Copy),
                          reads=['B3'], writes=['Gs', 'B3'])

                    def selop(e):
                        g = Gs[:, :, 0:own]
                        c4 = cmpb[:, 0:16 * own * own].rearrange("p (h n m) -> p h n m", h=16, n=own)
                        e.tensor_tensor(c4, g.unsqueeze(2).to_broadcast([128, 16, own, own]),
                                        g.unsqueeze(3).to_broadcast([128, 16, own, own]), ALU.is_gt)
                        r3 = rnk[:, 0:16 * own].rearrange("p (h n) -> p h n", h=16)
                        e.tensor_reduce(r3, c4, AX.X, ALU.add)
                        return e.tensor_scalar(sel[:, :, 0:own], r3, 3.0, None, ALU.is_lt)
                    P.add('dve', selop, reads=['Gs'], writes=['sel', 'cmpb', 'rnk'])
                units = []
                for hh in range(16):
                    units.append((hh, own))
                    for n in range(own):
                        units.append((hh, n))

                def emit_qk(u, idx):
                    hh, n = u
                    hp, h2 = hh // 2, hh % 2
                    pb = idx % 2
                    tiles = [2 * n, 2 * n + 1] if n < own else list(range(2 * own, j + 1))

                    def qk(e):
                        r = None
                        for i, kt in enumerate(tiles):
                            r = e.matmul(Bk[pb][:, i * 128:(i + 1) * 128],
                                         KT[64 * h2:64 * h2 + 64, hp, kt * 128:(kt + 1) * 128],
                                         qT[64 * h2:64 * h2 + 64, hp, :], start=True, stop=True)
                        return r
                    P.add('pe', qk, reads=['KT', 'qT'], writes=['B%d' % pb])
                    nt = len(tiles)
                    P.add('act', lambda e: e.activation(PTb[pb][:, 0:nt, :].rearrange("p k n -> p (k n)"),
                                                        Bk[pb][:, 0:nt * 128], AF.Exp),
                          reads=['B%d' % pb], writes=['PT%d' % pb, 'B%d' % pb])
                    if n == own:
                        P.add('pool', lambda e: e.tensor_tensor(PTb[pb][:, nt - 1, :], PTb[pb][:, nt - 1, :], maskT,
                                                                ALU.mult),
                              reads=['PT%d' % pb, 'maskT'], writes=['PT%d' % pb])
                    return tiles

                def emit_pv(u, idx, tiles):
                    hh, n = u
                    pb = idx % 2
                    ob = Bk[6 + pb][:, 0:65]

                    def pv(e):
                        r = None
                        for i, kt in enumerate(tiles):
                            r = e.matmul(ob, PTb[pb][:, i, :], Vp[:, kt, hh, :], start=(i == 0),
                                         stop=(i == len(tiles) - 1))
                        return r
                    P.add('pe', pv, reads=['PT%d' % pb, 'Vp'], writes=['B%d' % (6 + pb)])
                    if n == own:
                        P.add('dve', lambda e: e.tensor_copy(acc[:, hh, :], ob), reads=['B%d' % (6 + pb)],
                              writes=['acc', 'B%d' % (6 + pb)])
                    elif own <= 3:
                        P.add('dve', lambda e: e.tensor_tensor(acc[:, hh, :], acc[:, hh, :], ob, ALU.add),
                              reads=['B%d' % (6 + pb), 'acc'], writes=['acc', 'B%d' % (6 + pb)])
                    else:
                        P.add('dve', lambda e: e.scalar_tensor_tensor(acc[:, hh, :], ob, sel[:, hh, n:n + 1],
                                                                      acc[:, hh, :], ALU.mult, ALU.add),
                              reads=['B%d' % (6 + pb), 'acc', 'sel'], writes=['acc', 'B%d' % (6 + pb)])
                prev = None
                for idx, u in enumerate(units):
                    tiles = emit_qk(u, idx)
                    if prev is not None:
                        emit_pv(*prev)
                    prev = (u, idx, tiles)
                emit_pv(*prev)

                def fin(e):
                    e.reciprocal(rden, acc[:, :, 64:65].rearrange("p h o -> p (h o)"))
                    return e.tensor_tensor(obf.rearrange("p (h d) -> p h d", h=16), acc[:, :, 0:64],
                                           rden.unsqueeze(2).to_broadcast([128, 16, 64]), ALU.mult)
                P.add('dve', fin, reads=['acc'], writes=['obf', 'rden'])

                def o_tr(e):
                    r = None
                    for hp in range(8):
                        r = e.transpose(pT[:, hp, :], obf[:, hp * 128:(hp + 1) * 128], ident[:])
                    return r
                P.add('pe', o_tr, reads=['obf', 'ident'], writes=['B2'])
                P.add('act', lambda e: e.activation(oT[:, :, :], pT[:, :, :], AF.Copy), reads=['B2'],
                      writes=['oT', 'B2'])
                wo_and_residual(j, oT, 'oT')

            for j in range(16):
                attn_prompt_chunk(j)

            SN = ['qbd', 'Odl', 'ksum', 'kmTs', 'gTs', 'gate', 'sels', 'Vnb', 'PTo', 'ob2', 'oTs', 'maskh', 'masko',
                  'rows', 'accs', 'kTp0', 'kTp1', 'PTs0', 'PTs1', 'Kpg0', 'Kpg1', 'Kpg2', 'Vpg0', 'Vpg1', 'Vpg2',
                  'pt_i', 'rows_f', 'iota', 'ones', 'par', 'PTof', 'obT', 'B3l', 'B3o', 'B3g', 'B3t']
            P.barrier(SN)
            qbd = a2b(0, 1024).rearrange("p (b k a t) -> p b k a t", b=16, k=8, a=2)
            Kpg = [a2b(1024 + 512 * i, 1536 + 512 * i) for i in range(3)]
            Vpg = [a2b(2560 + 512 * i, 3072 + 512 * i) for i in range(3)]
            kTp = [a2b(4096 + 512 * i, 4608 + 512 * i).rearrange("p (k n) -> p k n", k=8) for i in range(2)]
            PTs = [a2b(5120 + 64 * i, 5184 + 64 * i) for i in range(2)]
            Odl = arena2[:, 5248:5833].rearrange("p (n d) -> p n d", n=9)
            ksum = arena2[:, 5840:5968].rearrange("p (n g k) -> p n g k", n=8, g=2)
            kmTs = a2b(5968, 6000).rearrange("p (k n) -> p k n", k=8)
            gTs = arena2[0:8, 6000:6128]
            gate = arena2[:, 6128:6136]
            cmps = arena2[:, 6136:6200].rearrange("p (n m) -> p n m", n=8)
            ranks = arena2[:, 6200:6208]
            sels = arena2[:, 6208:6216]
            Vnb = a2b(6216, 6728)
            PTof = arena2[0:8, 6728:6856]
            PTo = a2b(6856, 6920)
            ob2 = a2b(6920, 6984)
            oTs = a2b(6984, 7496).rearrange("p (k n) -> p k n", k=8)
            maskh = arena2[:, 7496:7512]
            par = arena2[:, 7512:7514]
            masko = arena2[0:8, 7514:7642]
            rows_f = arena2[:, 7642:7898]
            rm = arena2[:, 7898:7900]
            accs = arena2[:, 7900:7965]
            tmp9 = arena2[:, 7968:8488].rearrange("p (n d) -> p n d", n=8)
            ones_bf = a2b(8488, 8489)
            obT = a2b(8490, 8554)
            iota_p = arena2[:, 8554:8555]
            pt_f = arena2[:, 8556:8812]

            def s_consts(e):
                e.memset(maskh, 1.0)
                e.affine_select(maskh, maskh, [[-8, 16]], ALU.is_ge, 0.0, base=0, channel_multiplier=1)
                e.affine_select(maskh, maskh, [[8, 16]], ALU.is_ge, 0.0, base=7, channel_multiplier=-1)
                e.memset(masko, 1.0)
                e.affine_select(masko, masko, [[0, 16], [1, 8]], ALU.is_ge, 0.0, base=0, channel_multiplier=-1)
                e.memset(ones_bf, 1.0)
                e.memset(qbd.rearrange("p b k a t -> p (b k a t)"), 0.0)
                return e.iota(iota_p, [[0, 1]], base=0, channel_multiplier=1, allow_small_or_imprecise_dtypes=True)
            P.add('pool', s_consts, writes=['maskh', 'masko', 'ones', 'qbd', 'iota'])
            P.add('dve', lambda e: e.tensor_reduce(par, maskh.rearrange("p (k a) -> p a k", a=2), AX.X, ALU.add),
                  reads=['maskh'], writes=['par'])
            P.add('sync', lambda e: [e.dma_start(out=pt_i[:], in_=pt_d[0:1, :].partition_broadcast(128))],
                  writes=['pt_i'], dma=('pt', 1))

            def mk_rows(e):
                e.tensor_copy(pt_f, pt_i[:])
                return e.tensor_scalar(rows_f, pt_f, 128.0, iota_p, ALU.mult, ALU.add)
            P.add('dve', mk_rows, reads=['pt_i', 'iota'], writes=['rows_f'])
            P.add('pool', lambda e: e.tensor_copy(rows_i[:], rows_f), reads=['rows_f'], writes=['rows'])

            q_for_tile(16, qT[:, :, :], 'qT')

            def mk_qbd(e):
                e.tensor_copy(qbd[0:64, :, :, 0, :], qT[0:64].rearrange("p k (b t) -> p b k t", t=8))
                return e.tensor_copy(qbd[64:128, :, :, 1, :], qT[64:128].rearrange("p k (b t) -> p b k t", t=8))
            P.add('dve', mk_qbd, reads=['qT', 'qbd'], writes=['qbd'])

            def diag_extract(n, par_n):
                ba, bb = (Bk[4], Bk[5]) if par_n == 0 else (Bk[6], Bk[7])
                rn = ['B4', 'B5'] if par_n == 0 else ['B6', 'B7']

                def f(e):
                    j3 = junk[:, :].rearrange("p (h d) -> p h d", h=16)
                    e.tensor_tensor(j3[:, 0:8, :], ba[:, :].rearrange("p (h d) -> p h d", h=8),
                                    maskh[:, 0:8].unsqueeze(2).to_broadcast([128, 8, 64]), ALU.mult)
                    e.tensor_tensor(j3[:, 8:16, :], bb[:, :].rearrange("p (h d) -> p h d", h=8),
                                    maskh[:, 8:16].unsqueeze(2).to_broadcast([128, 8, 64]), ALU.mult)
                    e.tensor_reduce(Odl[:, n, 0:64], junk[:, :].rearrange("p (h d) -> p d h", h=16), AX.X, ALU.add)
                    return e.tensor_copy(Odl[:, n, 64:65], Bk[3][:, n:n + 1])
                P.add('dve', f, reads=rn + ['B3l', 'maskh'], writes=['junk', 'Odl'] + rn + ['B3l'])

            def sample_batch(b):
                def vsel(e):
                    e.matmul(Bk[4][0:8, :], ident[:, b * 8:(b + 1) * 8], Vs_bf[:, 0:512], start=True, stop=True)
                    return e.matmul(Bk[5][0:8, :], ident[:, b * 8:(b + 1) * 8], Vs_bf[:, 512:1024], start=True,
                                    stop=True)
                P.add('pe', vsel, reads=['ident', 'Vs_bf'], writes=['B4', 'B5'])

                def vsel_ev(e):
                    e.activation(Vnb[0:8, 0:512], Bk[4][0:8, :], AF.Copy)
                    return e.activation(Vnb[0:8, 512:1024], Bk[5][0:8, :], AF.Copy)
                P.add('act', vsel_ev, reads=['B4', 'B5'], writes=['Vnb', 'B4', 'B5'])

                for n in range(8):
                    par_n = n % 2
                    On = (Bk[4], Bk[5]) if par_n == 0 else (Bk[6], Bk[7])
                    rn = ['B4', 'B5'] if par_n == 0 else ['B6', 'B7']
                    for pg in range(2):
                        c = b * 16 + 2 * n + pg
                        s3, s2 = c % 3, c % 2
                        P.add('pool', (lambda c, s3: lambda e: [e.indirect_dma_start(
                            out=Kpg[s3], out_offset=None, in_=ck_d,
                            in_offset=bass.IndirectOffsetOnAxis(ap=rows_i[:, c:c + 1], axis=0))])(c, s3),
                            reads=['rows'], writes=['Kpg%d' % s3], dma=('kpg%d' % s3, 1))
                        P.add('pool', (lambda c, s3: lambda e: [e.indirect_dma_start(
                            out=Vpg[s3], out_offset=None, in_=cv_d,
                            in_offset=bass.IndirectOffsetOnAxis(ap=rows_i[:, c:c + 1], axis=0))])(c, s3),
                            reads=['rows'], writes=['Vpg%d' % s3], dma=('vpg%d' % s3, 1))

                        def ktr(e, s3=s3):
                            r = None
                            for hp in range(8):
                                r = e.transpose(pT[:, hp, :], Kpg[s3][:, hp * 128:(hp + 1) * 128], ident[:])
                            return r
                        P.add('pe', ktr, reads=['Kpg%d' % s3, 'ident'], writes=['B2'])
                        def kev(e, s2=s2, n=n, pg=pg):
                            r = None
                            for hp in range(8):
                                r = e.activation(kTp[s2][:, hp, :], pT[:, hp, :], AF.Copy,
                                                 accum_out=ksum[:, n, pg, hp:hp + 1])
                            return r
                        P.add('act', kev, reads=['B2'], writes=['kTp%d' % s2, 'B2', 'ksum'])

                        def st_mm(e, s2=s2):
                            r = None
                            for hp in range(8):
                                r = e.matmul(Bk[s2][:, hp * 16:(hp + 1) * 16], kTp[s2][:, hp, :],
                                             qbd[:, b, hp, :, :].rearrange("p a t -> p (a t)"), start=True, stop=True)
                            return r
                        P.add('pe', st_mm, reads=['kTp%d' % s2, 'qbd'], writes=['B%d' % s2])
                        P.add('act', (lambda s2: lambda e: e.activation(PTs[s2], Bk[s2][:, 0:128], AF.Exp))(s2),
                              reads=['B%d' % s2], writes=['PTs%d' % s2, 'B%d' % s2])

                        def pv_mm(e, s2=s2, s3=s3, pg=pg, On=On, n=n):
                            e.matmul(On[0][:, :], PTs[s2], Vpg[s3][:, 0:512], start=(pg == 0), stop=(pg == 1))
                            e.matmul(On[1][:, :], PTs[s2], Vpg[s3][:, 512:1024], start=(pg == 0), stop=(pg == 1))
                            return e.matmul(Bk[3][:, n:n + 1], PTs[s2], ones_bf[:, 0:1], start=(pg == 0),
                                            stop=(pg == 1))
                        P.add('pe', pv_mm, reads=['PTs%d' % s2, 'Vpg%d' % s3, 'ones'], writes=rn + ['B3l'])
                    diag_extract(n, par_n)

                def own_st(e):
                    r = None
                    for hp in range(8):
                        r = e.matmul(Bk[3][0:8, 128 + hp * 16:128 + (hp + 1) * 16], KTs[:, hp, b * 8:(b + 1) * 8],
                                     qbd[:, b, hp, :, :].rearrange("p a t -> p (a t)"), start=True, stop=True)
                    return r
                P.add('pe', own_st, reads=['KTs', 'qbd'], writes=['B3o'])
                P.add('act', lambda e: e.activation(PTof, Bk[3][0:8, 128:256], AF.Exp), reads=['B3o'],
                      writes=['PTof', 'B3o'])
                P.add('dve', lambda e: e.tensor_tensor(PTo[0:8, :], PTof, masko, ALU.mult), reads=['PTof', 'masko'],
                      writes=['PTo'])

                def own_pv(e):
                    e.matmul(Bk[4][:, :], PTo[0:8, :], Vnb[0:8, 0:512], start=True, stop=True)
                    e.matmul(Bk[5][:, :], PTo[0:8, :], Vnb[0:8, 512:1024], start=True, stop=True)
                    return e.matmul(Bk[3][:, 8:9], PTo[0:8, :], ones_bf[0:8, 0:1], start=True, stop=True)
                P.add('pe', own_pv, reads=['PTo', 'Vnb', 'ones'], writes=['B4', 'B5', 'B3l'])
                diag_extract(8, 0)

                def mk_km(e):
                    return e.tensor_tensor(kmTs.rearrange("p k n -> p n k"), ksum[:, :, 0, :], ksum[:, :, 1, :],
                                           ALU.add)
                P.add('dve', mk_km, reads=['ksum'], writes=['kmTs'])

                def g_mm(e):
                    r = None
                    for hp in range(8):
                        r = e.matmul(Bk[3][0:8, 256 + hp * 16:256 + (hp + 1) * 16], kmTs[:, hp, :],
                                     qbd[:, b, hp, :, :].rearrange("p a t -> p (a t)"), start=True, stop=True)
                    return r
                P.add('pe', g_mm, reads=['kmTs', 'qbd'], writes=['B3g'])
                P.add('act', lambda e: e.activation(gTs, Bk[3][0:8, 256:384], AF.Copy), reads=['B3g'],
                      writes=['gTs', 'B3g'])
                P.add('pe', lambda e: e.transpose(Bk[3][:, 400:408], gTs, identf[0:8, 0:8]), reads=['gTs', 'identf'],
                      writes=['B3t'])

                def selop(e):
                    e.tensor_copy(gate, Bk[3][:, 400:408])
                    e.tensor_tensor(cmps, gate.unsqueeze(1).to_broadcast([128, 8, 8]),
                                    gate.unsqueeze(2).to_broadcast([128, 8, 8]), ALU.is_gt)
                    e.tensor_reduce(ranks, cmps, AX.X, ALU.add)
                    return e.tensor_scalar(sels, ranks, 3.0, None, ALU.is_lt)
                P.add('dve', selop, reads=['B3t'], writes=['gate', 'sels', 'B3t'])

                def combine(e):
                    e.tensor_tensor(tmp9, Odl[:, 0:8, :], sels.unsqueeze(2).to_broadcast([128, 8, 65]), ALU.mult)
                    e.tensor_reduce(accs, tmp9.rearrange("p n d -> p d n"), AX.X, ALU.add)
                    e.tensor_tensor(accs, accs, Odl[:, 8, :], ALU.add)
                    e.reciprocal(rm[:, 0:1], accs[:, 64:65])
                    e.tensor_scalar(rm[:, 1:2], par[:, 1:2], rm[:, 0:1], None, ALU.mult)
                    e.tensor_scalar(rm[:, 0:1], par[:, 0:1], rm[:, 0:1], None, ALU.mult)
                    e.tensor_scalar(ob2[:, 0:64], accs[:, 0:64], rm[:, 0:1], None, ALU.mult)
                    return e.tensor_scalar(ob2[:, 64:128], accs[:, 0:64], rm[:, 1:2], None, ALU.mult)
                P.add('dve', combine, reads=['Odl', 'sels', 'par'], writes=['ob2', 'accs'])
                P.add('pe', lambda e: e.transpose(pT[:, 0, :], ob2, ident[:]), reads=['ob2', 'ident'], writes=['B2'])
                P.add('act', lambda e: e.activation(obT, pT[:, 0, :], AF.Copy), reads=['B2'], writes=['obT', 'B2'])

                def o_place(e):
                    T4 = obT.rearrange("p (k a t) -> p k a t", k=8, a=2)
                    return e.tensor_tensor(oTs[:, :, b * 8:(b + 1) * 8], T4[:, :, 0, :], T4[:, :, 1, :], ALU.add)
                P.add('dve', o_place, reads=['obT'], writes=['oTs'])

            for b in range(16):
                sample_batch(b)
            wo_and_residual(16, oTs, 'oTs')

            P.barrier(['arena', 'arena0', 'arena1', 'uT_all', 'rl'] + [('aT', f) for f in range(4)])
            mlp(1, 5, ())

        for tt in range(17):
            if tt < 16:
                P.add('sync', (lambda tt: lambda e: [e.dma_start(out=y_p[tt * 128:(tt + 1) * 128, :], in_=h[:, tt, :])])(tt),
                      reads=[('h', tt)], dma=('yout', 1))
            else:
                P.add('sync', lambda e: [e.dma_start(out=y_s[:, :], in_=h[:, 16, :])], reads=[('h', 16)],
                      dma=('yout', 1))

        cum = P.emit(lambda n: es.enter_context(nc.semaphore(n.replace(':', '_'))),
                     lambda e: e.memset(dummy[:, 0:1], 0.0), counts)
    return nc, cum


_NC_CACHE = {}


def _rope_table():
    half = 8
    inv = (np.float32(500000.0) ** (-np.arange(half, dtype=np.float32) / np.float32(half))).astype(np.float32)
    tab = np.zeros((128, 17, 16), np.float32)
    for tt in range(17):
        if tt < 16:
            pos = (tt * 128 + np.arange(128)).astype(np.float32)
        else:
            pos = (2048 + (np.arange(128) % 8)).astype(np.float32)
        ang = pos[:, None] * inv[None, :]
        tab[:, tt, 0:8] = np.cos(ang)
        tab[:, tt, 8:16] = np.sin(ang)
    return tab.reshape(128, -1)


def _host_shared(inp):
    f32 = np.float32
    m = {}
    wbm = np.zeros((4, 2, 16, NJ, 2, 2, 64), f32)
    for ri, key in enumerate(["ssm_b_re", "ssm_b_im"]):
        B5 = inp[key][0].reshape(NJ, 2, 64, 16)
        for j in range(NJ):
            for g2 in range(2):
                wbm[j % 4, g2, :, j, ri, g2, :] = B5[j, g2].T
    m["wb"] = wbm.reshape(128, -1)
    cxm = np.zeros((2, 64, NJ, 2, 4, 2, 16), f32)
    for ri, key in enumerate(["ssm_c_re", "ssm_c_im"]):
        C5 = inp[key][0].reshape(NJ, 2, 16, 64)
        for j in range(NJ):
            for g2 in range(2):
                cxm[g2, :, j, ri, j % 4, g2, :] = C5[j, g2].T
    m["cx"] = cxm.reshape(128, -1)
    lamm = np.zeros((128, 3 * NJ), f32)
    lamm[:, 0:NJ] = inp["ssm_lambda_re"][0].reshape(NJ, 2, 64).transpose(1, 2, 0).reshape(128, NJ)
    lamm[:, NJ:2 * NJ] = inp["ssm_lambda_im"][0].reshape(NJ, 2, 64).transpose(1, 2, 0).reshape(128, NJ)
    lamm[:, 2 * NJ:] = np.broadcast_to(inp["ssm_log_dt"][0].reshape(NJ, 2).T[:, None, :], (2, 64, NJ)).reshape(128, NJ)
    m["lam"] = lamm
    cols = np.zeros((128, 48), f32)
    for i, v in enumerate([inp["ssm_norm"][0], inp["ssm_d"][0], inp["mlp_norm"][0], inp["kv_norm"],
                           inp["attn_norm"][0], inp["mlp_norm"][1]]):
        cols[:, 8 * i:8 * i + 8] = v.reshape(8, 128).T
    m["cols"] = cols
    knb = np.zeros((128, 128), f32)
    knb[:, 0:64] = inp["k_norm"][None, :]
    knb[:, 64:128] = inp["q_norm"][0][None, :]
    m["knb"] = knb
    m["rope"] = _rope_table()

    def kmaj(w):
        K_, N_ = w.shape
        return np.ascontiguousarray(w.reshape(K_ // 128, 128, N_).transpose(1, 0, 2).reshape(128, -1))
    m["wglu"] = kmaj(inp["ssm_w_glu"][0])
    m["wkv"] = kmaj(inp["w_kv"])
    m["wq"] = kmaj(inp["w_q"][0])
    m["wo"] = kmaj(inp["w_o"][0])
    for l in range(2):
        m["wup%d" % l] = kmaj(inp["w_up"][l])
        m["wdn%d" % l] = kmaj(inp["w_down"][l])
    return m


def _host_core(inp, c):
    f32 = np.float32
    m = {}
    m["xp"] = np.ascontiguousarray(inp["x_prompt"][c])
    m["xs"] = np.ascontiguousarray(inp["x_sample"][16 * c:16 * c + 16].reshape(128, D))
    h0m = np.zeros((2, 64, NJ, 2, 16), f32)
    for ri, key in enumerate(["state_ssm_re", "state_ssm_im"]):
        S = inp[key][0, 16 * c:16 * c + 16].reshape(16, NJ, 2, 64)
        h0m[:, :, :, ri, :] = S.transpose(2, 3, 1, 0)
    m["h0"] = h0m.reshape(128, -1)
    m["pt"] = np.ascontiguousarray(inp["page_table"][16 * c:16 * c + 16].reshape(1, 256).astype(np.int32))
    m["ck"] = inp["cache_k"].reshape(NPOOL * 128, 1024)
    m["cv"] = inp["cache_v"].reshape(NPOOL * 128, 1024)
    return m


def kernel(**inp):
    inp = {k: np.asarray(v) for k, v in inp.items()}
    if "nc" not in _NC_CACHE:
        _, cum = build_nc(None)
        _NC_CACHE["nc"], _ = build_nc(cum)
    nc = _NC_CACHE["nc"]
    shared = _host_shared(inp)
    in_maps = []
    for c in range(NCORE):
        m = dict(shared)
        m.update(_host_core(inp, c))
        in_maps.append(m)
    res = run_bass_kernel_spmd(nc, in_maps, core_ids=list(range(NCORE)))
    R = res.results
    f32 = np.float32
    y_prompt = np.zeros((8, 2048, D), f32)
    y_sample = np.zeros((128, 8, D), f32)
    k_prompt = np.zeros((8, 2048, 16, 64), f32)
    v_prompt = np.zeros((8, 2048, 16, 64), f32)
    k_sample = np.zeros((128, 8, 16, 64), f32)
    v_sample = np.zeros((128, 8, 16, 64), f32)
    srp = np.zeros((1, 8, G, PST), f32)
    sip = np.zeros((1, 8, G, PST), f32)
    srs = np.zeros((1, 128, G, PST), f32)
    sis = np.zeros((1, 128, G, PST), f32)
    for c in range(NCORE):
        r = R[c]
        y_prompt[c] = np.asarray(r["y_p"])
        y_sample[16 * c:16 * c + 16] = np.asarray(r["y_s"]).reshape(16, 8, D)
        k_prompt[c] = np.asarray(r["k_p"]).reshape(2048, 16, 64)
        v_prompt[c] = np.asarray(r["v_p"]).reshape(2048, 16, 64)
        k_sample[16 * c:16 * c + 16] = np.asarray(r["k_s"]).reshape(16, 8, 16, 64)
        v_sample[16 * c:16 * c + 16] = np.asarray(r["v_s"]).reshape(16, 8, 16, 64)
        sp = np.asarray(r["st_p"]).reshape(2, 64, NJ, 2)
        srp[0, c] = sp[..., 0].transpose(2, 0, 1).reshape(G, PST)
        sip[0, c] = sp[..., 1].transpose(2, 0, 1).reshape(G, PST)
        s_ = np.asarray(r["st_s"]).reshape(2, 64, NJ, 2, 16)
        srs[0, 16 * c:16 * c + 16] = s_[:, :, :, 0, :].transpose(3, 2, 0, 1).reshape(16, G, PST)
        sis[0, 16 * c:16 * c + 16] = s_[:, :, :, 1, :].transpose(3, 2, 0, 1).reshape(16, G, PST)
    return (y_prompt, y_sample, k_prompt, v_prompt, k_sample, v_sample, srp, sip, srs, sis)
```

```python
import numpy as np
from contextlib import ExitStack
import concourse.bass as bass
import concourse.mybir as mybir
from concourse.bass_utils import run_bass_kernel_spmd

F32 = mybir.dt.float32
BF16 = mybir.dt.bfloat16
I32 = mybir.dt.int32
ALU = mybir.AluOpType
AF = mybir.ActivationFunctionType
AX = mybir.AxisListType

NCORE = 8
D = 1024
G, PST, CG = 64, 64, 16
NJ = 32
EPS = 1e-6


class Prog:
    def __init__(self, nc):
        self.nc = nc
        self.ops = []
        self.last_w = {}
        self.readers = {}

    def add(self, eng, fn, reads=(), writes=(), dma=None):
        deps = set()
        for r in reads:
            if r in self.last_w:
                deps.add(self.last_w[r])
        for w in writes:
            if w in self.last_w:
                deps.add(self.last_w[w])
            deps |= set(self.readers.get(w, ()))
        i = len(self.ops)
        self.ops.append(dict(eng=eng, fn=fn, deps=deps, dma=dma))
        for r in reads:
            self.readers.setdefault(r, []).append(i)
        for w in writes:
            self.last_w[w] = i
            self.readers[w] = []
        return i

    def _bar_fn(self, e):
        return e.memset(self.bar_ap, 0.0)

    def barrier(self, new_names, dummy_fn=None):
        names = set(self.last_w) | set(self.readers) | set(new_names)
        self.add('dve', self._bar_fn, reads=(), writes=list(names))

    def emit(self, semctx, pool_dummy, counts):
        nc = self.nc
        ops = self.ops
        engs = ['sync', 'act', 'dve', 'pool', 'pe']
        slot_cnt = {}
        dmaval = [None] * len(ops)
        for i, o in enumerate(ops):
            if o['dma'] is not None:
                slot, n = o['dma']
                slot_cnt[slot] = slot_cnt.get(slot, 0) + n
                dmaval[i] = ('dma:' + slot, 16 * slot_cnt[slot])
        semnames = list(engs) + ['dma:' + s for s in slot_cnt]
        sems = {n: semctx(n) for n in semnames}
        per_eng = {e: [] for e in engs}
        for i, o in enumerate(ops):
            per_eng[o['eng']].append(i)
        cum = [0] * len(ops)

        class Proxy:
            def __init__(pself, e, ename):
                pself.e = e
                pself.ename = ename
                pself.count = 0
                pself.selfsync = ename in ('act', 'dve', 'pool')
                pself.nosync = False

            def __getattr__(pself, name):
                f = getattr(pself.e, name)
                if name in ('wait_ge', 'dma_start', 'indirect_dma_start'):
                    return f
                if name in ('nosync',):
                    return object.__getattribute__(pself, name)

                def w(*a, **k):
                    if pself.selfsync and pself.count > 0 and not pself.nosync:
                        pself.e.wait_ge(sems[pself.ename], pself.count)
                    ins = f(*a, **k)
                    ins.then_inc(sems[pself.ename], 1)
                    pself.count += 1
                    return ins
                return w

        def run_engine(ename, e):
            waited = {}
            px = Proxy(e, ename)
            for i in per_eng[ename]:
                o = ops[i]
                nw = 0
                for d in sorted(o['deps']):
                    od = ops[d]
                    if od['dma'] is None and od['eng'] == ename:
                        continue
                    nw += 1
                    if od['dma'] is not None:
                        sn, v = dmaval[d]
                    else:
                        sn, v = od['eng'], (counts[d] if counts is not None else 1)
                    if waited.get(sn, 0) >= v:
                        continue
                    waited[sn] = v
                    e.wait_ge(sems[sn], v)
                if ename == 'pool' and o['dma'] is not None and nw > 0:
                    pool_dummy(px)
                r = o['fn'](px)
                if o['dma'] is not None:
                    sn, v = dmaval[i]
                    assert len(r) == o['dma'][1], (len(r), o['dma'])
                    for ins in r:
                        ins.then_inc(sems[sn], 16)
                cum[i] = px.count
            if ename == 'sync':
                for s, c in slot_cnt.items():
                    e.wait_ge(sems['dma:' + s], 16 * c)

        with nc.Block() as block:
            @block.sync
            def _(e):
                run_engine('sync', e)

            @block.scalar
            def _(e):
                run_engine('act', e)

            @block.vector
            def _(e):
                run_engine('dve', e)

            @block.gpsimd
            def _(e):
                run_engine('pool', e)

            @block.tensor
            def _(e):
                run_engine('pe', e)
        if counts is not None:
            assert counts == cum
        return cum


NPOOL = 2560
STAGE = 5


def build_nc(counts=None):
    nc = bass.Bass("TRN2", target_bir_lowering=False)

    def din(name, shape, dt=F32):
        return nc.dram_tensor(name, list(shape), dt, kind="ExternalInput").ap()

    def dout(name, shape, dt=F32):
        return nc.dram_tensor(name, list(shape), dt, kind="ExternalOutput").ap()

    xp = din("xp", [2048, D])
    xs = din("xs", [128, D])
    wb_d = din("wb", [128, NJ * 2 * 128])
    cx_d = din("cx", [128, NJ * 2 * 128])
    lam_d = din("lam", [128, 3 * NJ])
    h0_d = din("h0", [128, NJ * 2 * 16])
    cols_d = din("cols", [128, 48])
    knb_d = din("knb", [128, 128])
    rope_d = din("rope", [128, 17 * 16])
    wglu_d = din("wglu", [128, 8 * 2048])
    wkv_d = din("wkv", [128, 8 * 2048])
    wq_d = din("wq", [128, 8 * 1024])
    ck_d = din("ck", [NPOOL * 128, 1024])
    cv_d = din("cv", [NPOOL * 128, 1024])
    pt_d = din("pt", [1, 256], I32)
    wo_d = din("wo", [128, 8 * 1024])
    wup_d = [din("wup%d" % l, [128, 8 * 4096]) for l in range(2)]
    wdn_d = [din("wdn%d" % l, [128, 32 * 1024]) for l in range(2)]
    st_p = dout("st_p", [128, NJ * 2])
    st_s = dout("st_s", [128, NJ * 2 * 16])
    k_p = dout("k_p", [2048, D])
    v_p = dout("v_p", [2048, D])
    k_s = dout("k_s", [128, D])
    v_s = dout("v_s", [128, D])
    y_p = dout("y_p", [2048, D])
    y_s = dout("y_s", [128, D])

    with ExitStack() as es:
        def sb(name, shape, dt=F32):
            return es.enter_context(nc.sbuf_tensor("s_" + name, list(shape), dt))

        def ps(name, shape, dt=F32):
            return es.enter_context(nc.psum_tensor("q_" + name, list(shape), dt))

        P = Prog(nc)

        h = sb("h", [128, 17, D])
        arena = sb("arena", [128, 16384], BF16)
        arena2 = sb("arena2", [128, 16896])
        Bk = [ps("bank%d" % i, [128, 512]) for i in range(8)]
        identf = sb("identf", [128, 128])
        ident = sb("ident", [128, 128], BF16)
        dummy = sb("dummy_t", [128, 8])
        P.bar_ap = dummy[:, 1:2]
        lam = sb("lam", [128, 3 * NJ])
        cols = sb("cols", [128, 48])
        knb = sb("knb", [128, 128])
        rope = sb("rope", [128, 17, 16])
        NPW = 9
        apr = sb("apr", [128, NPW, NJ])
        api = sb("api", [128, NPW, NJ])
        apn = sb("apn", [128, NPW, NJ])
        fr = sb("fr", [128, NJ])
        fi = sb("fi", [128, NJ])
        tsm = [sb("tsm%d" % i, [128, NJ]) for i in range(10)]
        xend = sb("xend", [128, NJ, 2])
        scr = sb("scr", [128, 6144])

        def scb(lo, hi):
            return scr[:, lo:hi].bitcast(BF16)
        uT = scb(0, 2048).rearrange("p (k n) -> p k n", k=8)
        xrb = [scb(2048 + 256 * i, 2304 + 256 * i) for i in range(2)]
        xib = [scb(2560 + 256 * i, 2816 + 256 * i) for i in range(2)]
        junk2 = scr[:, 3072:4096]
        tmpc = scr[:, 4096:4608]
        ysb = scr[:, 4608:5120]
        aT = scb(5120, 5632).rearrange("p (f n) -> p f n", f=4)
        rl = scr[:, 5632:5888]
        junk = sb("junk", [128, D])
        xn = sb("xn", [128, D], BF16)
        ss = sb("ss", [128, 4])
        ssh = sb("ssh", [128, 4, 16])
        KTs = scb(5088, 5600).rearrange("p (k n) -> p k n", k=8)
        Vs_bf = scb(5600, 6112)

        def a2b(lo, hi):
            return arena2[:, lo:hi].bitcast(BF16)
        wb = a2b(0, 4096).rearrange("p (a b c) -> p a b c", a=NJ, b=2)
        cx = a2b(4096, 8192).rearrange("p (a b c) -> p a b c", a=NJ, b=2)
        bufs = [[arena2[:, 8192 + (pp * 2 + ri) * 768: 8192 + (pp * 2 + ri + 1) * 768] for ri in range(2)]
                for pp in range(2)]
        sbufS = [[arena2[:, 11264 + (pp * 2 + ri) * 192: 11264 + (pp * 2 + ri + 1) * 192]
                  .rearrange("p (b t) -> p b t", t=12) for ri in range(2)] for pp in range(2)]
        ygT = a2b(12032, 14080).rearrange("p (k n) -> p k n", k=8)
        h0 = arena2[:, 14080:15104].rearrange("p (a b c) -> p a b c", a=NJ, b=2)
        st_s_sb = arena2[:, 15104:16128].rearrange("p (a b c) -> p a b c", a=NJ, b=2)
        uT_all = a2b(0, 8704).rearrange("p (k n) -> p k n", k=8)
        KT = a2b(0, 8192).rearrange("p (k n) -> p k n", k=8)
        Vp = a2b(8192, 16512).rearrange("p (t h d) -> p t h d", t=16, h=16)
        rows_i = arena2[:, 8812:9068].bitcast(I32)
        pt_i = arena2[:, 9068:9324].bitcast(I32)
        S5RES = ['wb', 'cx', 'bufA', 'bufB', 'ygT', 'h0', 'res']
        wglu = arena[:, :].rearrange("p (k n) -> p k n", k=8)
        wkv = wglu

        def wup_v(s):
            return arena[:, s * 8192: s * 8192 + 4096].rearrange("p (k n) -> p k n", k=8)

        def wdn_v(s):
            return arena[:, s * 8192 + 4096: s * 8192 + 8192].rearrange("p (f n) -> p f n", f=4)

        pPr, pPi = Bk[0], Bk[1]
        pT = Bk[2][:, :].bitcast(BF16).rearrange("p (k n) -> p k n", k=8)
        pT2 = Bk[3][:, :].bitcast(BF16).rearrange("p (k n) -> p k n", k=8)
        yT = Bk[3]
        zb = Bk[4:8]

        def mkident(e):
            e.memset(identf[:], 0.0)
            e.memset(dummy[:], 0.0)
            return e.affine_select(identf[:], identf[:], [[-1, 128]], ALU.not_equal, 1.0,
                                   base=0, channel_multiplier=1)
        P.add('pool', mkident, writes=['identf'])
        P.add('dve', lambda e: e.tensor_copy(ident[:], identf[:]), reads=['identf'], writes=['ident'])

        P.add('sync', lambda e: [e.dma_start(out=lam[:], in_=lam_d[:, :]),
                                 e.dma_start(out=cols[:], in_=cols_d[:, :]),
                                 e.dma_start(out=knb[:], in_=knb_d[:, :]),
                                 e.dma_start(out=rope[:].rearrange("p a b -> p (a b)"), in_=rope_d[:, :]),
                                 e.dma_start(out=h0.rearrange("p a b c -> p (a b c)"), in_=h0_d[:, :])],
              writes=['lam', 'cols', 'knb', 'rope', 'h0'], dma=('params', 5))
        P.add('pool', lambda e: [e.dma_start(out=wb[:, 8 * q:8 * q + 8].rearrange("p a b c -> p (a b c)"),
                                             in_=wb_d[:, 2048 * q:2048 * (q + 1)]) for q in range(4)],
              writes=['wb'], dma=('wbl', 4))
        P.add('pool', lambda e: [e.dma_start(out=cx[:, 8 * q:8 * q + 8].rearrange("p a b c -> p (a b c)"),
                                             in_=cx_d[:, 2048 * q:2048 * (q + 1)]) for q in range(4)],
              writes=['cx'], dma=('cx', 4))
        if STAGE >= 2:
            P.add('pool', lambda e: [e.dma_start(out=wglu[:, kc, :], in_=wglu_d[:, 2048 * kc:2048 * (kc + 1)])
                                     for kc in range(8)], writes=['arena'], dma=('arena', 8))
        for t in range(16):
            P.add('sync', (lambda t: lambda e: [e.dma_start(out=h[:, t, :], in_=xp[t * 128:(t + 1) * 128, :])])(t),
                  writes=[('h', t)], dma=('h%d' % t, 1))
        P.add('sync', lambda e: [e.dma_start(out=h[:, 16, :], in_=xs[:, :])], writes=[('h', 16)], dma=('h16', 1))

        lr, li, ld = lam[:, 0:NJ], lam[:, NJ:2 * NJ], lam[:, 2 * NJ:3 * NJ]
        dtv, rho, th, mag, kf, r_, s2, s4, c2, den = [t[:] for t in tsm]
        TWO_PI = float(2 * np.pi)
        MAGIC = 12582912.0
        P.add('act', lambda e: e.activation(dtv, ld, AF.Exp), reads=['lam'], writes=['dtv'])
        P.add('dve', lambda e: e.tensor_tensor(rho, lr, dtv, ALU.mult), reads=['lam', 'dtv'], writes=['rho'])
        P.add('dve', lambda e: e.tensor_tensor(th, li, dtv, ALU.mult), reads=['lam', 'dtv'], writes=['th'])
        P.add('act', lambda e: e.activation(mag, rho, AF.Exp), reads=['rho'], writes=['mag'])

        def rr(e):
            e.tensor_scalar(kf, th, 1.0 / TWO_PI, None, ALU.mult)
            e.tensor_scalar(kf, kf, MAGIC, None, ALU.add)
            e.tensor_scalar(kf, kf, -MAGIC, None, ALU.add)
            return e.scalar_tensor_tensor(r_, kf, -TWO_PI, th, ALU.mult, ALU.add)
        P.add('dve', rr, reads=['th'], writes=['r', 'kf'])
        P.add('act', lambda e: e.activation(s2, r_, AF.Sin, scale=0.5), reads=['r'], writes=['s2'])
        P.add('act', lambda e: e.activation(s4, r_, AF.Sin, scale=0.25), reads=['r'], writes=['s4'])

        def trig(e):
            e.tensor_tensor(c2, s4, s4, ALU.mult)
            e.tensor_scalar(c2, c2, -2.0, 1.0, ALU.mult, ALU.add)
            e.tensor_tensor(den, s2, s2, ALU.mult)
            e.tensor_scalar(den, den, -2.0, 1.0, ALU.mult, ALU.add)
            e.tensor_tensor(apr[:, 0, :], den, mag, ALU.mult)
            e.tensor_tensor(c2, c2, s2, ALU.mult)
            e.scalar_tensor_tensor(api[:, 0, :], c2, 2.0, mag, ALU.mult, ALU.mult)
            nr, tmp1, tmp2 = kf, r_, th
            e.tensor_scalar(nr, apr[:, 0, :], -1.0, None, ALU.add)
            e.tensor_tensor(den, lr, lr, ALU.mult)
            e.tensor_tensor(tmp1, li, li, ALU.mult)
            e.tensor_tensor(den, den, tmp1, ALU.add)
            e.reciprocal(den, den)
            e.tensor_tensor(tmp1, nr, lr, ALU.mult)
            e.tensor_tensor(tmp2, api[:, 0, :], li, ALU.mult)
            e.tensor_tensor(tmp1, tmp1, tmp2, ALU.add)
            e.tensor_tensor(fr[:], tmp1, den, ALU.mult)
            e.tensor_tensor(tmp1, api[:, 0, :], lr, ALU.mult)
            e.tensor_tensor(tmp2, nr, li, ALU.mult)
            e.tensor_tensor(tmp1, tmp1, tmp2, ALU.subtract)
            e.tensor_tensor(fi[:], tmp1, den, ALU.mult)
            for k in range(1, NPW):
                e.tensor_tensor(tmp1, apr[:, k - 1, :], apr[:, k - 1, :], ALU.mult)
                e.tensor_tensor(tmp2, api[:, k - 1, :], api[:, k - 1, :], ALU.mult)
                e.tensor_tensor(apr[:, k, :], tmp1, tmp2, ALU.subtract)
                e.scalar_tensor_tensor(api[:, k, :], apr[:, k - 1, :], 2.0, api[:, k - 1, :], ALU.mult, ALU.mult)
            return e.tensor_scalar(apn[:].rearrange("p a b -> p (a b)"), api[:].rearrange("p a b -> p (a b)"),
                                   -1.0, None, ALU.mult)
        P.add('dve', trig, reads=['s2', 's4', 'mag', 'lam', 'r', 'th', 'kf'], writes=['apw', 'f', 'r', 'th', 'kf'])

        def rms_to_uT(tt, dst, col0, gcol, dst_res, extra_w=()):
            def f1(e):
                e.scalar_tensor_tensor(junk[:], h[:, tt, :], 1.0, h[:, tt, :], ALU.mult, ALU.mult,
                                       accum_out=ss[:, 0:1])
                return e.tensor_scalar(ss[:, 1:2], ss[:, 0:1], 1.0 / D, EPS, ALU.mult, ALU.add)
            P.add('dve', f1, reads=[('h', tt)], writes=['ss', 'junk'])
            P.add('act', lambda e: e.activation(ss[:, 2:3], ss[:, 1:2], AF.Sqrt), reads=['ss'], writes=['ss2'])

            def f2(e):
                e.reciprocal(ss[:, 3:4], ss[:, 2:3])
                return e.tensor_scalar(xn[:], h[:, tt, :], ss[:, 3:4], None, ALU.mult)
            P.add('dve', f2, reads=['ss2', ('h', tt)], writes=['xn', 'ss'])

            def f3(e):
                r = None
                for kc in range(8):
                    r = e.transpose(pT[:, kc, :], xn[:, kc * 128:(kc + 1) * 128], ident[:])
                return r
            P.add('pe', f3, reads=['xn', 'ident'], writes=['B2'])

            def f4(e):
                r = None
                for kc in range(8):
                    r = e.activation(dst[:, kc, col0:col0 + 128], pT[:, kc, :], AF.Copy,
                                     scale=cols[:, gcol * 8 + kc:gcol * 8 + kc + 1])
                return r
            P.add('act', f4, reads=['B2', 'cols'], writes=[dst_res] + list(extra_w))

        LCH = 512
        PAD = 256

        def zero_bufs(e):
            return e.memset(arena2[:, 8192:12032], 0.0)
        P.add('pool', zero_bufs, writes=['bufA', 'bufB'])

        def s5_chunk(ci):
            sample = (ci == 4)
            L = 128 if sample else LCH
            tts = [16] if sample else [ci * 4 + q for q in range(4)]
            for q, tt in enumerate(tts):
                rms_to_uT(tt, uT, q * 128, 0, 'uT')
            for j in range(NJ):
                kc, j4 = j // 4, j % 4

                def bproj(e, j=j, kc=kc):
                    r = None
                    for ri, pp in enumerate((pPr, pPi)):
                        r = e.matmul(pp[:, 0:L], wb[:, j, ri, :], uT[:, kc, 0:L], start=True, stop=True)
                    return r
                P.add('pe', bproj, reads=['uT', 'wb'], writes=['B0', 'B1'])

                if sample:
                    A = [sbufS[0][ri][:, :, 4:12] for ri in range(2)]
                    pv = [pp[:, 0:128].rearrange("p (b t) -> p b t", t=8) for pp in (pPr, pPi)]
                    tv = tmpc[:, 0:128].rearrange("p (b t) -> p b t", t=8)
                else:
                    A = [bufs[0][ri][:, PAD:PAD + L] for ri in range(2)]
                    pv = [pp[:, 0:L] for pp in (pPr, pPi)]
                    tv = tmpc[:, 0:L]

                if sample:
                    tva = junk2[:, 0:128].rearrange("p (b t) -> p b t", t=8)
                    tvb = junk2[:, 512:640].rearrange("p (b t) -> p b t", t=8)
                else:
                    tva, tvb = junk2[:, 0:L], junk2[:, 512:512 + L]

                def evac_a(e, j=j, pv=pv, tva=tva, tvb=tvb):
                    e.activation(tva, pv[1], AF.Copy, scale=fi[:, j:j + 1])
                    return e.activation(tvb, pv[1], AF.Copy, scale=fr[:, j:j + 1])
                P.add('act', evac_a, reads=['B1', 'f'], writes=['tvab'])

                def evac(e, j=j, A=A, pv=pv, tva=tva, tvb=tvb):
                    frj, fij = fr[:, j:j + 1], fi[:, j:j + 1]
                    e.scalar_tensor_tensor(A[0], pv[0], frj, tva, ALU.mult, ALU.subtract)
                    e.nosync = True
                    r = e.scalar_tensor_tensor(A[1], pv[0], fij, tvb, ALU.mult, ALU.add)
                    e.nosync = False
                    a_r, a_i, a_n = apr[:, 0, j:j + 1], api[:, 0, j:j + 1], apn[:, 0, j:j + 1]
                    if sample:
                        c0 = [sbufS[0][ri][:, :, 4:5] for ri in range(2)]
                        er = h0[:, j, 0, :].unsqueeze(2)
                        ei = h0[:, j, 1, :].unsqueeze(2)
                    elif ci > 0:
                        c0 = [bufs[0][ri][:, PAD:PAD + 1] for ri in range(2)]
                        er, ei = xend[:, j, 0:1], xend[:, j, 1:2]
                    else:
                        return r
                    e.scalar_tensor_tensor(c0[0], er, a_r, c0[0], ALU.mult, ALU.add)
                    e.scalar_tensor_tensor(c0[0], ei, a_n, c0[0], ALU.mult, ALU.add)
                    e.scalar_tensor_tensor(c0[1], ei, a_r, c0[1], ALU.mult, ALU.add)
                    r = e.scalar_tensor_tensor(c0[1], er, a_i, c0[1], ALU.mult, ALU.add)
                    return r
                P.add('dve', evac, reads=['B0', 'B1', 'f', 'apw', 'xend', 'h0', 'tvab'], writes=['bufA'])

                nsteps = 3 if sample else 9
                assert nsteps % 2 == 1

                def scan(e, j=j):
                    r = None
                    cur = 0
                    for k in range(nsteps):
                        s = 1 << k
                        a_r, a_i, a_n = apr[:, k, j:j + 1], api[:, k, j:j + 1], apn[:, k, j:j + 1]
                        if sample:
                            X = [sbufS[cur][ri][:, :, 4:12] for ri in range(2)]
                            Xs = [sbufS[cur][ri][:, :, 4 - s:12 - s] for ri in range(2)]
                            Y = [sbufS[1 - cur][ri][:, :, 4:12] for ri in range(2)]
                        else:
                            X = [bufs[cur][ri][:, PAD:PAD + L] for ri in range(2)]
                            Xs = [bufs[cur][ri][:, PAD - s:PAD + L - s] for ri in range(2)]
                            Y = [bufs[1 - cur][ri][:, PAD:PAD + L] for ri in range(2)]
                        e.scalar_tensor_tensor(Y[0], Xs[0], a_r, X[0], ALU.mult, ALU.add)
                        if not sample:
                            e.nosync = True
                        e.scalar_tensor_tensor(Y[0], Xs[1], a_n, Y[0], ALU.mult, ALU.add)
                        e.scalar_tensor_tensor(Y[1], Xs[1], a_r, X[1], ALU.mult, ALU.add)
                        r = e.scalar_tensor_tensor(Y[1], Xs[0], a_i, Y[1], ALU.mult, ALU.add)
                        cur = 1 - cur
                    e.nosync = False
                    assert cur == 1
                    return r
                P.add('dve', scan, reads=['bufA', 'apw'], writes=['bufA', 'bufB'])

                if STAGE < 2:
                    continue
                pb = j % 2
                if sample:
                    R = [sbufS[1][ri][:, :, 4:12] for ri in range(2)]
                    ov = [t[:, 0:128].rearrange("p (b t) -> p b t", t=8) for t in (xrb[pb], xib[pb])]
                else:
                    R = [bufs[1][ri][:, PAD:PAD + L] for ri in range(2)]
                    ov = [xrb[pb][:, 0:L], xib[pb][:, 0:L]]

                def conv(e, R=R, ov=ov, j=j):
                    if sample:
                        for ri in range(2):
                            e.activation(st_s_sb[:, j, ri, :].unsqueeze(2), sbufS[1][ri][:, :, 11:12], AF.Copy)
                    else:
                        for ri in range(2):
                            e.activation(xend[:, j, ri:ri + 1], bufs[1][ri][:, PAD + L - 1:PAD + L], AF.Copy)
                    e.activation(ov[0], R[0], AF.Copy)
                    return e.activation(ov[1], R[1], AF.Copy, scale=-1.0)
                P.add('act', conv, reads=['bufB'], writes=['xb%d' % pb, 'xend', 'res'])

                def cproj(e, j=j, j4=j4, pb=pb):
                    e.matmul(yT[:, 0:L], cx[:, j, 0, :], xrb[pb][:, 0:L], start=(j4 == 0), stop=False)
                    return e.matmul(yT[:, 0:L], cx[:, j, 1, :], xib[pb][:, 0:L], start=False, stop=(j4 == 3))
                P.add('pe', cproj, reads=['xb%d' % pb, 'cx'], writes=['B3'])

                if j4 == 3:
                    def yfin(e, kc=kc):
                        return e.scalar_tensor_tensor(ysb[:, 0:L], uT[:, kc, 0:L], cols[:, 8 + kc:9 + kc],
                                                      yT[:, 0:L], ALU.mult, ALU.add)
                    P.add('dve', yfin, reads=['B3', 'uT', 'cols'], writes=['ysb'])
                    P.add('act', (lambda kc: lambda e: e.activation(ygT[:, kc, 0:L], ysb[:, 0:L],
                                                                    AF.Gelu_apprx_tanh))(kc),
                          reads=['ysb'], writes=['ygT'])

            if STAGE < 2:
                return
            for q, tt in enumerate(tts):
                def glu_mm(e, q=q):
                    r = None
                    for n in range(4):
                        for kc in range(8):
                            r = e.matmul(zb[n][:, :], ygT[:, kc, q * 128:(q + 1) * 128],
                                         wglu[:, kc, n * 512:(n + 1) * 512], start=(kc == 0), stop=(kc == 7))
                    return r
                P.add('pe', glu_mm, reads=['ygT', 'arena'], writes=['B4', 'B5', 'B6', 'B7'])

                def glu_sig(e):
                    e.activation(junk[:, 0:512], zb[2][:, :], AF.Sigmoid)
                    return e.activation(junk[:, 512:1024], zb[3][:, :], AF.Sigmoid)
                P.add('act', glu_sig, reads=['B6', 'B7'], writes=['junk'])

                def glu_fin(e, tt=tt):
                    e.tensor_tensor(junk[:, 0:512], zb[0][:, :], junk[:, 0:512], ALU.mult)
                    e.tensor_tensor(junk[:, 512:1024], zb[1][:, :], junk[:, 512:1024], ALU.mult)
                    return e.tensor_tensor(h[:, tt, :], h[:, tt, :], junk[:], ALU.add)
                P.add('dve', glu_fin, reads=['junk', 'B4', 'B5', ('h', tt)],
                      writes=['junk', ('h', tt), 'B4', 'B5', 'B6', 'B7'])

        for ci in range(5):
            s5_chunk(ci)

        P.add('sync', lambda e: [e.dma_start(out=st_p[:, :], in_=xend[:].rearrange("p a b -> p (a b)")),
                                 e.dma_start(out=st_s[:, :], in_=st_s_sb.rearrange("p a b c -> p (a b c)"))],
              reads=['xend', 'res'], dma=('out', 2))

        def mlp(l, gcol, first_extra):
            for tt in range(17):
                rms_to_uT(tt, uT_all, tt * 128, gcol, 'uT_all', extra_w=(first_extra if tt == 0 else ()))
            groups = [(g * 256, 256, [2 * g, 2 * g + 1]) for g in range(8)] + [(2048, 128, [16])]
            for e8 in range(8):
                s = e8 % 2

                def wload(e, e8=e8, s=s):
                    return [e.dma_start(out=wup_v(s),
                                        in_=wup_d[l].rearrange("p (k n) -> p k n", k=8)[:, :, e8 * 512:(e8 + 1) * 512]),
                            e.dma_start(out=wdn_v(s),
                                        in_=wdn_d[l].rearrange("p (f n) -> p f n", f=32)[:, e8 * 4:(e8 + 1) * 4, :])]
                P.add('pool', wload, writes=['arena%d' % s] + (['arena'] if e8 < 2 else []),
                      dma=('arena%d' % s, 2))
                for (c0, N, tiles) in groups:
                    for f in range(4):
                        ub = Bk[f % 2]

                        def up_mm(e, f=f, ub=ub, c0=c0, N=N, s=s):
                            r = None
                            for kc in range(8):
                                r = e.matmul(ub[:, 0:N], wup_v(s)[:, kc, f * 128:(f + 1) * 128],
                                             uT_all[:, kc, c0:c0 + N], start=(kc == 0), stop=(kc == 7))
                            return r
                        P.add('pe', up_mm, reads=['uT_all', 'arena%d' % s], writes=['B%d' % (f % 2)])

                        def relu2(e, f=f, ub=ub, N=N):
                            e.activation(rl[:, 0:N], ub[:, 0:N], AF.Relu)
                            return e.activation(aT[:, f, 0:N], rl[:, 0:N], AF.Square)
                        P.add('act', relu2, reads=['B%d' % (f % 2)], writes=[('aT', f), 'rl'])
                    for ti, tt in enumerate(tiles):
                        dn = [Bk[4 + 2 * ti], Bk[5 + 2 * ti]]

                        def dn_mm(e, ti=ti, dn=dn, s=s):
                            r = None
                            for hf in range(2):
                                for f in range(4):
                                    r = e.matmul(dn[hf][:, :], aT[:, f, ti * 128:(ti + 1) * 128],
                                                 wdn_v(s)[:, f, hf * 512:(hf + 1) * 512],
                                                 start=(f == 0), stop=(f == 3))
                            return r
                        P.add('pe', dn_mm, reads=[('aT', f) for f in range(4)] + ['arena%d' % s],
                              writes=['B%d' % (4 + 2 * ti), 'B%d' % (5 + 2 * ti)])

                        def dn_add(e, tt=tt, dn=dn):
                            e.tensor_tensor(h[:, tt, 0:512], h[:, tt, 0:512], dn[0][:, :], ALU.add)
                            return e.tensor_tensor(h[:, tt, 512:1024], h[:, tt, 512:1024], dn[1][:, :], ALU.add)
                        P.add('dve', dn_add, reads=['B%d' % (4 + 2 * ti), 'B%d' % (5 + 2 * ti), ('h', tt)],
                              writes=[('h', tt), 'B%d' % (4 + 2 * ti), 'B%d' % (5 + 2 * ti)])

        if STAGE >= 3:
            mlp(0, 2, tuple(S5RES))

            P.add('pool', lambda e: [e.dma_start(out=wkv[:, kc, :], in_=wkv_d[:, 2048 * kc:2048 * (kc + 1)])
                                     for kc in range(8)], writes=['arena', 'arena0', 'arena1'], dma=('arena', 8))

            def ones_col(e):
                return e.memset(Vp[:, :, :, 64:65], 1.0)
            P.add('pool', ones_col, writes=['Vp', 'uT_all'])

            def kv_tile(tt):
                rms_to_uT(tt, uT, 0, 3, 'uT')

                def kv_mm(e):
                    r = None
                    for n in range(4):
                        for kc in range(8):
                            r = e.matmul(zb[n][:, :], uT[:, kc, 0:128], wkv[:, kc, n * 512:(n + 1) * 512],
                                         start=(kc == 0), stop=(kc == 7))
                    return r
                P.add('pe', kv_mm, reads=['uT', 'arena'], writes=['B4', 'B5', 'B6', 'B7'])

                kraw = junk
                vraw = junk2

                def kv_evac(e):
                    e.activation(kraw[:, 0:512], zb[0][:, :], AF.Copy)
                    e.activation(kraw[:, 512:1024], zb[1][:, :], AF.Copy)
                    e.activation(vraw[:, 0:512], zb[2][:, :], AF.Copy)
                    return e.activation(vraw[:, 512:1024], zb[3][:, :], AF.Copy)
                P.add('act', kv_evac, reads=['B4', 'B5', 'B6', 'B7'], writes=['junk', 'junk2', 'B4', 'B5', 'B6', 'B7'])

                def head_norm_rope(e, X, gain_off, tt=tt):
                    X3 = X.rearrange("p (h d) -> p h d", h=16)
                    sq = tmpc[:, :].rearrange("p (h d) -> p h d", h=16)
                    for half in range(2):
                        e.tensor_tensor(sq, X3[:, :, half * 32:(half + 1) * 32], X3[:, :, half * 32:(half + 1) * 32],
                                        ALU.mult)
                        e.tensor_reduce(ssh[:, half, :], sq, AX.X, ALU.add)
                    e.tensor_tensor(ssh[:, 2, :], ssh[:, 0, :], ssh[:, 1, :], ALU.add)
                    return e.tensor_scalar(ssh[:, 2, :], ssh[:, 2, :], 1.0 / 64, EPS, ALU.mult, ALU.add)

                def k_norm1(e):
                    return head_norm_rope(e, kraw[:, :], 0)
                P.add('dve', k_norm1, reads=['junk'], writes=['ssh', 'tmpc'])
                P.add('act', lambda e: e.activation(ssh[:, 3, :], ssh[:, 2, :], AF.Sqrt), reads=['ssh'], writes=['ssh3'])

                def k_norm2(e, tt=tt):
                    X3 = kraw[:, :].rearrange("p (h d) -> p h d", h=16)
                    e.reciprocal(ssh[:, 2, :], ssh[:, 3, :])
                    e.tensor_tensor(X3, X3, ssh[:, 2, :].unsqueeze(2).to_broadcast([128, 16, 64]), ALU.mult)
                    e.tensor_tensor(X3, X3, knb[:, 0:64].unsqueeze(1).to_broadcast([128, 16, 64]), ALU.mult)
                    cs = rope[:, tt, 0:8].unsqueeze(1).to_broadcast([128, 16, 8])
                    sn = rope[:, tt, 8:16].unsqueeze(1).to_broadcast([128, 16, 8])
                    t1 = tmpc[:, 0:128].rearrange("p (h d) -> p h d", h=16)
                    t2 = tmpc[:, 128:256].rearrange("p (h d) -> p h d", h=16)
                    t3 = tmpc[:, 256:384].rearrange("p (h d) -> p h d", h=16)
                    x1, x2 = X3[:, :, 0:8], X3[:, :, 8:16]
                    e.tensor_tensor(t1, x1, sn, ALU.mult)
                    e.tensor_tensor(t2, x2, sn, ALU.mult)
                    e.tensor_tensor(t3, x1, cs, ALU.mult)
                    e.tensor_tensor(x1, t3, t2, ALU.subtract)
                    e.tensor_tensor(t3, x2, cs, ALU.mult)
                    e.tensor_tensor(x2, t3, t1, ALU.add)
                    return e.tensor_copy(xn[:], kraw[:, :])
                P.add('dve', k_norm2, reads=['ssh3', 'junk', 'knb', 'rope'], writes=['junk', 'ssh', 'tmpc', 'xn'])

                if tt < 16:
                    P.add('sync', (lambda tt: lambda e: [
                        e.dma_start(out=k_p[tt * 128:(tt + 1) * 128, :], in_=kraw[:, :]),
                        e.dma_start(out=v_p[tt * 128:(tt + 1) * 128, :], in_=vraw[:, :])])(tt),
                        reads=['junk', 'junk2'], dma=('kvout', 2))

                    def kT_tr(e):
                        r = None
                        for hp in range(8):
                            r = e.transpose(pT2[:, hp, :], xn[:, hp * 128:(hp + 1) * 128], ident[:])
                        return r
                    P.add('pe', kT_tr, reads=['xn', 'ident'], writes=['B3'])
                    P.add('act', (lambda tt: lambda e: e.activation(KT[:, :, tt * 128:(tt + 1) * 128], pT2[:, :, :],
                                                                    AF.Copy))(tt),
                          reads=['B3'], writes=['KT', 'B3'])
                    P.add('pool', (lambda tt: lambda e: e.tensor_copy(
                        Vp[:, tt, :, 0:64], vraw[:, :].rearrange("p (h d) -> p h d", h=16)))(tt),
                        reads=['junk2'], writes=['Vp'])
                else:
                    P.add('sync', lambda e: [e.dma_start(out=k_s[:, :], in_=kraw[:, :]),
                                             e.dma_start(out=v_s[:, :], in_=vraw[:, :])],
                          reads=['junk', 'junk2'], dma=('kvout', 2))

                    def kT_tr_s(e):
                        r = None
                        for hp in range(8):
                            r = e.transpose(pT2[:, hp, :], xn[:, hp * 128:(hp + 1) * 128], ident[:])
                        return r
                    P.add('pe', kT_tr_s, reads=['xn', 'ident'], writes=['B3'])
                    P.add('act', lambda e: e.activation(KTs[:, :, :], pT2[:, :, :], AF.Copy), reads=['B3'],
                          writes=['KTs', 'B3'])
                    P.add('pool', lambda e: e.tensor_copy(Vs_bf[:], vraw[:, :]), reads=['junk2'], writes=['Vs_bf'])
            for tt in range(17):
                kv_tile(tt)

            P.barrier(['arena', 'arena0', 'arena1', 'scr_attn'] + ['S%d' % i for i in range(4)] + ['O%d' % i for i in range(4)] + ['PT%d' % i for i in range(4)])
            wq = arena[:, 0:8192].rearrange("p (k n) -> p k n", k=8)
            wo = arena[:, 8192:16384].rearrange("p (k n) -> p k n", k=8)
            P.add('pool', lambda e: [e.dma_start(out=wq[:, kc, :], in_=wq_d[:, 1024 * kc:1024 * (kc + 1)])
                                     for kc in range(8)] +
                                    [e.dma_start(out=wo[:, kc, :], in_=wo_d[:, 1024 * kc:1024 * (kc + 1)])
                                     for kc in range(8)],
                  writes=['arena'], dma=('arena', 16))
            uTq = scb(0, 512).rearrange("p (k n) -> p k n", k=8)
            qT = scb(512, 1024).rearrange("p (k n) -> p k n", k=8)
            PTb = [scb(1024 + 128 * i, 1152 + 128 * i).rearrange("p (k n) -> p k n", k=2) for i in range(2)] + \
                  [scb(128 * i, 128 * i + 128).rearrange("p (k n) -> p k n", k=2) for i in range(1)]
            Sb = [Bk[0][:, 0:256], Bk[1][:, 0:256], Bk[3][:, 0:256]]
            Sn = ['B0', 'B1', 'B3']
            Ob = [Bk[6][:, 0:65], Bk[7][:, 0:65], Bk[5][:, 0:65]]
            On_ = ['B6', 'B7', 'B5']
            NBUF = 3
            acc = scr[:, 1280:2320].rearrange("p (h d) -> p h d", h=16)
            Gs = scr[:, 2320:2448].rearrange("p (h n) -> p h n", h=16)
            sel = scr[:, 2448:2576].rearrange("p (h n) -> p h n", h=16)
            cmpb = scr[:, 2576:3600]
            rnk = scr[:, 3600:3728]
            kmT = scr[:, 3728:3792]
            kmBD = scb(3792, 3856).rearrange("p (k a n) -> p k a n", k=8, a=2)
            maskT = scb(3856, 3920)
            rden = scr[:, 3920:3936]
            obf = scb(3936, 4448)
            oT = scb(4448, 4960).rearrange("p (k n) -> p k n", k=8)
            maskf = scr[:, 4960:5088]

            def mk_mask(e):
                e.memset(maskf, 1.0)
                e.affine_select(maskf, maskf, [[1, 128]], ALU.is_ge, 0.0, base=0, channel_multiplier=-1)
                return e.tensor_copy(maskT, maskf)
            P.add('pool', mk_mask, writes=['maskT', 'scr_attn'])

            def mk_kmean(e):
                e.tensor_reduce(kmT, KT.rearrange("p k (n s) -> p (k n) s", s=256), AX.X, ALU.add)
                e.memset(kmBD, 0.0)
                km3 = kmT.rearrange("p (k n) -> p k n", k=8)
                e.tensor_scalar(kmBD[0:64, :, 0, :], km3[0:64], 1.0 / 256, None, ALU.mult)
                return e.tensor_scalar(kmBD[64:128, :, 1, :], km3[64:128], 1.0 / 256, None, ALU.mult)
            P.add('dve', mk_kmean, reads=['KT', 'scr_attn'], writes=['kmBD'])

            def head_norm(Xap, gain_lo, tt, post_scale, tag):
                X3 = Xap.rearrange("p (h d) -> p h d", h=16)

                def n1(e):
                    sq = cmpb[:, 0:512].rearrange("p (h d) -> p h d", h=16)
                    for half in range(2):
                        e.tensor_tensor(sq, X3[:, :, half * 32:(half + 1) * 32], X3[:, :, half * 32:(half + 1) * 32],
                                        ALU.mult)
                        e.tensor_reduce(ssh[:, half, :], sq, AX.X, ALU.add)
                    e.tensor_tensor(ssh[:, 2, :], ssh[:, 0, :], ssh[:, 1, :], ALU.add)
                    return e.tensor_scalar(ssh[:, 2, :], ssh[:, 2, :], 1.0 / 64, EPS, ALU.mult, ALU.add)
                P.add('dve', n1, reads=[tag], writes=['ssh', 'cmpb'])
                P.add('act', lambda e: e.activation(ssh[:, 3, :], ssh[:, 2, :], AF.Sqrt), reads=['ssh'], writes=['ssh3'])

                def n2(e):
                    e.reciprocal(ssh[:, 2, :], ssh[:, 3, :])
                    e.tensor_tensor(X3, X3, ssh[:, 2, :].unsqueeze(2).to_broadcast([128, 16, 64]), ALU.mult)
                    e.tensor_tensor(X3, X3, knb[:, gain_lo:gain_lo + 64].unsqueeze(1).to_broadcast([128, 16, 64]),
                                    ALU.mult)
                    cs = rope[:, tt, 0:8].unsqueeze(1).to_broadcast([128, 16, 8])
                    sn = rope[:, tt, 8:16].unsqueeze(1).to_broadcast([128, 16, 8])
                    t1 = cmpb[:, 0:128].rearrange("p (h d) -> p h d", h=16)
                    t2 = cmpb[:, 128:256].rearrange("p (h d) -> p h d", h=16)
                    t3 = cmpb[:, 256:384].rearrange("p (h d) -> p h d", h=16)
                    x1, x2 = X3[:, :, 0:8], X3[:, :, 8:16]
                    e.tensor_tensor(t1, x1, sn, ALU.mult)
                    e.tensor_tensor(t2, x2, sn, ALU.mult)
                    e.tensor_tensor(t3, x1, cs, ALU.mult)
                    e.tensor_tensor(x1, t3, t2, ALU.subtract)
                    e.tensor_tensor(t3, x2, cs, ALU.mult)
                    e.tensor_tensor(x2, t3, t1, ALU.add)
                    return e.tensor_scalar(xn[:], Xap, post_scale, None, ALU.mult)
                P.add('dve', n2, reads=['ssh3', tag, 'knb', 'rope'], writes=[tag, 'ssh', 'cmpb', 'xn'])

            def q_for_tile(tt, dstT, dst_res):
                rms_to_uT(tt, uTq, 0, 4, 'uTq', extra_w=('PT2',))

                def q_mm(e):
                    r = None
                    for n in range(2):
                        for kc in range(8):
                            r = e.matmul(Bk[4 + n][:, :], uTq[:, kc, :], wq[:, kc, n * 512:(n + 1) * 512],
                                         start=(kc == 0), stop=(kc == 7))
                    return r
                P.add('pe', q_mm, reads=['uTq', 'arena', 'PT2'], writes=['B4', 'B5'])

                def q_evac(e):
                    e.activation(junk[:, 0:512], Bk[4][:, :], AF.Copy)
                    return e.activation(junk[:, 512:1024], Bk[5][:, :], AF.Copy)
                P.add('act', q_evac, reads=['B4', 'B5'], writes=['junk', 'B4', 'B5'])
                head_norm(junk[:, :], 64, tt, 0.125, 'junk')

                def q_tr(e):
                    r = None
                    for hp in range(8):
                        r = e.transpose(pT[:, hp, :], xn[:, hp * 128:(hp + 1) * 128], ident[:])
                    return r
                P.add('pe', q_tr, reads=['xn', 'ident'], writes=['B2'])
                P.add('act', lambda e: e.activation(dstT, pT[:, :, :], AF.Copy), reads=['B2'], writes=[dst_res, 'B2'])

            def wo_and_residual(tt, oT_ap, oT_res):
                def wo_mm(e):
                    r = None
                    for n in range(2):
                        for hp in range(8):
                            r = e.matmul(Bk[4 + n][:, :], oT_ap[:, hp, :], wo[:, hp, n * 512:(n + 1) * 512],
                                         start=(hp == 0), stop=(hp == 7))
                    return r
                P.add('pe', wo_mm, reads=[oT_res, 'arena'], writes=['B4', 'B5'])

                def wo_add(e):
                    e.tensor_tensor(h[:, tt, 0:512], h[:, tt, 0:512], Bk[4][:, :], ALU.add)
                    return e.tensor_tensor(h[:, tt, 512:1024], h[:, tt, 512:1024], Bk[5][:, :], ALU.add)
                P.add('dve', wo_add, reads=['B4', 'B5', ('h', tt)], writes=[('h', tt), 'B4', 'B5'])

            def attn_prompt_chunk(j):
                own = j // 2
                q_for_tile(j, qT[:, :, :], 'qT')
                if own > 3:
                    def gate_mm(e):
                        r = None
                        for hp in range(8):
                            r = e.matmul(Bk[3][:, hp * 16:(hp + 1) * 16], qT[:, hp, :],
                                         kmBD[:, hp, :, :].rearrange("p a n -> p (a n)"), start=True, stop=True)
                        return r
                    P.add('pe', gate_mm, reads=['qT', 'kmBD'], writes=['B3'])
                    P.add('act', lambda e: e.activation(Gs.rearrange("p h n -> p (h n)"), Bk[3][:, 0:128], AF.Copy),
                          reads=['B3'], writes=['Gs', 'B3'])

                    def selop(e):
                        g = Gs[:, :, 0:own]
                        c4 = cmpb[:, 0:16 * own * own].rearrange("p (h n m) -> p h n m", h=16, n=own)
                        e.tensor_tensor(c4, g.unsqueeze(2).to_broadcast([128, 16, own, own]),
                                        g.unsqueeze(3).to_broadcast([128, 16, own, own]), ALU.is_gt)
                        r3 = rnk[:, 0:16 * own].rearrange("p (h n) -> p h n", h=16)
                        e.tensor_reduce(r3, c4, AX.X, ALU.add)
                        return e.tensor_scalar(sel[:, :, 0:own], r3, 3.0, None, ALU.is_lt)
                    P.add('dve', selop, reads=['Gs'], writes=['sel', 'cmpb', 'rnk'])
                units = []
                for hh in range(16):
                    units.append((hh, own))
                    for n in range(own):
                        units.append((hh, n))

                def emit_qk(u, idx):
                    hh, n = u
                    hp, h2 = hh // 2, hh % 2
                    pb = idx % NBUF
                    tiles = [2 * n, 2 * n + 1] if n < own else list(range(2 * own, j + 1))

                    def qk(e):
                        r = None
                        for i, kt in enumerate(tiles):
                            r = e.matmul(Sb[pb][:, i * 128:(i + 1) * 128],
                                         KT[64 * h2:64 * h2 + 64, hp, kt * 128:(kt + 1) * 128],
                                         qT[64 * h2:64 * h2 + 64, hp, :], start=True, stop=True)
                        return r
                    P.add('pe', qk, reads=['KT', 'qT'], writes=[Sn[pb]])
                    nt = len(tiles)
                    P.add('act', lambda e: e.activation(PTb[pb][:, 0:nt, :].rearrange("p k n -> p (k n)"),
                                                        Sb[pb][:, 0:nt * 128], AF.Exp),
                          reads=[Sn[pb]], writes=['PT%d' % pb, Sn[pb]])
                    if n == own:
                        P.add('pool', lambda e: e.tensor_tensor(PTb[pb][:, nt - 1, :], PTb[pb][:, nt - 1, :], maskT,
                                                                ALU.mult),
                              reads=['PT%d' % pb, 'maskT'], writes=['PT%d' % pb])
                    return tiles

                def emit_pv(u, idx, tiles):
                    hh, n = u
                    pb = idx % NBUF
                    ob = Ob[pb]

                    def pv(e):
                        r = None
                        for i, kt in enumerate(tiles):
                            r = e.matmul(ob, PTb[pb][:, i, :], Vp[:, kt, hh, :], start=(i == 0),
                                         stop=(i == len(tiles) - 1))
                        return r
                    P.add('pe', pv, reads=['PT%d' % pb, 'Vp'], writes=[On_[pb]])
                    if n == own:
                        P.add('dve', lambda e: e.tensor_copy(acc[:, hh, :], ob), reads=[On_[pb]],
                              writes=['acc', On_[pb]])
                    elif own <= 3:
                        P.add('dve', lambda e: e.tensor_tensor(acc[:, hh, :], acc[:, hh, :], ob, ALU.add),
                              reads=[On_[pb], 'acc'], writes=['acc', On_[pb]])
                    else:
                        P.add('dve', lambda e: e.scalar_tensor_tensor(acc[:, hh, :], ob, sel[:, hh, n:n + 1],
                                                                      acc[:, hh, :], ALU.mult, ALU.add),
                              reads=[On_[pb], 'acc', 'sel'], writes=['acc', On_[pb]])
                pend = []
                for idx, u in enumerate(units):
                    tiles = emit_qk(u, idx)
                    pend.append((u, idx, tiles))
                    if len(pend) > 2:
                        emit_pv(*pend.pop(0))
                while pend:
                    emit_pv(*pend.pop(0))

                def fin(e):
                    e.reciprocal(rden, acc[:, :, 64:65].rearrange("p h o -> p (h o)"))
                    return e.tensor_tensor(obf.rearrange("p (h d) -> p h d", h=16), acc[:, :, 0:64],
                                           rden.unsqueeze(2).to_broadcast([128, 16, 64]), ALU.mult)
                P.add('dve', fin, reads=['acc'], writes=['obf', 'rden'])

                def o_tr(e):
                    r = None
                    for hp in range(8):
                        r = e.transpose(pT[:, hp, :], obf[:, hp * 128:(hp + 1) * 128], ident[:])
                    return r
                P.add('pe', o_tr, reads=['obf', 'ident'], writes=['B2'])
                P.add('act', lambda e: e.activation(oT[:, :, :], pT[:, :, :], AF.Copy), reads=['B2'],
                      writes=['oT', 'B2'])
                wo_and_residual(j, oT, 'oT')

            for j in range(16):
                attn_prompt_chunk(j)

            SN = ['qbd', 'Odl', 'ksum', 'kmTs', 'gTs', 'gate', 'sels', 'Vnb', 'PTo', 'ob2', 'oTs', 'maskh', 'masko',
                  'rows', 'accs', 'kTp0', 'kTp1', 'PTs0', 'PTs1', 'Kpg0', 'Kpg1', 'Kpg2', 'Vpg0', 'Vpg1', 'Vpg2',
                  'pt_i', 'rows_f', 'iota', 'ones', 'par', 'PTof', 'obT', 'B3l', 'B3o', 'B3g', 'B3t']
            P.barrier(SN)
            qbd = a2b(0, 1024).rearrange("p (b k a t) -> p b k a t", b=16, k=8, a=2)
            Kpg = [a2b(1024 + 512 * i, 1536 + 512 * i) for i in range(3)]
            Vpg = [a2b(2560 + 512 * i, 3072 + 512 * i) for i in range(3)]
            kTp = [a2b(4096 + 512 * i, 4608 + 512 * i).rearrange("p (k n) -> p k n", k=8) for i in range(2)]
            PTs = [a2b(5120 + 64 * i, 5184 + 64 * i) for i in range(2)]
            Odl = arena2[:, 5248:5833].rearrange("p (n d) -> p n d", n=9)
            ksum = arena2[:, 5840:5968].rearrange("p (n g k) -> p n g k", n=8, g=2)
            kmTs = a2b(5968, 6000).rearrange("p (k n) -> p k n", k=8)
            gTs = arena2[0:8, 6000:6128]
            gate = arena2[:, 6128:6136]
            cmps = arena2[:, 6136:6200].rearrange("p (n m) -> p n m", n=8)
            ranks = arena2[:, 6200:6208]
            sels = arena2[:, 6208:6216]
            Vnb = a2b(6216, 6728)
            PTof = arena2[0:8, 6728:6856]
            PTo = a2b(6856, 6920)
            ob2 = a2b(6920, 6984)
            oTs = a2b(6984, 7496).rearrange("p (k n) -> p k n", k=8)
            maskh = arena2[:, 7496:7512]
            par = arena2[:, 7512:7514]
            masko = arena2[0:8, 7514:7642]
            rows_f = arena2[:, 7642:7898]
            rm = arena2[:, 7898:7900]
            accs = arena2[:, 7900:7965]
            tmp9 = arena2[:, 7968:8488].rearrange("p (n d) -> p n d", n=8)
            ones_bf = a2b(8488, 8489)
            obT = a2b(8490, 8554)
            iota_p = arena2[:, 8554:8555]
            pt_f = arena2[:, 8556:8812]

            def s_consts(e):
                e.memset(maskh, 1.0)
                e.affine_select(maskh, maskh, [[-8, 16]], ALU.is_ge, 0.0, base=0, channel_multiplier=1)
                e.affine_select(maskh, maskh, [[8, 16]], ALU.is_ge, 0.0, base=7, channel_multiplier=-1)
                e.memset(masko, 1.0)
                e.affine_select(masko, masko, [[0, 16], [1, 8]], ALU.is_ge, 0.0, base=0, channel_multiplier=-1)
                e.memset(ones_bf, 1.0)
                e.memset(qbd.rearrange("p b k a t -> p (b k a t)"), 0.0)
                return e.iota(iota_p, [[0, 1]], base=0, channel_multiplier=1, allow_small_or_imprecise_dtypes=True)
            P.add('pool', s_consts, writes=['maskh', 'masko', 'ones', 'qbd', 'iota'])
            P.add('dve', lambda e: e.tensor_reduce(par, maskh.rearrange("p (k a) -> p a k", a=2), AX.X, ALU.add),
                  reads=['maskh'], writes=['par'])
            P.add('sync', lambda e: [e.dma_start(out=pt_i[:], in_=pt_d[0:1, :].partition_broadcast(128))],
                  writes=['pt_i'], dma=('pt', 1))

            def mk_rows(e):
                e.tensor_copy(pt_f, pt_i[:])
                return e.tensor_scalar(rows_f, pt_f, 128.0, iota_p, ALU.mult, ALU.add)
            P.add('dve', mk_rows, reads=['pt_i', 'iota'], writes=['rows_f'])
            P.add('pool', lambda e: e.tensor_copy(rows_i[:], rows_f), reads=['rows_f'], writes=['rows'])

            q_for_tile(16, qT[:, :, :], 'qT')

            def mk_qbd(e):
                e.tensor_copy(qbd[0:64, :, :, 0, :], qT[0:64].rearrange("p k (b t) -> p b k t", t=8))
                return e.tensor_copy(qbd[64:128, :, :, 1, :], qT[64:128].rearrange("p k (b t) -> p b k t", t=8))
            P.add('dve', mk_qbd, reads=['qT', 'qbd'], writes=['qbd'])

            def diag_extract(n, par_n):
                ba, bb = (Bk[4], Bk[5]) if par_n == 0 else (Bk[6], Bk[7])
                rn = ['B4', 'B5'] if par_n == 0 else ['B6', 'B7']

                def f(e):
                    j3 = junk[:, :].rearrange("p (h d) -> p h d", h=16)
                    e.tensor_tensor(j3[:, 0:8, :], ba[:, :].rearrange("p (h d) -> p h d", h=8),
                                    maskh[:, 0:8].unsqueeze(2).to_broadcast([128, 8, 64]), ALU.mult)
                    e.tensor_tensor(j3[:, 8:16, :], bb[:, :].rearrange("p (h d) -> p h d", h=8),
                                    maskh[:, 8:16].unsqueeze(2).to_broadcast([128, 8, 64]), ALU.mult)
                    e.tensor_reduce(Odl[:, n, 0:64], junk[:, :].rearrange("p (h d) -> p d h", h=16), AX.X, ALU.add)
                    return e.tensor_copy(Odl[:, n, 64:65], Bk[3][:, n:n + 1])
                P.add('dve', f, reads=rn + ['B3l', 'maskh'], writes=['junk', 'Odl'] + rn + ['B3l'])

            def sample_batch(b):
                def vsel(e):
                    e.matmul(Bk[4][0:8, :], ident[:, b * 8:(b + 1) * 8], Vs_bf[:, 0:512], start=True, stop=True)
                    return e.matmul(Bk[5][0:8, :], ident[:, b * 8:(b + 1) * 8], Vs_bf[:, 512:1024], start=True,
                                    stop=True)
                P.add('pe', vsel, reads=['ident', 'Vs_bf'], writes=['B4', 'B5'])

                def vsel_ev(e):
                    e.activation(Vnb[0:8, 0:512], Bk[4][0:8, :], AF.Copy)
                    return e.activation(Vnb[0:8, 512:1024], Bk[5][0:8, :], AF.Copy)
                P.add('act', vsel_ev, reads=['B4', 'B5'], writes=['Vnb', 'B4', 'B5'])

                for n in range(8):
                    par_n = n % 2
                    On = (Bk[4], Bk[5]) if par_n == 0 else (Bk[6], Bk[7])
                    rn = ['B4', 'B5'] if par_n == 0 else ['B6', 'B7']
                    for pg in range(2):
                        c = b * 16 + 2 * n + pg
                        s3, s2 = c % 3, c % 2
                        P.add('pool', (lambda c, s3: lambda e: [e.indirect_dma_start(
                            out=Kpg[s3], out_offset=None, in_=ck_d,
                            in_offset=bass.IndirectOffsetOnAxis(ap=rows_i[:, c:c + 1], axis=0))])(c, s3),
                            reads=['rows'], writes=['Kpg%d' % s3], dma=('kpg%d' % s3, 1))
                        P.add('pool', (lambda c, s3: lambda e: [e.indirect_dma_start(
                            out=Vpg[s3], out_offset=None, in_=cv_d,
                            in_offset=bass.IndirectOffsetOnAxis(ap=rows_i[:, c:c + 1], axis=0))])(c, s3),
                            reads=['rows'], writes=['Vpg%d' % s3], dma=('vpg%d' % s3, 1))

                        def ktr(e, s3=s3):
                            r = None
                            for hp in range(8):
                                r = e.transpose(pT[:, hp, :], Kpg[s3][:, hp * 128:(hp + 1) * 128], ident[:])
                            return r
                        P.add('pe', ktr, reads=['Kpg%d' % s3, 'ident'], writes=['B2'])
                        def kev(e, s2=s2, n=n, pg=pg):
                            r = None
                            for hp in range(8):
                                r = e.activation(kTp[s2][:, hp, :], pT[:, hp, :], AF.Copy,
                                                 accum_out=ksum[:, n, pg, hp:hp + 1])
                            return r
                        P.add('act', kev, reads=['B2'], writes=['kTp%d' % s2, 'B2', 'ksum'])

                        def st_mm(e, s2=s2):
                            r = None
                            for hp in range(8):
                                r = e.matmul(Bk[s2][:, hp * 16:(hp + 1) * 16], kTp[s2][:, hp, :],
                                             qbd[:, b, hp, :, :].rearrange("p a t -> p (a t)"), start=True, stop=True)
                            return r
                        P.add('pe', st_mm, reads=['kTp%d' % s2, 'qbd'], writes=['B%d' % s2])
                        P.add('act', (lambda s2: lambda e: e.activation(PTs[s2], Bk[s2][:, 0:128], AF.Exp))(s2),
                              reads=['B%d' % s2], writes=['PTs%d' % s2, 'B%d' % s2])

                        def pv_mm(e, s2=s2, s3=s3, pg=pg, On=On, n=n):
                            e.matmul(On[0][:, :], PTs[s2], Vpg[s3][:, 0:512], start=(pg == 0), stop=(pg == 1))
                            e.matmul(On[1][:, :], PTs[s2], Vpg[s3][:, 512:1024], start=(pg == 0), stop=(pg == 1))
                            return e.matmul(Bk[3][:, n:n + 1], PTs[s2], ones_bf[:, 0:1], start=(pg == 0),
                                            stop=(pg == 1))
                        P.add('pe', pv_mm, reads=['PTs%d' % s2, 'Vpg%d' % s3, 'ones'], writes=rn + ['B3l'])
                    diag_extract(n, par_n)

                def own_st(e):
                    r = None
                    for hp in range(8):
                        r = e.matmul(Bk[3][0:8, 128 + hp * 16:128 + (hp + 1) * 16], KTs[:, hp, b * 8:(b + 1) * 8],
                                     qbd[:, b, hp, :, :].rearrange("p a t -> p (a t)"), start=True, stop=True)
                    return r
                P.add('pe', own_st, reads=['KTs', 'qbd'], writes=['B3o'])
                P.add('act', lambda e: e.activation(PTof, Bk[3][0:8, 128:256], AF.Exp), reads=['B3o'],
                      writes=['PTof', 'B3o'])
                P.add('dve', lambda e: e.tensor_tensor(PTo[0:8, :], PTof, masko, ALU.mult), reads=['PTof', 'masko'],
                      writes=['PTo'])

                def own_pv(e):
                    e.matmul(Bk[4][:, :], PTo[0:8, :], Vnb[0:8, 0:512], start=True, stop=True)
                    e.matmul(Bk[5][:, :], PTo[0:8, :], Vnb[0:8, 512:1024], start=True, stop=True)
                    return e.matmul(Bk[3][:, 8:9], PTo[0:8, :], ones_bf[0:8, 0:1], start=True, stop=True)
                P.add('pe', own_pv, reads=['PTo', 'Vnb', 'ones'], writes=['B4', 'B5', 'B3l'])
                diag_extract(8, 0)

                def mk_km(e):
                    return e.tensor_tensor(kmTs.rearrange("p k n -> p n k"), ksum[:, :, 0, :], ksum[:, :, 1, :],
                                           ALU.add)
                P.add('dve', mk_km, reads=['ksum'], writes=['kmTs'])

                def g_mm(e):
                    r = None
                    for hp in range(8):
                        r = e.matmul(Bk[3][0:8, 256 + hp * 16:256 + (hp + 1) * 16], kmTs[:, hp, :],
                                     qbd[:, b, hp, :, :].rearrange("p a t -> p (a t)"), start=True, stop=True)
                    return r
                P.add('pe', g_mm, reads=['kmTs', 'qbd'], writes=['B3g'])
                P.add('act', lambda e: e.activation(gTs, Bk[3][0:8, 256:384], AF.Copy), reads=['B3g'],
                      writes=['gTs', 'B3g'])
                P.add('pe', lambda e: e.transpose(Bk[3][:, 400:408], gTs, identf[0:8, 0:8]), reads=['gTs', 'identf'],
                      writes=['B3t'])

                def selop(e):
                    e.tensor_copy(gate, Bk[3][:, 400:408])
                    e.tensor_tensor(cmps, gate.unsqueeze(1).to_broadcast([128, 8, 8]),
                                    gate.unsqueeze(2).to_broadcast([128, 8, 8]), ALU.is_gt)
                    e.tensor_reduce(ranks, cmps, AX.X, ALU.add)
                    return e.tensor_scalar(sels, ranks, 3.0, None, ALU.is_lt)
                P.add('dve', selop, reads=['B3t'], writes=['gate', 'sels', 'B3t'])

                def combine(e):
                    e.tensor_tensor(tmp9, Odl[:, 0:8, :], sels.unsqueeze(2).to_broadcast([128, 8, 65]), ALU.mult)
                    e.tensor_reduce(accs, tmp9.rearrange("p n d -> p d n"), AX.X, ALU.add)
                    e.tensor_tensor(accs, accs, Odl[:, 8, :], ALU.add)
                    e.reciprocal(rm[:, 0:1], accs[:, 64:65])
                    e.tensor_scalar(rm[:, 1:2], par[:, 1:2], rm[:, 0:1], None, ALU.mult)
                    e.tensor_scalar(rm[:, 0:1], par[:, 0:1], rm[:, 0:1], None, ALU.mult)
                    e.tensor_scalar(ob2[:, 0:64], accs[:, 0:64], rm[:, 0:1], None, ALU.mult)
                    return e.tensor_scalar(ob2[:, 64:128], accs[:, 0:64], rm[:, 1:2], None, ALU.mult)
                P.add('dve', combine, reads=['Odl', 'sels', 'par'], writes=['ob2', 'accs'])
                P.add('pe', lambda e: e.transpose(pT[:, 0, :], ob2, ident[:]), reads=['ob2', 'ident'], writes=['B2'])
                P.add('act', lambda e: e.activation(obT, pT[:, 0, :], AF.Copy), reads=['B2'], writes=['obT', 'B2'])

                def o_place(e):
                    T4 = obT.rearrange("p (k a t) -> p k a t", k=8, a=2)
                    return e.tensor_tensor(oTs[:, :, b * 8:(b + 1) * 8], T4[:, :, 0, :], T4[:, :, 1, :], ALU.add)
                P.add('dve', o_place, reads=['obT'], writes=['oTs'])

            for b in range(16):
                sample_batch(b)
            wo_and_residual(16, oTs, 'oTs')

            P.barrier(['arena', 'arena0', 'arena1', 'uT_all', 'rl'] + [('aT', f) for f in range(4)])
            mlp(1, 5, ())

        for tt in range(17):
            if tt < 16:
                P.add('sync', (lambda tt: lambda e: [e.dma_start(out=y_p[tt * 128:(tt + 1) * 128, :], in_=h[:, tt, :])])(tt),
                      reads=[('h', tt)], dma=('yout', 1))
            else:
                P.add('sync', lambda e: [e.dma_start(out=y_s[:, :], in_=h[:, 16, :])], reads=[('h', 16)],
                      dma=('yout', 1))

        cum = P.emit(lambda n: es.enter_context(nc.semaphore(n.replace(':', '_'))),
                     lambda e: e.memset(dummy[:, 0:1], 0.0), counts)
    return nc, cum


_NC_CACHE = {}


def _rope_table():
    half = 8
    inv = (np.float32(500000.0) ** (-np.arange(half, dtype=np.float32) / np.float32(half))).astype(np.float32)
    tab = np.zeros((128, 17, 16), np.float32)
    for tt in range(17):
        if tt < 16:
            pos = (tt * 128 + np.arange(128)).astype(np.float32)
        else:
            pos = (2048 + (np.arange(128) % 8)).astype(np.float32)
        ang = pos[:, None] * inv[None, :]
        tab[:, tt, 0:8] = np.cos(ang)
        tab[:, tt, 8:16] = np.sin(ang)
    return tab.reshape(128, -1)


def _host_shared(inp):
    f32 = np.float32
    m = {}
    wbm = np.zeros((4, 2, 16, NJ, 2, 2, 64), f32)
    for ri, key in enumerate(["ssm_b_re", "ssm_b_im"]):
        B5 = inp[key][0].reshape(NJ, 2, 64, 16)
        for j in range(NJ):
            for g2 in range(2):
                wbm[j % 4, g2, :, j, ri, g2, :] = B5[j, g2].T
    m["wb"] = wbm.reshape(128, -1)
    cxm = np.zeros((2, 64, NJ, 2, 4, 2, 16), f32)
    for ri, key in enumerate(["ssm_c_re", "ssm_c_im"]):
        C5 = inp[key][0].reshape(NJ, 2, 16, 64)
        for j in range(NJ):
            for g2 in range(2):
                cxm[g2, :, j, ri, j % 4, g2, :] = C5[j, g2].T
    m["cx"] = cxm.reshape(128, -1)
    lamm = np.zeros((128, 3 * NJ), f32)
    lamm[:, 0:NJ] = inp["ssm_lambda_re"][0].reshape(NJ, 2, 64).transpose(1, 2, 0).reshape(128, NJ)
    lamm[:, NJ:2 * NJ] = inp["ssm_lambda_im"][0].reshape(NJ, 2, 64).transpose(1, 2, 0).reshape(128, NJ)
    lamm[:, 2 * NJ:] = np.broadcast_to(inp["ssm_log_dt"][0].reshape(NJ, 2).T[:, None, :], (2, 64, NJ)).reshape(128, NJ)
    m["lam"] = lamm
    cols = np.zeros((128, 48), f32)
    for i, v in enumerate([inp["ssm_norm"][0], inp["ssm_d"][0], inp["mlp_norm"][0], inp["kv_norm"],
                           inp["attn_norm"][0], inp["mlp_norm"][1]]):
        cols[:, 8 * i:8 * i + 8] = v.reshape(8, 128).T
    m["cols"] = cols
    knb = np.zeros((128, 128), f32)
    knb[:, 0:64] = inp["k_norm"][None, :]
    knb[:, 64:128] = inp["q_norm"][0][None, :]
    m["knb"] = knb
    m["rope"] = _rope_table()

    def kmaj(w):
        K_, N_ = w.shape
        return np.ascontiguousarray(w.reshape(K_ // 128, 128, N_).transpose(1, 0, 2).reshape(128, -1))
    m["wglu"] = kmaj(inp["ssm_w_glu"][0])
    m["wkv"] = kmaj(inp["w_kv"])
    m["wq"] = kmaj(inp["w_q"][0])
    m["wo"] = kmaj(inp["w_o"][0])
    for l in range(2):
        m["wup%d" % l] = kmaj(inp["w_up"][l])
        m["wdn%d" % l] = kmaj(inp["w_down"][l])
    return m


def _host_core(inp, c):
    f32 = np.float32
    m = {}
    m["xp"] = np.ascontiguousarray(inp["x_prompt"][c])
    m["xs"] = np.ascontiguousarray(inp["x_sample"][16 * c:16 * c + 16].reshape(128, D))
    h0m = np.zeros((2, 64, NJ, 2, 16), f32)
    for ri, key in enumerate(["state_ssm_re", "state_ssm_im"]):
        S = inp[key][0, 16 * c:16 * c + 16].reshape(16, NJ, 2, 64)
        h0m[:, :, :, ri, :] = S.transpose(2, 3, 1, 0)
    m["h0"] = h0m.reshape(128, -1)
    m["pt"] = np.ascontiguousarray(inp["page_table"][16 * c:16 * c + 16].reshape(1, 256).astype(np.int32))
    m["ck"] = inp["cache_k"].reshape(NPOOL * 128, 1024)
    m["cv"] = inp["cache_v"].reshape(NPOOL * 128, 1024)
    return m


def kernel(**inp):
    inp = {k: np.asarray(v) for k, v in inp.items()}
    if "nc" not in _NC_CACHE:
        _, cum = build_nc(None)
        _NC_CACHE["nc"], _ = build_nc(cum)
    nc = _NC_CACHE["nc"]
    shared = _host_shared(inp)
    in_maps = []
    for c in range(NCORE):
        m = dict(shared)
        m.update(_host_core(inp, c))
        in_maps.append(m)
    res = run_bass_kernel_spmd(nc, in_maps, core_ids=list(range(NCORE)))
    R = res.results
    f32 = np.float32
    y_prompt = np.zeros((8, 2048, D), f32)
    y_sample = np.zeros((128, 8, D), f32)
    k_prompt = np.zeros((8, 2048, 16, 64), f32)
    v_prompt = np.zeros((8, 2048, 16, 64), f32)
    k_sample = np.zeros((128, 8, 16, 64), f32)
    v_sample = np.zeros((128, 8, 16, 64), f32)
    srp = np.zeros((1, 8, G, PST), f32)
    sip = np.zeros((1, 8, G, PST), f32)
    srs = np.zeros((1, 128, G, PST), f32)
    sis = np.zeros((1, 128, G, PST), f32)
    for c in range(NCORE):
        r = R[c]
        y_prompt[c] = np.asarray(r["y_p"])
        y_sample[16 * c:16 * c + 16] = np.asarray(r["y_s"]).reshape(16, 8, D)
        k_prompt[c] = np.asarray(r["k_p"]).reshape(2048, 16, 64)
        v_prompt[c] = np.asarray(r["v_p"]).reshape(2048, 16, 64)
        k_sample[16 * c:16 * c + 16] = np.asarray(r["k_s"]).reshape(16, 8, 16, 64)
        v_sample[16 * c:16 * c + 16] = np.asarray(r["v_s"]).reshape(16, 8, 16, 64)
        sp = np.asarray(r["st_p"]).reshape(2, 64, NJ, 2)
        srp[0, c] = sp[..., 0].transpose(2, 0, 1).reshape(G, PST)
        sip[0, c] = sp[..., 1].transpose(2, 0, 1).reshape(G, PST)
        s_ = np.asarray(r["st_s"]).reshape(2, 64, NJ, 2, 16)
        srs[0, 16 * c:16 * c + 16] = s_[:, :, :, 0, :].transpose(3, 2, 0, 1).reshape(16, G, PST)
        sis[0, 16 * c:16 * c + 16] = s_[:, :, :, 1, :].transpose(3, 2, 0, 1).reshape(16, G, PST)
    return (y_prompt, y_sample, k_prompt, v_prompt, k_sample, v_sample, srp, sip, srs, sis)
```

```python
import numpy as np
from contextlib import ExitStack
import concourse.bass as bass
import concourse.mybir as mybir
from concourse.bass_utils import run_bass_kernel_spmd

F32 = mybir.dt.float32
BF16 = mybir.dt.bfloat16
I32 = mybir.dt.int32
ALU = mybir.AluOpType
AF = mybir.ActivationFunctionType
AX = mybir.AxisListType

NCORE = 8
D = 1024
G, PST, CG = 64, 64, 16
NJ = 32
EPS = 1e-6


class Prog:
    def __init__(self, nc):
        self.nc = nc
        self.ops = []
        self.last_w = {}
        self.readers = {}

    def add(self, eng, fn, reads=(), writes=(), dma=None):
        deps = set()
        for r in reads:
            if r in self.last_w:
                deps.add(self.last_w[r])
        for w in writes:
            if w in self.last_w:
                deps.add(self.last_w[w])
            deps |= set(self.readers.get(w, ()))
        i = len(self.ops)
        self.ops.append(dict(eng=eng, fn=fn, deps=deps, dma=dma))
        for r in reads:
            self.readers.setdefault(r, []).append(i)
        for w in writes:
            self.last_w[w] = i
            self.readers[w] = []
        return i

    def _bar_fn(self, e):
        return e.memset(self.bar_ap, 0.0)

    def barrier(self, new_names, dummy_fn=None):
        names = set(self.last_w) | set(self.readers) | set(new_names)
        self.add('dve', self._bar_fn, reads=(), writes=list(names))

    def emit(self, semctx, pool_dummy, counts):
        nc = self.nc
        ops = self.ops
        engs = ['sync', 'act', 'dve', 'pool', 'pe']
        slot_cnt = {}
        dmaval = [None] * len(ops)
        for i, o in enumerate(ops):
            if o['dma'] is not None:
                slot, n = o['dma']
                slot_cnt[slot] = slot_cnt.get(slot, 0) + n
                dmaval[i] = ('dma:' + slot, 16 * slot_cnt[slot])
        semnames = list(engs) + ['dma:' + s for s in slot_cnt]
        sems = {n: semctx(n) for n in semnames}
        per_eng = {e: [] for e in engs}
        for i, o in enumerate(ops):
            per_eng[o['eng']].append(i)
        cum = [0] * len(ops)

        class Proxy:
            def __init__(pself, e, ename):
                pself.e = e
                pself.ename = ename
                pself.count = 0
                pself.selfsync = ename in ('act', 'dve', 'pool')
                pself.nosync = False

            def __getattr__(pself, name):
                f = getattr(pself.e, name)
                if name in ('wait_ge', 'dma_start', 'indirect_dma_start'):
                    return f
                if name in ('nosync',):
                    return object.__getattribute__(pself, name)

                def w(*a, **k):
                    if pself.selfsync and pself.count > 0 and not pself.nosync:
                        pself.e.wait_ge(sems[pself.ename], pself.count)
                    ins = f(*a, **k)
                    ins.then_inc(sems[pself.ename], 1)
                    pself.count += 1
                    return ins
                return w

        def run_engine(ename, e):
            waited = {}
            px = Proxy(e, ename)
            for i in per_eng[ename]:
                o = ops[i]
                nw = 0
                for d in sorted(o['deps']):
                    od = ops[d]
                    if od['dma'] is None and od['eng'] == ename:
                        continue
                    nw += 1
                    if od['dma'] is not None:
                        sn, v = dmaval[d]
                    else:
                        sn, v = od['eng'], (counts[d] if counts is not None else 1)
                    if waited.get(sn, 0) >= v:
                        continue
                    waited[sn] = v
                    e.wait_ge(sems[sn], v)
                if ename == 'pool' and o['dma'] is not None and nw > 0:
                    pool_dummy(px)
                r = o['fn'](px)
                if o['dma'] is not None:
                    sn, v = dmaval[i]
                    assert len(r) == o['dma'][1], (len(r), o['dma'])
                    for ins in r:
                        ins.then_inc(sems[sn], 16)
                cum[i] = px.count
            if ename == 'sync':
                for s, c in slot_cnt.items():
                    e.wait_ge(sems['dma:' + s], 16 * c)

        with nc.Block() as block:
            @block.sync
            def _(e):
                run_engine('sync', e)

            @block.scalar
            def _(e):
                run_engine('act', e)

            @block.vector
            def _(e):
                run_engine('dve', e)

            @block.gpsimd
            def _(e):
                run_engine('pool', e)

            @block.tensor
            def _(e):
                run_engine('pe', e)
        if counts is not None:
            assert counts == cum
        return cum


NPOOL = 2560
STAGE = 5


def build_nc(counts=None):
    nc = bass.Bass("TRN2", target_bir_lowering=False)

    def din(name, shape, dt=F32):
        return nc.dram_tensor(name, list(shape), dt, kind="ExternalInput").ap()

    def dout(name, shape, dt=F32):
        return nc.dram_tensor(name, list(shape), dt, kind="ExternalOutput").ap()

    xp = din("xp", [2048, D])
    xs = din("xs", [128, D])
    wb_d = din("wb", [128, NJ * 2 * 128])
    cx_d = din("cx", [128, NJ * 2 * 128])
    lam_d = din("lam", [128, 3 * NJ])
    h0_d = din("h0", [128, NJ * 2 * 16])
    cols_d = din("cols", [128, 48])
    knb_d = din("knb", [128, 128])
    rope_d = din("rope", [128, 17 * 16])
    wglu_d = din("wglu", [128, 8 * 2048])
    wkv_d = din("wkv", [128, 8 * 2048])
    wq_d = din("wq", [128, 8 * 1024])
    ck_d = din("ck", [NPOOL * 128, 1024])
    cv_d = din("cv", [NPOOL * 128, 1024])
    pt_d = din("pt", [1, 256], I32)
    wo_d = din("wo", [128, 8 * 1024])
    wup_d = [din("wup%d" % l, [128, 8 * 4096]) for l in range(2)]
    wdn_d = [din("wdn%d" % l, [128, 32 * 1024]) for l in range(2)]
    st_p = dout("st_p", [128, NJ * 2])
    st_s = dout("st_s", [128, NJ * 2 * 16])
    k_p = dout("k_p", [2048, D])
    v_p = dout("v_p", [2048, D])
    k_s = dout("k_s", [128, D])
    v_s = dout("v_s", [128, D])
    y_p = dout("y_p", [2048, D])
    y_s = dout("y_s", [128, D])

    with ExitStack() as es:
        def sb(name, shape, dt=F32):
            return es.enter_context(nc.sbuf_tensor("s_" + name, list(shape), dt))

        def ps(name, shape, dt=F32):
            return es.enter_context(nc.psum_tensor("q_" + name, list(shape), dt))

        P = Prog(nc)

        h = sb("h", [128, 17, D])
        arena = sb("arena", [128, 16384], BF16)
        arena2 = sb("arena2", [128, 16896])
        Bk = [ps("bank%d" % i, [128, 512]) for i in range(8)]
        identf = sb("identf", [128, 128])
        ident = sb("ident", [128, 128], BF16)
        dummy = sb("dummy_t", [128, 8])
        P.bar_ap = dummy[:, 1:2]
        lam = sb("lam", [128, 3 * NJ])
        cols = sb("cols", [128, 48])
        knb = sb("knb", [128, 128])
        rope = sb("rope", [128, 17, 16])
        NPW = 9
        apr = sb("apr", [128, NPW, NJ])
        api = sb("api", [128, NPW, NJ])
        apn = sb("apn", [128, NPW, NJ])
        fr = sb("fr", [128, NJ])
        fi = sb("fi", [128, NJ])
        tsm = [sb("tsm%d" % i, [128, NJ]) for i in range(10)]
        xend = sb("xend", [128, NJ, 2])
        scr = sb("scr", [128, 6144])

        def scb(lo, hi):
            return scr[:, lo:hi].bitcast(BF16)
        uT = scb(0, 2048).rearrange("p (k n) -> p k n", k=8)
        xrb = [scb(2048 + 256 * i, 2304 + 256 * i) for i in range(2)]
        xib = [scb(2560 + 256 * i, 2816 + 256 * i) for i in range(2)]
        junk2 = scr[:, 3072:4096]
        tmpc = scr[:, 4096:4608]
        ysb = scr[:, 4608:5120]
        aT = scb(5120, 5632).rearrange("p (f n) -> p f n", f=4)
        rl = scr[:, 5632:5888]
        junk = sb("junk", [128, D])
        xn = sb("xn", [128, D], BF16)
        ss = sb("ss", [128, 4])
        ssh = sb("ssh", [128, 4, 16])
        KTs = scb(5088, 5600).rearrange("p (k n) -> p k n", k=8)
        Vs_bf = scb(5600, 6112)

        def a2b(lo, hi):
            return arena2[:, lo:hi].bitcast(BF16)
        wb = a2b(0, 4096).rearrange("p (a b c) -> p a b c", a=NJ, b=2)
        cx = a2b(4096, 8192).rearrange("p (a b c) -> p a b c", a=NJ, b=2)
        bufs = [[arena2[:, 8192 + (pp * 2 + ri) * 768: 8192 + (pp * 2 + ri + 1) * 768] for ri in range(2)]
                for pp in range(2)]
        sbufS = [[arena2[:, 11264 + (pp * 2 + ri) * 192: 11264 + (pp * 2 + ri + 1) * 192]
                  .rearrange("p (b t) -> p b t", t=12) for ri in range(2)] for pp in range(2)]
        ygT = a2b(12032, 14080).rearrange("p (k n) -> p k n", k=8)
        h0 = arena2[:, 14080:15104].rearrange("p (a b c) -> p a b c", a=NJ, b=2)
        st_s_sb = arena2[:, 15104:16128].rearrange("p (a b c) -> p a b c", a=NJ, b=2)
        uT_all = a2b(0, 8704).rearrange("p (k n) -> p k n", k=8)
        KT = a2b(0, 8192).rearrange("p (k n) -> p k n", k=8)
        Vp = a2b(8192, 16512).rearrange("p (t h d) -> p t h d", t=16, h=16)
        rows_i = arena2[:, 8812:9068].bitcast(I32)
        pt_i = arena2[:, 9068:9324].bitcast(I32)
        S5RES = ['wb', 'cx', 'bufA', 'bufB', 'ygT', 'h0', 'res']
        wglu = arena[:, :].rearrange("p (k n) -> p k n", k=8)
        wkv = wglu

        def wup_v(s):
            return arena[:, s * 8192: s * 8192 + 4096].rearrange("p (k n) -> p k n", k=8)

        def wdn_v(s):
            return arena[:, s * 8192 + 4096: s * 8192 + 8192].rearrange("p (f n) -> p f n", f=4)

        pPr, pPi = Bk[0], Bk[1]
        pT = Bk[2][:, :].bitcast(BF16).rearrange("p (k n) -> p k n", k=8)
        pT2 = Bk[3][:, :].bitcast(BF16).rearrange("p (k n) -> p k n", k=8)
        yT = Bk[3]
        zb = Bk[4:8]

        def mkident(e):
            e.memset(identf[:], 0.0)
            e.memset(dummy[:], 0.0)
            return e.affine_select(identf[:], identf[:], [[-1, 128]], ALU.not_equal, 1.0,
                                   base=0, channel_multiplier=1)
        P.add('pool', mkident, writes=['identf'])
        P.add('dve', lambda e: e.tensor_copy(ident[:], identf[:]), reads=['identf'], writes=['ident'])

        P.add('sync', lambda e: [e.dma_start(out=lam[:], in_=lam_d[:, :]),
                                 e.dma_start(out=cols[:], in_=cols_d[:, :]),
                                 e.dma_start(out=knb[:], in_=knb_d[:, :]),
                                 e.dma_start(out=rope[:].rearrange("p a b -> p (a b)"), in_=rope_d[:, :]),
                                 e.dma_start(out=h0.rearrange("p a b c -> p (a b c)"), in_=h0_d[:, :])],
              writes=['lam', 'cols', 'knb', 'rope', 'h0'], dma=('params', 5))
        P.add('pool', lambda e: [e.dma_start(out=wb[:, 8 * q:8 * q + 8].rearrange("p a b c -> p (a b c)"),
                                             in_=wb_d[:, 2048 * q:2048 * (q + 1)]) for q in range(4)],
              writes=['wb'], dma=('wbl', 4))
        P.add('pool', lambda e: [e.dma_start(out=cx[:, 8 * q:8 * q + 8].rearrange("p a b c -> p (a b c)"),
                                             in_=cx_d[:, 2048 * q:2048 * (q + 1)]) for q in range(4)],
              writes=['cx'], dma=('cx', 4))
        if STAGE >= 2:
            P.add('pool', lambda e: [e.dma_start(out=wglu[:, kc, :], in_=wglu_d[:, 2048 * kc:2048 * (kc + 1)])
                                     for kc in range(8)], writes=['arena'], dma=('arena', 8))
        for t in range(16):
            P.add('sync', (lambda t: lambda e: [e.dma_start(out=h[:, t, :], in_=xp[t * 128:(t + 1) * 128, :])])(t),
                  writes=[('h', t)], dma=('h%d' % t, 1))
        P.add('sync', lambda e: [e.dma_start(out=h[:, 16, :], in_=xs[:, :])], writes=[('h', 16)], dma=('h16', 1))

        lr, li, ld = lam[:, 0:NJ], lam[:, NJ:2 * NJ], lam[:, 2 * NJ:3 * NJ]
        dtv, rho, th, mag, kf, r_, s2, s4, c2, den = [t[:] for t in tsm]
        TWO_PI = float(2 * np.pi)
        MAGIC = 12582912.0
        P.add('act', lambda e: e.activation(dtv, ld, AF.Exp), reads=['lam'], writes=['dtv'])
        P.add('dve', lambda e: e.tensor_tensor(rho, lr, dtv, ALU.mult), reads=['lam', 'dtv'], writes=['rho'])
        P.add('dve', lambda e: e.tensor_tensor(th, li, dtv, ALU.mult), reads=['lam', 'dtv'], writes=['th'])
        P.add('act', lambda e: e.activation(mag, rho, AF.Exp), reads=['rho'], writes=['mag'])

        def rr(e):
            e.tensor_scalar(kf, th, 1.0 / TWO_PI, None, ALU.mult)
            e.tensor_scalar(kf, kf, MAGIC, None, ALU.add)
            e.tensor_scalar(kf, kf, -MAGIC, None, ALU.add)
            return e.scalar_tensor_tensor(r_, kf, -TWO_PI, th, ALU.mult, ALU.add)
        P.add('dve', rr, reads=['th'], writes=['r', 'kf'])
        P.add('act', lambda e: e.activation(s2, r_, AF.Sin, scale=0.5), reads=['r'], writes=['s2'])
        P.add('act', lambda e: e.activation(s4, r_, AF.Sin, scale=0.25), reads=['r'], writes=['s4'])

        def trig(e):
            e.tensor_tensor(c2, s4, s4, ALU.mult)
            e.tensor_scalar(c2, c2, -2.0, 1.0, ALU.mult, ALU.add)
            e.tensor_tensor(den, s2, s2, ALU.mult)
            e.tensor_scalar(den, den, -2.0, 1.0, ALU.mult, ALU.add)
            e.tensor_tensor(apr[:, 0, :], den, mag, ALU.mult)
            e.tensor_tensor(c2, c2, s2, ALU.mult)
            e.scalar_tensor_tensor(api[:, 0, :], c2, 2.0, mag, ALU.mult, ALU.mult)
            nr, tmp1, tmp2 = kf, r_, th
            e.tensor_scalar(nr, apr[:, 0, :], -1.0, None, ALU.add)
            e.tensor_tensor(den, lr, lr, ALU.mult)
            e.tensor_tensor(tmp1, li, li, ALU.mult)
            e.tensor_tensor(den, den, tmp1, ALU.add)
            e.reciprocal(den, den)
            e.tensor_tensor(tmp1, nr, lr, ALU.mult)
            e.tensor_tensor(tmp2, api[:, 0, :], li, ALU.mult)
            e.tensor_tensor(tmp1, tmp1, tmp2, ALU.add)
            e.tensor_tensor(fr[:], tmp1, den, ALU.mult)
            e.tensor_tensor(tmp1, api[:, 0, :], lr, ALU.mult)
            e.tensor_tensor(tmp2, nr, li, ALU.mult)
            e.tensor_tensor(tmp1, tmp1, tmp2, ALU.subtract)
            e.tensor_tensor(fi[:], tmp1, den, ALU.mult)
            for k in range(1, NPW):
                e.tensor_tensor(tmp1, apr[:, k - 1, :], apr[:, k - 1, :], ALU.mult)
                e.tensor_tensor(tmp2, api[:, k - 1, :], api[:, k - 1, :], ALU.mult)
                e.tensor_tensor(apr[:, k, :], tmp1, tmp2, ALU.subtract)
                e.scalar_tensor_tensor(api[:, k, :], apr[:, k - 1, :], 2.0, api[:, k - 1, :], ALU.mult, ALU.mult)
            return e.tensor_scalar(apn[:].rearrange("p a b -> p (a b)"), api[:].rearrange("p a b -> p (a b)"),
                                   -1.0, None, ALU.mult)
        P.add('dve', trig, reads=['s2', 's4', 'mag', 'lam', 'r', 'th', 'kf'], writes=['apw', 'f', 'r', 'th', 'kf'])

        def rms_to_uT(tt, dst, col0, gcol, dst_res, extra_w=()):
            def f1(e):
                e.scalar_tensor_tensor(junk[:], h[:, tt, :], 1.0, h[:, tt, :], ALU.mult, ALU.mult,
                                       accum_out=ss[:, 0:1])
                return e.tensor_scalar(ss[:, 1:2], ss[:, 0:1], 1.0 / D, EPS, ALU.mult, ALU.add)
            P.add('dve', f1, reads=[('h', tt)], writes=['ss', 'junk'])
            P.add('act', lambda e: e.activation(ss[:, 2:3], ss[:, 1:2], AF.Sqrt), reads=['ss'], writes=['ss2'])

            def f2(e):
                e.reciprocal(ss[:, 3:4], ss[:, 2:3])
                return e.tensor_scalar(xn[:], h[:, tt, :], ss[:, 3:4], None, ALU.mult)
            P.add('dve', f2, reads=['ss2', ('h', tt)], writes=['xn', 'ss'])

            def f3(e):
                r = None
                for kc in range(8):
                    r = e.transpose(pT[:, kc, :], xn[:, kc * 128:(kc + 1) * 128], ident[:])
                return r
            P.add('pe', f3, reads=['xn', 'ident'], writes=['B2'])

            def f4(e):
                r = None
                for kc in range(8):
                    r = e.activation(dst[:, kc, col0:col0 + 128], pT[:, kc, :], AF.Copy,
                                     scale=cols[:, gcol * 8 + kc:gcol * 8 + kc + 1])
                return r
            P.add('act', f4, reads=['B2', 'cols'], writes=[dst_res] + list(extra_w))

        LCH = 512
        PAD = 256

        def zero_bufs(e):
            return e.memset(arena2[:, 8192:12032], 0.0)
        P.add('pool', zero_bufs, writes=['bufA', 'bufB'])

        def s5_chunk(ci):
            sample = (ci == 4)
            L = 128 if sample else LCH
            tts = [16] if sample else [ci * 4 + q for q in range(4)]
            for q, tt in enumerate(tts):
                rms_to_uT(tt, uT, q * 128, 0, 'uT')
            for j in range(NJ):
                kc, j4 = j // 4, j % 4

                def bproj(e, j=j, kc=kc):
                    r = None
                    for ri, pp in enumerate((pPr, pPi)):
                        r = e.matmul(pp[:, 0:L], wb[:, j, ri, :], uT[:, kc, 0:L], start=True, stop=True)
                    return r
                P.add('pe', bproj, reads=['uT', 'wb'], writes=['B0', 'B1'])

                if sample:
                    A = [sbufS[0][ri][:, :, 4:12] for ri in range(2)]
                    pv = [pp[:, 0:128].rearrange("p (b t) -> p b t", t=8) for pp in (pPr, pPi)]
                    tv = tmpc[:, 0:128].rearrange("p (b t) -> p b t", t=8)
                else:
                    A = [bufs[0][ri][:, PAD:PAD + L] for ri in range(2)]
                    pv = [pp[:, 0:L] for pp in (pPr, pPi)]
                    tv = tmpc[:, 0:L]

                if sample:
                    tva = junk2[:, 0:128].rearrange("p (b t) -> p b t", t=8)
                    tvb = junk2[:, 512:640].rearrange("p (b t) -> p b t", t=8)
                else:
                    tva, tvb = junk2[:, 0:L], junk2[:, 512:512 + L]

                def evac_a(e, j=j, pv=pv, tva=tva, tvb=tvb):
                    e.activation(tva, pv[1], AF.Copy, scale=fi[:, j:j + 1])
                    return e.activation(tvb, pv[1], AF.Copy, scale=fr[:, j:j + 1])
                P.add('act', evac_a, reads=['B1', 'f'], writes=['tvab'])

                def evac(e, j=j, A=A, pv=pv, tva=tva, tvb=tvb):
                    frj, fij = fr[:, j:j + 1], fi[:, j:j + 1]
                    e.scalar_tensor_tensor(A[0], pv[0], frj, tva, ALU.mult, ALU.subtract)
                    e.nosync = True
                    r = e.scalar_tensor_tensor(A[1], pv[0], fij, tvb, ALU.mult, ALU.add)
                    e.nosync = False
                    a_r, a_i, a_n = apr[:, 0, j:j + 1], api[:, 0, j:j + 1], apn[:, 0, j:j + 1]
                    if sample:
                        c0 = [sbufS[0][ri][:, :, 4:5] for ri in range(2)]
                        er = h0[:, j, 0, :].unsqueeze(2)
                        ei = h0[:, j, 1, :].unsqueeze(2)
                    elif ci > 0:
                        c0 = [bufs[0][ri][:, PAD:PAD + 1] for ri in range(2)]
                        er, ei = xend[:, j, 0:1], xend[:, j, 1:2]
                    else:
                        return r
                    e.scalar_tensor_tensor(c0[0], er, a_r, c0[0], ALU.mult, ALU.add)
                    e.scalar_tensor_tensor(c0[0], ei, a_n, c0[0], ALU.mult, ALU.add)
                    e.scalar_tensor_tensor(c0[1], ei, a_r, c0[1], ALU.mult, ALU.add)
                    r = e.scalar_tensor_tensor(c0[1], er, a_i, c0[1], ALU.mult, ALU.add)
                    return r
                P.add('dve', evac, reads=['B0', 'B1', 'f', 'apw', 'xend', 'h0', 'tvab'], writes=['bufA'])

                nsteps = 3 if sample else 9
                assert nsteps % 2 == 1

                def scan(e, j=j):
                    r = None
                    cur = 0
                    for k in range(nsteps):
                        s = 1 << k
                        a_r, a_i, a_n = apr[:, k, j:j + 1], api[:, k, j:j + 1], apn[:, k, j:j + 1]
                        if sample:
                            X = [sbufS[cur][ri][:, :, 4:12] for ri in range(2)]
                            Xs = [sbufS[cur][ri][:, :, 4 - s:12 - s] for ri in range(2)]
                            Y = [sbufS[1 - cur][ri][:, :, 4:12] for ri in range(2)]
                        else:
                            X = [bufs[cur][ri][:, PAD:PAD + L] for ri in range(2)]
                            Xs = [bufs[cur][ri][:, PAD - s:PAD + L - s] for ri in range(2)]
                            Y = [bufs[1 - cur][ri][:, PAD:PAD + L] for ri in range(2)]
                        e.scalar_tensor_tensor(Y[0], Xs[0], a_r, X[0], ALU.mult, ALU.add)
                        if not sample:
                            e.nosync = True
                        e.scalar_tensor_tensor(Y[0], Xs[1], a_n, Y[0], ALU.mult, ALU.add)
                        e.scalar_tensor_tensor(Y[1], Xs[1], a_r, X[1], ALU.mult, ALU.add)
                        r = e.scalar_tensor_tensor(Y[1], Xs[0], a_i, Y[1], ALU.mult, ALU.add)
                        cur = 1 - cur
                    e.nosync = False
                    assert cur == 1
                    return r
                P.add('dve', scan, reads=['bufA', 'apw'], writes=['bufA', 'bufB'])

                if STAGE < 2:
                    continue
                pb = j % 2
                if sample:
                    R = [sbufS[1][ri][:, :, 4:12] for ri in range(2)]
                    ov = [t[:, 0:128].rearrange("p (b t) -> p b t", t=8) for t in (xrb[pb], xib[pb])]
                else:
                    R = [bufs[1][ri][:, PAD:PAD + L] for ri in range(2)]
                    ov = [xrb[pb][:, 0:L], xib[pb][:, 0:L]]

                def conv(e, R=R, ov=ov, j=j):
                    if sample:
                        for ri in range(2):
                            e.activation(st_s_sb[:, j, ri, :].unsqueeze(2), sbufS[1][ri][:, :, 11:12], AF.Copy)
                    else:
                        for ri in range(2):
                            e.activation(xend[:, j, ri:ri + 1], bufs[1][ri][:, PAD + L - 1:PAD + L], AF.Copy)
                    e.activation(ov[0], R[0], AF.Copy)
                    return e.activation(ov[1], R[1], AF.Copy, scale=-1.0)
                P.add('act', conv, reads=['bufB'], writes=['xb%d' % pb, 'xend', 'res'])

                def cproj(e, j=j, j4=j4, pb=pb):
                    e.matmul(yT[:, 0:L], cx[:, j, 0, :], xrb[pb][:, 0:L], start=(j4 == 0), stop=False)
                    return e.matmul(yT[:, 0:L], cx[:, j, 1, :], xib[pb][:, 0:L], start=False, stop=(j4 == 3))
                P.add('pe', cproj, reads=['xb%d' % pb, 'cx'], writes=['B3'])

                if j4 == 3:
                    def yfin(e, kc=kc):
                        return e.scalar_tensor_tensor(ysb[:, 0:L], uT[:, kc, 0:L], cols[:, 8 + kc:9 + kc],
                                                      yT[:, 0:L], ALU.mult, ALU.add)
                    P.add('dve', yfin, reads=['B3', 'uT', 'cols'], writes=['ysb'])
                    P.add('act', (lambda kc: lambda e: e.activation(ygT[:, kc, 0:L], ysb[:, 0:L],
                                                                    AF.Gelu_apprx_tanh))(kc),
                          reads=['ysb'], writes=['ygT'])

            if STAGE < 2:
                return
            for q, tt in enumerate(tts):
                def glu_mm(e, q=q):
                    r = None
                    for n in range(4):
                        for kc in range(8):
                            r = e.matmul(zb[n][:, :], ygT[:, kc, q * 128:(q + 1) * 128],
                                         wglu[:, kc, n * 512:(n + 1) * 512], start=(kc == 0), stop=(kc == 7))
                    return r
                P.add('pe', glu_mm, reads=['ygT', 'arena'], writes=['B4', 'B5', 'B6', 'B7'])

                def glu_sig(e):
                    e.activation(junk[:, 0:512], zb[2][:, :], AF.Sigmoid)
                    return e.activation(junk[:, 512:1024], zb[3][:, :], AF.Sigmoid)
                P.add('act', glu_sig, reads=['B6', 'B7'], writes=['junk'])

                def glu_fin(e, tt=tt):
                    e.tensor_tensor(junk[:, 0:512], zb[0][:, :], junk[:, 0:512], ALU.mult)
                    e.tensor_tensor(junk[:, 512:1024], zb[1][:, :], junk[:, 512:1024], ALU.mult)
                    return e.tensor_tensor(h[:, tt, :], h[:, tt, :], junk[:], ALU.add)
                P.add('dve', glu_fin, reads=['junk', 'B4', 'B5', ('h', tt)],
                      writes=['junk', ('h', tt), 'B4', 'B5', 'B6', 'B7'])

        for ci in range(5):
            s5_chunk(ci)

        P.add('sync', lambda e: [e.dma_start(out=st_p[:, :], in_=xend[:].rearrange("p a b -> p (a b)")),
                                 e.dma_start(out=st_s[:, :], in_=st_s_sb.rearrange("p a b c -> p (a b c)"))],
              reads=['xend', 'res'], dma=('out', 2))

        def mlp(l, gcol, first_extra):
            for tt in range(17):
                rms_to_uT(tt, uT_all, tt * 128, gcol, 'uT_all', extra_w=(first_extra if tt == 0 else ()))
            groups = [(g * 256, 256, [2 * g, 2 * g + 1]) for g in range(8)] + [(2048, 128, [16])]
            for e8 in range(8):
                s = e8 % 2

                def wload(e, e8=e8, s=s):
                    return [e.dma_start(out=wup_v(s),
                                        in_=wup_d[l].rearrange("p (k n) -> p k n", k=8)[:, :, e8 * 512:(e8 + 1) * 512]),
                            e.dma_start(out=wdn_v(s),
                                        in_=wdn_d[l].rearrange("p (f n) -> p f n", f=32)[:, e8 * 4:(e8 + 1) * 4, :])]
                P.add('pool', wload, writes=['arena%d' % s] + (['arena'] if e8 < 2 else []),
                      dma=('arena%d' % s, 2))
                for (c0, N, tiles) in groups:
                    for f in range(4):
                        ub = Bk[f % 2]

                        def up_mm(e, f=f, ub=ub, c0=c0, N=N, s=s):
                            r = None
                            for kc in range(8):
                                r = e.matmul(ub[:, 0:N], wup_v(s)[:, kc, f * 128:(f + 1) * 128],
                                             uT_all[:, kc, c0:c0 + N], start=(kc == 0), stop=(kc == 7))
                            return r
                        P.add('pe', up_mm, reads=['uT_all', 'arena%d' % s], writes=['B%d' % (f % 2)])

                        def relu2(e, f=f, ub=ub, N=N):
                            e.activation(rl[:, 0:N], ub[:, 0:N], AF.Relu)
                            return e.activation(aT[:, f, 0:N], rl[:, 0:N], AF.Square)
                        P.add('act', relu2, reads=['B%d' % (f % 2)], writes=[('aT', f), 'rl'])
                    for ti, tt in enumerate(tiles):
                        dn = [Bk[4 + 2 * ti], Bk[5 + 2 * ti]]

                        def dn_mm(e, ti=ti, dn=dn, s=s):
                            r = None
                            for hf in range(2):
                                for f in range(4):
                                    r = e.matmul(dn[hf][:, :], aT[:, f, ti * 128:(ti + 1) * 128],
                                                 wdn_v(s)[:, f, hf * 512:(hf + 1) * 512],
                                                 start=(f == 0), stop=(f == 3))
                            return r
                        P.add('pe', dn_mm, reads=[('aT', f) for f in range(4)] + ['arena%d' % s],
                              writes=['B%d' % (4 + 2 * ti), 'B%d' % (5 + 2 * ti)])

                        def dn_add(e, tt=tt, dn=dn):
                            e.tensor_tensor(h[:, tt, 0:512], h[:, tt, 0:512], dn[0][:, :], ALU.add)
                            return e.tensor_tensor(h[:, tt, 512:1024], h[:, tt, 512:1024], dn[1][:, :], ALU.add)
                        P.add('dve', dn_add, reads=['B%d' % (4 + 2 * ti), 'B%d' % (5 + 2 * ti), ('h', tt)],
                              writes=[('h', tt), 'B%d' % (4 + 2 * ti), 'B%d' % (5 + 2 * ti)])

        if STAGE >= 3:
            mlp(0, 2, tuple(S5RES))

            P.add('pool', lambda e: [e.dma_start(out=wkv[:, kc, :], in_=wkv_d[:, 2048 * kc:2048 * (kc + 1)])
                                     for kc in range(8)], writes=['arena', 'arena0', 'arena1'], dma=('arena', 8))

            def ones_col(e):
                return e.memset(Vp[:, :, :, 64:65], 1.0)
            P.add('pool', ones_col, writes=['Vp', 'uT_all'])

            def kv_tile(tt):
                rms_to_uT(tt, uT, 0, 3, 'uT')

                def kv_mm(e):
                    r = None
                    for n in range(4):
                        for kc in range(8):
                            r = e.matmul(zb[n][:, :], uT[:, kc, 0:128], wkv[:, kc, n * 512:(n + 1) * 512],
                                         start=(kc == 0), stop=(kc == 7))
                    return r
                P.add('pe', kv_mm, reads=['uT', 'arena'], writes=['B4', 'B5', 'B6', 'B7'])

                kraw = junk
                vraw = junk2

                def kv_evac(e):
                    e.activation(kraw[:, 0:512], zb[0][:, :], AF.Copy)
                    e.activation(kraw[:, 512:1024], zb[1][:, :], AF.Copy)
                    e.activation(vraw[:, 0:512], zb[2][:, :], AF.Copy)
                    return e.activation(vraw[:, 512:1024], zb[3][:, :], AF.Copy)
                P.add('act', kv_evac, reads=['B4', 'B5', 'B6', 'B7'], writes=['junk', 'junk2', 'B4', 'B5', 'B6', 'B7'])

                def head_norm_rope(e, X, gain_off, tt=tt):
                    X3 = X.rearrange("p (h d) -> p h d", h=16)
                    sq = tmpc[:, :].rearrange("p (h d) -> p h d", h=16)
                    for half in range(2):
                        e.tensor_tensor(sq, X3[:, :, half * 32:(half + 1) * 32], X3[:, :, half * 32:(half + 1) * 32],
                                        ALU.mult)
                        e.tensor_reduce(ssh[:, half, :], sq, AX.X, ALU.add)
                    e.tensor_tensor(ssh[:, 2, :], ssh[:, 0, :], ssh[:, 1, :], ALU.add)
                    return e.tensor_scalar(ssh[:, 2, :], ssh[:, 2, :], 1.0 / 64, EPS, ALU.mult, ALU.add)

                def k_norm1(e):
                    return head_norm_rope(e, kraw[:, :], 0)
                P.add('dve', k_norm1, reads=['junk'], writes=['ssh', 'tmpc'])
                P.add('act', lambda e: e.activation(ssh[:, 3, :], ssh[:, 2, :], AF.Sqrt), reads=['ssh'], writes=['ssh3'])

                def k_norm2(e, tt=tt):
                    X3 = kraw[:, :].rearrange("p (h d) -> p h d", h=16)
                    e.reciprocal(ssh[:, 2, :], ssh[:, 3, :])
                    e.tensor_tensor(X3, X3, ssh[:, 2, :].unsqueeze(2).to_broadcast([128, 16, 64]), ALU.mult)
                    e.tensor_tensor(X3, X3, knb[:, 0:64].unsqueeze(1).to_broadcast([128, 16, 64]), ALU.mult)
                    cs = rope[:, tt, 0:8].unsqueeze(1).to_broadcast([128, 16, 8])
                    sn = rope[:, tt, 8:16].unsqueeze(1).to_broadcast([128, 16, 8])
                    t1 = tmpc[:, 0:128].rearrange("p (h d) -> p h d", h=16)
                    t2 = tmpc[:, 128:256].rearrange("p (h d) -> p h d", h=16)
                    t3 = tmpc[:, 256:384].rearrange("p (h d) -> p h d", h=16)
                    x1, x2 = X3[:, :, 0:8], X3[:, :, 8:16]
                    e.tensor_tensor(t1, x1, sn, ALU.mult)
                    e.tensor_tensor(t2, x2, sn, ALU.mult)
                    e.tensor_tensor(t3, x1, cs, ALU.mult)
                    e.tensor_tensor(x1, t3, t2, ALU.subtract)
                    e.tensor_tensor(t3, x2, cs, ALU.mult)
                    e.tensor_tensor(x2, t3, t1, ALU.add)
                    return e.tensor_copy(xn[:], kraw[:, :])
                P.add('dve', k_norm2, reads=['ssh3', 'junk', 'knb', 'rope'], writes=['junk', 'ssh', 'tmpc', 'xn'])

                if tt < 16:
                    P.add('sync', (lambda tt: lambda e: [
                        e.dma_start(out=k_p[tt * 128:(tt + 1) * 128, :], in_=kraw[:, :]),
                        e.dma_start(out=v_p[tt * 128:(tt + 1) * 128, :], in_=vraw[:, :])])(tt),
                        reads=['junk', 'junk2'], dma=('kvout', 2))

                    def kT_tr(e):
                        r = None
                        for hp in range(8):
                            r = e.transpose(pT2[:, hp, :], xn[:, hp * 128:(hp + 1) * 128], ident[:])
                        return r
                    P.add('pe', kT_tr, reads=['xn', 'ident'], writes=['B3'])
                    P.add('act', (lambda tt: lambda e: e.activation(KT[:, :, tt * 128:(tt + 1) * 128], pT2[:, :, :],
                                                                    AF.Copy))(tt),
                          reads=['B3'], writes=['KT', 'B3'])
                    P.add('pool', (lambda tt: lambda e: e.tensor_copy(
                        Vp[:, tt, :, 0:64], vraw[:, :].rearrange("p (h d) -> p h d", h=16)))(tt),
                        reads=['junk2'], writes=['Vp'])
                else:
                    P.add('sync', lambda e: [e.dma_start(out=k_s[:, :], in_=kraw[:, :]),
                                             e.dma_start(out=v_s[:, :], in_=vraw[:, :])],
                          reads=['junk', 'junk2'], dma=('kvout', 2))

                    def kT_tr_s(e):
                        r = None
                        for hp in range(8):
                            r = e.transpose(pT2[:, hp, :], xn[:, hp * 128:(hp + 1) * 128], ident[:])
                        return r
                    P.add('pe', kT_tr_s, reads=['xn', 'ident'], writes=['B3'])
                    P.add('act', lambda e: e.activation(KTs[:, :, :], pT2[:, :, :], AF.Copy), reads=['B3'],
                          writes=['KTs', 'B3'])
                    P.add('pool', lambda e: e.tensor_copy(Vs_bf[:], vraw[:, :]), reads=['junk2'], writes=['Vs_bf'])
            for tt in range(17):
                kv_tile(tt)

            P.barrier(['arena', 'arena0', 'arena1', 'scr_attn'] + ['S%d' % i for i in range(4)] + ['O%d' % i for i in range(4)] + ['PT%d' % i for i in range(4)])
            wq = arena[:, 0:8192].rearrange("p (k n) -> p k n", k=8)
            wo = arena[:, 8192:16384].rearrange("p (k n) -> p k n", k=8)
            P.add('pool', lambda e: [e.dma_start(out=wq[:, kc, :], in_=wq_d[:, 1024 * kc:1024 * (kc + 1)])
                                     for kc in range(8)] +
                                    [e.dma_start(out=wo[:, kc, :], in_=wo_d[:, 1024 * kc:1024 * (kc + 1)])
                                     for kc in range(8)],
                  writes=['arena'], dma=('arena', 16))
            uTq = scb(0, 512).rearrange("p (k n) -> p k n", k=8)
            qT = scb(512, 1024).rearrange("p (k n) -> p k n", k=8)
            PTb = [scb(1024 + 128 * i, 1152 + 128 * i).rearrange("p (k n) -> p k n", k=2) for i in range(2)] + \
                  [scb(128 * i, 128 * i + 128).rearrange("p (k n) -> p k n", k=2) for i in range(1)]
            Sb = [Bk[0][:, 0:256], Bk[1][:, 0:256], Bk[3][:, 0:256]]
            Sn = ['B0', 'B1', 'B3']
            Ob = [Bk[6][:, 0:65], Bk[7][:, 0:65], Bk[5][:, 0:65]]
            On_ = ['B6', 'B7', 'B5']
            NBUF = 3
            acc = scr[:, 1280:2320].rearrange("p (h d) -> p h d", h=16)
            Gs = scr[:, 2320:2448].rearrange("p (h n) -> p h n", h=16)
            sel = scr[:, 2448:2576].rearrange("p (h n) -> p h n", h=16)
            cmpb = scr[:, 2576:3600]
            rnk = scr[:, 3600:3728]
            kmT = scr[:, 3728:3792]
            kmBD = scb(3792, 3856).rearrange("p (k a n) -> p k a n", k=8, a=2)
            maskT = scb(3856, 3920)
            rden = scr[:, 3920:3936]
            obf = scb(3936, 4448)
            oT = scb(4448, 4960).rearrange("p (k n) -> p k n", k=8)
            maskf = scr[:, 4960:5088]

            def mk_mask(e):
                e.memset(maskf, 1.0)
                e.affine_select(maskf, maskf, [[1, 128]], ALU.is_ge, 0.0, base=0, channel_multiplier=-1)
                return e.tensor_copy(maskT, maskf)
            P.add('pool', mk_mask, writes=['maskT', 'scr_attn'])

            def mk_kmean(e):
                e.tensor_reduce(kmT, KT.rearrange("p k (n s) -> p (k n) s", s=256), AX.X, ALU.add)
                e.memset(kmBD, 0.0)
                km3 = kmT.rearrange("p (k n) -> p k n", k=8)
                e.tensor_scalar(kmBD[0:64, :, 0, :], km3[0:64], 1.0 / 256, None, ALU.mult)
                return e.tensor_scalar(kmBD[64:128, :, 1, :], km3[64:128], 1.0 / 256, None, ALU.mult)
            P.add('dve', mk_kmean, reads=['KT', 'scr_attn'], writes=['kmBD'])

            def head_norm(Xap, gain_lo, tt, post_scale, tag):
                X3 = Xap.rearrange("p (h d) -> p h d", h=16)

                def n1(e):
                    sq = cmpb[:, 0:512].rearrange("p (h d) -> p h d", h=16)
                    for half in range(2):
                        e.tensor_tensor(sq, X3[:, :, half * 32:(half + 1) * 32], X3[:, :, half * 32:(half + 1) * 32],
                                        ALU.mult)
                        e.tensor_reduce(ssh[:, half, :], sq, AX.X, ALU.add)
                    e.tensor_tensor(ssh[:, 2, :], ssh[:, 0, :], ssh[:, 1, :], ALU.add)
                    return e.tensor_scalar(ssh[:, 2, :], ssh[:, 2, :], 1.0 / 64, EPS, ALU.mult, ALU.add)
                P.add('dve', n1, reads=[tag], writes=['ssh', 'cmpb'])
                P.add('act', lambda e: e.activation(ssh[:, 3, :], ssh[:, 2, :], AF.Sqrt), reads=['ssh'], writes=['ssh3'])

                def n2(e):
                    e.reciprocal(ssh[:, 2, :], ssh[:, 3, :])
                    e.tensor_tensor(X3, X3, ssh[:, 2, :].unsqueeze(2).to_broadcast([128, 16, 64]), ALU.mult)
                    e.tensor_tensor(X3, X3, knb[:, gain_lo:gain_lo + 64].unsqueeze(1).to_broadcast([128, 16, 64]),
                                    ALU.mult)
                    cs = rope[:, tt, 0:8].unsqueeze(1).to_broadcast([128, 16, 8])
                    sn = rope[:, tt, 8:16].unsqueeze(1).to_broadcast([128, 16, 8])
                    t1 = cmpb[:, 0:128].rearrange("p (h d) -> p h d", h=16)
                    t2 = cmpb[:, 128:256].rearrange("p (h d) -> p h d", h=16)
                    t3 = cmpb[:, 256:384].rearrange("p (h d) -> p h d", h=16)
                    x1, x2 = X3[:, :, 0:8], X3[:, :, 8:16]
                    e.tensor_tensor(t1, x1, sn, ALU.mult)
                    e.tensor_tensor(t2, x2, sn, ALU.mult)
                    e.tensor_tensor(t3, x1, cs, ALU.mult)
                    e.tensor_tensor(x1, t3, t2, ALU.subtract)
                    e.tensor_tensor(t3, x2, cs, ALU.mult)
                    e.tensor_tensor(x2, t3, t1, ALU.add)
                    return e.tensor_scalar(xn[:], Xap, post_scale, None, ALU.mult)
                P.add('dve', n2, reads=['ssh3', tag, 'knb', 'rope'], writes=[tag, 'ssh', 'cmpb', 'xn'])

            def q_for_tile(tt, dstT, dst_res):
                rms_to_uT(tt, uTq, 0, 4, 'uTq', extra_w=('PT2',))

                def q_mm(e):
                    r = None
                    for n in range(2):
                        for kc in range(8):
                            r = e.matmul(Bk[4 + n][:, :], uTq[:, kc, :], wq[:, kc, n * 512:(n + 1) * 512],
                                         start=(kc == 0), stop=(kc == 7))
                    return r
                P.add('pe', q_mm, reads=['uTq', 'arena', 'PT2'], writes=['B4', 'B5'])

                def q_evac(e):
                    e.activation(junk[:, 0:512], Bk[4][:, :], AF.Copy)
                    return e.activation(junk[:, 512:1024], Bk[5][:, :], AF.Copy)
                P.add('act', q_evac, reads=['B4', 'B5'], writes=['junk', 'B4', 'B5'])
                head_norm(junk[:, :], 64, tt, 0.125, 'junk')

                def q_tr(e):
                    r = None
                    for hp in range(8):
                        r = e.transpose(pT[:, hp, :], xn[:, hp * 128:(hp + 1) * 128], ident[:])
                    return r
                P.add('pe', q_tr, reads=['xn', 'ident'], writes=['B2'])
                P.add('act', lambda e: e.activation(dstT, pT[:, :, :], AF.Copy), reads=['B2'], writes=[dst_res, 'B2'])

            def wo_and_residual(tt, oT_ap, oT_res):
                def wo_mm(e):
                    r = None
                    for n in range(2):
                        for hp in range(8):
                            r = e.matmul(Bk[4 + n][:, :], oT_ap[:, hp, :], wo[:, hp, n * 512:(n + 1) * 512],
                                         start=(hp == 0), stop=(hp == 7))
                    return r
                P.add('pe', wo_mm, reads=[oT_res, 'arena'], writes=['B4', 'B5'])

                def wo_add(e):
                    e.tensor_tensor(h[:, tt, 0:512], h[:, tt, 0:512], Bk[4][:, :], ALU.add)
                    return e.tensor_tensor(h[:, tt, 512:1024], h[:, tt, 512:1024], Bk[5][:, :], ALU.add)
                P.add('dve', wo_add, reads=['B4', 'B5', ('h', tt)], writes=[('h', tt), 'B4', 'B5'])

            def attn_prompt_chunk(j):
                own = j // 2
                q_for_tile(j, qT[:, :, :], 'qT')
                if own > 3:
                    def gate_mm(e):
                        r = None
                        for hp in range(8):
                            r = e.matmul(Bk[3][:, hp * 16:(hp + 1) * 16], qT[:, hp, :],
                                         kmBD[:, hp, :, :].rearrange("p a n -> p (a n)"), start=True, stop=True)
                        return r
                    P.add('pe', gate_mm, reads=['qT', 'kmBD'], writes=['B3'])
                    P.add('act', lambda e: e.activation(Gs.rearrange("p h n -> p (h n)"), Bk[3][:, 0:128], AF.Copy),
                          reads=['B3'], writes=['Gs', 'B3'])

                    def selop(e):
                        g = Gs[:, :, 0:own]
                        c4 = cmpb[:, 0:16 * own * own].rearrange("p (h n m) -> p h n m", h=16, n=own)
                        e.tensor_tensor(c4, g.unsqueeze(2).to_broadcast([128, 16, own, own]),
                                        g.unsqueeze(3).to_broadcast([128, 16, own, own]), ALU.is_gt)
                        r3 = rnk[:, 0:16 * own].rearrange("p (h n) -> p h n", h=16)
                        e.tensor_reduce(r3, c4, AX.X, ALU.add)
                        return e.tensor_scalar(sel[:, :, 0:own], r3, 3.0, None, ALU.is_lt)
                    P.add('dve', selop, reads=['Gs'], writes=['sel', 'cmpb', 'rnk'])
                units = []
                for hh in range(16):
                    units.append((hh, own))
                    for n in range(own):
                        units.append((hh, n))

                def emit_qk(u, idx):
                    hh, n = u
                    hp, h2 = hh // 2, hh % 2
                    pb = idx % NBUF
                    tiles = [2 * n, 2 * n + 1] if n < own else list(range(2 * own, j + 1))

                    def qk(e):
                        r = None
                        for i, kt in enumerate(tiles):
                            r = e.matmul(Sb[pb][:, i * 128:(i + 1) * 128],
                                         KT[64 * h2:64 * h2 + 64, hp, kt * 128:(kt + 1) * 128],
                                         qT[64 * h2:64 * h2 + 64, hp, :], start=True, stop=True)
                        return r
                    P.add('pe', qk, reads=['KT', 'qT'], writes=[Sn[pb]])
                    nt = len(tiles)
                    P.add('act', lambda e: e.activation(PTb[pb][:, 0:nt, :].rearrange("p k n -> p (k n)"),
                                                        Sb[pb][:, 0:nt * 128], AF.Exp),
                          reads=[Sn[pb]], writes=['PT%d' % pb, Sn[pb]])
                    if n == own:
                        P.add('pool', lambda e: e.tensor_tensor(PTb[pb][:, nt - 1, :], PTb[pb][:, nt - 1, :], maskT,
                                                                ALU.mult),
                              reads=['PT%d' % pb, 'maskT'], writes=['PT%d' % pb])
                    return tiles

                def emit_pv(u, idx, tiles):
                    hh, n = u
                    pb = idx % NBUF
                    ob = Ob[pb]

                    def pv(e):
                        r = None
                        for i, kt in enumerate(tiles):
                            r = e.matmul(ob, PTb[pb][:, i, :], Vp[:, kt, hh, :], start=(i == 0),
                                         stop=(i == len(tiles) - 1))
                        return r
                    P.add('pe', pv, reads=['PT%d' % pb, 'Vp'], writes=[On_[pb]])
                    if n == own:
                        P.add('dve', lambda e: e.tensor_copy(acc[:, hh, :], ob), reads=[On_[pb]],
                              writes=['acc', On_[pb]])
                    elif own <= 3:
                        P.add('dve', lambda e: e.tensor_tensor(acc[:, hh, :], acc[:, hh, :], ob, ALU.add),
                              reads=[On_[pb], 'acc'], writes=['acc', On_[pb]])
                    else:
                        P.add('dve', lambda e: e.scalar_tensor_tensor(acc[:, hh, :], ob, sel[:, hh, n:n + 1],
                                                                      acc[:, hh, :], ALU.mult, ALU.add),
                              reads=[On_[pb], 'acc', 'sel'], writes=['acc', On_[pb]])
                pend = []
                for idx, u in enumerate(units):
                    tiles = emit_qk(u, idx)
                    pend.append((u, idx, tiles))
                    if len(pend) > 2:
                        emit_pv(*pend.pop(0))
                while pend:
                    emit_pv(*pend.pop(0))

                def fin(e):
                    e.reciprocal(rden, acc[:, :, 64:65].rearrange("p h o -> p (h o)"))
                    return e.tensor_tensor(obf.rearrange("p (h d) -> p h d", h=16), acc[:, :, 0:64],
                                           rden.unsqueeze(2).to_broadcast([128, 16, 64]), ALU.mult)
                P.add('dve', fin, reads=['acc'], writes=['obf', 'rden'])

                def o_tr(e):
                    r = None
                    for hp in range(8):
                        r = e.transpose(pT[:, hp, :], obf[:, hp * 128:(hp + 1) * 128], ident[:])
                    return r
                P.add('pe', o_tr, reads=['obf', 'ident'], writes=['B2'])
                P.add('act', lambda e: e.activation(oT[:, :, :], pT[:, :, :], AF.Copy), reads=['B2'],
                      writes=['oT', 'B2'])
                wo_and_residual(j, oT, 'oT')

            for j in range(16):
                attn_prompt_chunk(j)

            SN = ['qbd', 'Odl', 'ksum', 'kmTs', 'gTs', 'gate', 'sels', 'Vnb', 'PTo', 'ob2', 'oTs', 'maskh', 'masko',
                  'rows', 'accs', 'kTp0', 'kTp1', 'PTs0', 'PTs1', 'Kpg0', 'Kpg1', 'Kpg2', 'Vpg0', 'Vpg1', 'Vpg2',
                  'pt_i', 'rows_f', 'iota', 'ones', 'par', 'PTof', 'obT', 'B3l', 'B3o', 'B3g', 'B3t']
            P.barrier(SN)
            qbd = a2b(0, 1024).rearrange("p (b k a t) -> p b k a t", b=16, k=8, a=2)
            Kpg = [a2b(1024 + 512 * i, 1536 + 512 * i) for i in range(3)]
            Vpg = [a2b(2560 + 512 * i, 3072 + 512 * i) for i in range(3)]
            kTp = [a2b(4096 + 512 * i, 4608 + 512 * i).rearrange("p (k n) -> p k n", k=8) for i in range(2)]
            PTs = [a2b(5120 + 64 * i, 5184 + 64 * i) for i in range(2)]
            Odl = arena2[:, 5248:5833].rearrange("p (n d) -> p n d", n=9)
            ksum = arena2[:, 5840:5968].rearrange("p (n g k) -> p n g k", n=8, g=2)
            kmTs = a2b(5968, 6000).rearrange("p (k n) -> p k n", k=8)
            gTs = arena2[0:8, 6000:6128]
            gate = arena2[:, 6128:6136]
            cmps = arena2[:, 6136:6200].rearrange("p (n m) -> p n m", n=8)
            ranks = arena2[:, 6200:6208]
            sels = arena2[:, 6208:6216]
            Vnb = a2b(6216, 6728)
            PTof = arena2[0:8, 6728:6856]
            PTo = a2b(6856, 6920)
            ob2 = a2b(6920, 6984)
            oTs = a2b(6984, 7496).rearrange("p (k n) -> p k n", k=8)
            maskh = arena2[:, 7496:7512]
            par = arena2[:, 7512:7514]
            masko = arena2[0:8, 7514:7642]
            rows_f = arena2[:, 7642:7898]
            rm = arena2[:, 7898:7900]
            accs = arena2[:, 7900:7965]
            tmp9 = arena2[:, 7968:8488].rearrange("p (n d) -> p n d", n=8)
            ones_bf = a2b(8488, 8489)
            obT = a2b(8490, 8554)
            iota_p = arena2[:, 8554:8555]
            pt_f = arena2[:, 8556:8812]

            def s_consts(e):
                e.memset(maskh, 1.0)
                e.affine_select(maskh, maskh, [[-8, 16]], ALU.is_ge, 0.0, base=0, channel_multiplier=1)
                e.affine_select(maskh, maskh, [[8, 16]], ALU.is_ge, 0.0, base=7, channel_multiplier=-1)
                e.memset(masko, 1.0)
                e.affine_select(masko, masko, [[0, 16], [1, 8]], ALU.is_ge, 0.0, base=0, channel_multiplier=-1)
                e.memset(ones_bf, 1.0)
                e.memset(qbd.rearrange("p b k a t -> p (b k a t)"), 0.0)
                return e.iota(iota_p, [[0, 1]], base=0, channel_multiplier=1, allow_small_or_imprecise_dtypes=True)
            P.add('pool', s_consts, writes=['maskh', 'masko', 'ones', 'qbd', 'iota'])
            P.add('dve', lambda e: e.tensor_reduce(par, maskh.rearrange("p (k a) -> p a k", a=2), AX.X, ALU.add),
                  reads=['maskh'], writes=['par'])
            P.add('sync', lambda e: [e.dma_start(out=pt_i[:], in_=pt_d[0:1, :].partition_broadcast(128))],
                  writes=['pt_i'], dma=('pt', 1))

            def mk_rows(e):
                e.tensor_copy(pt_f, pt_i[:])
                return e.tensor_scalar(rows_f, pt_f, 128.0, iota_p, ALU.mult, ALU.add)
            P.add('dve', mk_rows, reads=['pt_i', 'iota'], writes=['rows_f'])
            P.add('pool', lambda e: e.tensor_copy(rows_i[:], rows_f), reads=['rows_f'], writes=['rows'])

            q_for_tile(16, qT[:, :, :], 'qT')

            def mk_qbd(e):
                e.tensor_copy(qbd[0:64, :, :, 0, :], qT[0:64].rearrange("p k (b t) -> p b k t", t=8))
                return e.tensor_copy(qbd[64:128, :, :, 1, :], qT[64:128].rearrange("p k (b t) -> p b k t", t=8))
            P.add('dve', mk_qbd, reads=['qT', 'qbd'], writes=['qbd'])

            def diag_extract(n, par_n):
                ba, bb = (Bk[4], Bk[5]) if par_n == 0 else (Bk[6], Bk[7])
                rn = ['B4', 'B5'] if par_n == 0 else ['B6', 'B7']

                def f(e):
                    j3 = junk[:, :].rearrange("p (h d) -> p h d", h=16)
                    e.tensor_tensor(j3[:, 0:8, :], ba[:, :].rearrange("p (h d) -> p h d", h=8),
                                    maskh[:, 0:8].unsqueeze(2).to_broadcast([128, 8, 64]), ALU.mult)
                    e.tensor_tensor(j3[:, 8:16, :], bb[:, :].rearrange("p (h d) -> p h d", h=8),
                                    maskh[:, 8:16].unsqueeze(2).to_broadcast([128, 8, 64]), ALU.mult)
                    e.tensor_reduce(Odl[:, n, 0:64], junk[:, :].rearrange("p (h d) -> p d h", h=16), AX.X, ALU.add)
                    return e.tensor_copy(Odl[:, n, 64:65], Bk[3][:, n:n + 1])
                P.add('dve', f, reads=rn + ['B3l', 'maskh'], writes=['junk', 'Odl'] + rn + ['B3l'])

            def sample_batch(b):
                def vsel(e):
                    e.matmul(Bk[4][0:8, :], ident[:, b * 8:(b + 1) * 8], Vs_bf[:, 0:512], start=True, stop=True)
                    return e.matmul(Bk[5][0:8, :], ident[:, b * 8:(b + 1) * 8], Vs_bf[:, 512:1024], start=True,
                                    stop=True)
                P.add('pe', vsel, reads=['ident', 'Vs_bf'], writes=['B4', 'B5'])

                def vsel_ev(e):
                    e.activation(Vnb[0:8, 0:512], Bk[4][0:8, :], AF.Copy)
                    return e.activation(Vnb[0:8, 512:1024], Bk[5][0:8, :], AF.Copy)
                P.add('act', vsel_ev, reads=['B4', 'B5'], writes=['Vnb', 'B4', 'B5'])

                for n in range(8):
                    par_n = n % 2
                    On = (Bk[4], Bk[5]) if par_n == 0 else (Bk[6], Bk[7])
                    rn = ['B4', 'B5'] if par_n == 0 else ['B6', 'B7']
                    for pg in range(2):
                        c = b * 16 + 2 * n + pg
                        s3, s2 = c % 3, c % 2
                        P.add('pool', (lambda c, s3: lambda e: [e.indirect_dma_start(
                            out=Kpg[s3], out_offset=None, in_=ck_d,
                            in_offset=bass.IndirectOffsetOnAxis(ap=rows_i[:, c:c + 1], axis=0))])(c, s3),
                            reads=['rows'], writes=['Kpg%d' % s3], dma=('kpg%d' % s3, 1))
                        P.add('pool', (lambda c, s3: lambda e: [e.indirect_dma_start(
                            out=Vpg[s3], out_offset=None, in_=cv_d,
                            in_offset=bass.IndirectOffsetOnAxis(ap=rows_i[:, c:c + 1], axis=0))])(c, s3),
                            reads=['rows'], writes=['Vpg%d' % s3], dma=('vpg%d' % s3, 1))

                        def ktr(e, s3=s3):
                            r = None
                            for hp in range(8):
                                r = e.transpose(pT[:, hp, :], Kpg[s3][:, hp * 128:(hp + 1) * 128], ident[:])
                            return r
                        P.add('pe', ktr, reads=['Kpg%d' % s3, 'ident'], writes=['B2'])
                        P.add('act', (lambda s2: lambda e: e.activation(kTp[s2][:, :, :], pT[:, :, :], AF.Copy))(s2),
                              reads=['B2'], writes=['kTp%d' % s2, 'B2'])
                        P.add('dve', (lambda s2, n, pg: lambda e: e.tensor_reduce(ksum[:, n, pg, :], kTp[s2][:, :, :],
                                                                                 AX.X, ALU.add))(s2, n, pg),
                              reads=['kTp%d' % s2], writes=['ksum'])

                        def st_mm(e, s2=s2):
                            r = None
                            for hp in range(8):
                                r = e.matmul(Bk[s2][:, hp * 16:(hp + 1) * 16], kTp[s2][:, hp, :],
                                             qbd[:, b, hp, :, :].rearrange("p a t -> p (a t)"), start=True, stop=True)
                            return r
                        P.add('pe', st_mm, reads=['kTp%d' % s2, 'qbd'], writes=['B%d' % s2])
                        P.add('act', (lambda s2: lambda e: e.activation(PTs[s2], Bk[s2][:, 0:128], AF.Exp))(s2),
                              reads=['B%d' % s2], writes=['PTs%d' % s2, 'B%d' % s2])

                        def pv_mm(e, s2=s2, s3=s3, pg=pg, On=On, n=n):
                            e.matmul(On[0][:, :], PTs[s2], Vpg[s3][:, 0:512], start=(pg == 0), stop=(pg == 1))
                            e.matmul(On[1][:, :], PTs[s2], Vpg[s3][:, 512:1024], start=(pg == 0), stop=(pg == 1))
                            return e.matmul(Bk[3][:, n:n + 1], PTs[s2], ones_bf[:, 0:1], start=(pg == 0),
                                            stop=(pg == 1))
                        P.add('pe', pv_mm, reads=['PTs%d' % s2, 'Vpg%d' % s3, 'ones'], writes=rn + ['B3l'])
                    diag_extract(n, par_n)

                def own_st(e):
                    r = None
                    for hp in range(8):
                        r = e.matmul(Bk[3][0:8, 128 + hp * 16:128 + (hp + 1) * 16], KTs[:, hp, b * 8:(b + 1) * 8],
                                     qbd[:, b, hp, :, :].rearrange("p a t -> p (a t)"), start=True, stop=True)
                    return r
                P.add('pe', own_st, reads=['KTs', 'qbd'], writes=['B3o'])
                P.add('act', lambda e: e.activation(PTof, Bk[3][0:8, 128:256], AF.Exp), reads=['B3o'],
                      writes=['PTof', 'B3o'])
                P.add('dve', lambda e: e.tensor_tensor(PTo[0:8, :], PTof, masko, ALU.mult), reads=['PTof', 'masko'],
                      writes=['PTo'])

                def own_pv(e):
                    e.matmul(Bk[4][:, :], PTo[0:8, :], Vnb[0:8, 0:512], start=True, stop=True)
                    e.matmul(Bk[5][:, :], PTo[0:8, :], Vnb[0:8, 512:1024], start=True, stop=True)
                    return e.matmul(Bk[3][:, 8:9], PTo[0:8, :], ones_bf[0:8, 0:1], start=True, stop=True)
                P.add('pe', own_pv, reads=['PTo', 'Vnb', 'ones'], writes=['B4', 'B5', 'B3l'])
                diag_extract(8, 0)

                def mk_km(e):
                    return e.tensor_tensor(kmTs.rearrange("p k n -> p n k"), ksum[:, :, 0, :], ksum[:, :, 1, :],
                                           ALU.add)
                P.add('dve', mk_km, reads=['ksum'], writes=['kmTs'])

                def g_mm(e):
                    r = None
                    for hp in range(8):
                        r = e.matmul(Bk[3][0:8, 256 + hp * 16:256 + (hp + 1) * 16], kmTs[:, hp, :],
                                     qbd[:, b, hp, :, :].rearrange("p a t -> p (a t)"), start=True, stop=True)
                    return r
                P.add('pe', g_mm, reads=['kmTs', 'qbd'], writes=['B3g'])
                P.add('act', lambda e: e.activation(gTs, Bk[3][0:8, 256:384], AF.Copy), reads=['B3g'],
                      writes=['gTs', 'B3g'])
                P.add('pe', lambda e: e.transpose(Bk[3][:, 400:408], gTs, identf[0:8, 0:8]), reads=['gTs', 'identf'],
                      writes=['B3t'])

                def selop(e):
                    e.tensor_copy(gate, Bk[3][:, 400:408])
                    e.tensor_tensor(cmps, gate.unsqueeze(1).to_broadcast([128, 8, 8]),
                                    gate.unsqueeze(2).to_broadcast([128, 8, 8]), ALU.is_gt)
                    e.tensor_reduce(ranks, cmps, AX.X, ALU.add)
                    return e.tensor_scalar(sels, ranks, 3.0, None, ALU.is_lt)
                P.add('dve', selop, reads=['B3t'], writes=['gate', 'sels', 'B3t'])

                def combine(e):
                    e.tensor_tensor(tmp9, Odl[:, 0:8, :], sels.unsqueeze(2).to_broadcast([128, 8, 65]), ALU.mult)
                    e.tensor_reduce(accs, tmp9.rearrange("p n d -> p d n"), AX.X, ALU.add)
                    e.tensor_tensor(accs, accs, Odl[:, 8, :], ALU.add)
                    e.reciprocal(rm[:, 0:1], accs[:, 64:65])
                    e.tensor_scalar(rm[:, 1:2], par[:, 1:2], rm[:, 0:1], None, ALU.mult)
                    e.tensor_scalar(rm[:, 0:1], par[:, 0:1], rm[:, 0:1], None, ALU.mult)
                    e.tensor_scalar(ob2[:, 0:64], accs[:, 0:64], rm[:, 0:1], None, ALU.mult)
                    return e.tensor_scalar(ob2[:, 64:128], accs[:, 0:64], rm[:, 1:2], None, ALU.mult)
                P.add('dve', combine, reads=['Odl', 'sels', 'par'], writes=['ob2', 'accs'])
                P.add('pe', lambda e: e.transpose(pT[:, 0, :], ob2, ident[:]), reads=['ob2', 'ident'], writes=['B2'])
                P.add('act', lambda e: e.activation(obT, pT[:, 0, :], AF.Copy), reads=['B2'], writes=['obT', 'B2'])

                def o_place(e):
                    T4 = obT.rearrange("p (k a t) -> p k a t", k=8, a=2)
                    return e.tensor_tensor(oTs[:, :, b * 8:(b + 1) * 8], T4[:, :, 0, :], T4[:, :, 1, :], ALU.add)
                P.add('dve', o_place, reads=['obT'], writes=['oTs'])

            for b in range(16):
                sample_batch(b)
            wo_and_residual(16, oTs, 'oTs')

            P.barrier(['arena', 'arena0', 'arena1', 'uT_all', 'rl'] + [('aT', f) for f in range(4)])
            mlp(1, 5, ())

        for tt in range(17):
            if tt < 16:
                P.add('sync', (lambda tt: lambda e: [e.dma_start(out=y_p[tt * 128:(tt + 1) * 128, :], in_=h[:, tt, :])])(tt),
                      reads=[('h', tt)], dma=('yout', 1))
            else:
                P.add('sync', lambda e: [e.dma_start(out=y_s[:, :], in_=h[:, 16, :])], reads=[('h', 16)],
                      dma=('yout', 1))

        cum = P.emit(lambda n: es.enter_context(nc.semaphore(n.replace(':', '_'))),
                     lambda e: e.memset(dummy[:, 0:1], 0.0), counts)
    return nc, cum


_NC_CACHE = {}


def _rope_table():
    half = 8
    inv = (np.float32(500000.0) ** (-np.arange(half, dtype=np.float32) / np.float32(half))).astype(np.float32)
    tab = np.zeros((128, 17, 16), np.float32)
    for tt in range(17):
        if tt < 16:
            pos = (tt * 128 + np.arange(128)).astype(np.float32)
        else:
            pos = (2048 + (np.arange(128) % 8)).astype(np.float32)
        ang = pos[:, None] * inv[None, :]
        tab[:, tt, 0:8] = np.cos(ang)
        tab[:, tt, 8:16] = np.sin(ang)
    return tab.reshape(128, -1)


def _host_shared(inp):
    f32 = np.float32
    m = {}
    wbm = np.zeros((4, 2, 16, NJ, 2, 2, 64), f32)
    for ri, key in enumerate(["ssm_b_re", "ssm_b_im"]):
        B5 = inp[key][0].reshape(NJ, 2, 64, 16)
        for j in range(NJ):
            for g2 in range(2):
                wbm[j % 4, g2, :, j, ri, g2, :] = B5[j, g2].T
    m["wb"] = wbm.reshape(128, -1)
    cxm = np.zeros((2, 64, NJ, 2, 4, 2, 16), f32)
    for ri, key in enumerate(["ssm_c_re", "ssm_c_im"]):
        C5 = inp[key][0].reshape(NJ, 2, 16, 64)
        for j in range(NJ):
            for g2 in range(2):
                cxm[g2, :, j, ri, j % 4, g2, :] = C5[j, g2].T
    m["cx"] = cxm.reshape(128, -1)
    lamm = np.zeros((128, 3 * NJ), f32)
    lamm[:, 0:NJ] = inp["ssm_lambda_re"][0].reshape(NJ, 2, 64).transpose(1, 2, 0).reshape(128, NJ)
    lamm[:, NJ:2 * NJ] = inp["ssm_lambda_im"][0].reshape(NJ, 2, 64).transpose(1, 2, 0).reshape(128, NJ)
    lamm[:, 2 * NJ:] = np.broadcast_to(inp["ssm_log_dt"][0].reshape(NJ, 2).T[:, None, :], (2, 64, NJ)).reshape(128, NJ)
    m["lam"] = lamm
    cols = np.zeros((128, 48), f32)
    for i, v in enumerate([inp["ssm_norm"][0], inp["ssm_d"][0], inp["mlp_norm"][0], inp["kv_norm"],
                           inp["attn_norm"][0], inp["mlp_norm"][1]]):
        cols[:, 8 * i:8 * i + 8] = v.reshape(8, 128).T
    m["cols"] = cols
    knb = np.zeros((128, 128), f32)
    knb[:, 0:64] = inp["k_norm"][None, :]
    knb[:, 64:128] = inp["q_norm"][0][None, :]
    m["knb"] = knb
    m["rope"] = _rope_table()

    def kmaj(w):
        K_, N_ = w.shape
        return np.ascontiguousarray(w.reshape(K_ // 128, 128, N_).transpose(1, 0, 2).reshape(128, -1))
    m["wglu"] = kmaj(inp["ssm_w_glu"][0])
    m["wkv"] = kmaj(inp["w_kv"])
    m["wq"] = kmaj(inp["w_q"][0])
    m["wo"] = kmaj(inp["w_o"][0])
    for l in range(2):
        m["wup%d" % l] = kmaj(inp["w_up"][l])
        m["wdn%d" % l] = kmaj(inp["w_down"][l])
    return m


def _host_core(inp, c):
    f32 = np.float32
    m = {}
    m["xp"] = np.ascontiguousarray(inp["x_prompt"][c])
    m["xs"] = np.ascontiguousarray(inp["x_sample"][16 * c:16 * c + 16].reshape(128, D))
    h0m = np.zeros((2, 64, NJ, 2, 16), f32)
    for ri, key in enumerate(["state_ssm_re", "state_ssm_im"]):
        S = inp[key][0, 16 * c:16 * c + 16].reshape(16, NJ, 2, 64)
        h0m[:, :, :, ri, :] = S.transpose(2, 3, 1, 0)
    m["h0"] = h0m.reshape(128, -1)
    m["pt"] = np.ascontiguousarray(inp["page_table"][16 * c:16 * c + 16].reshape(1, 256).astype(np.int32))
    m["ck"] = inp["cache_k"].reshape(NPOOL * 128, 1024)
    m["cv"] = inp["cache_v"].reshape(NPOOL * 128, 1024)
    return m


def kernel(**inp):
    inp = {k: np.asarray(v) for k, v in inp.items()}
    if "nc" not in _NC_CACHE:
        _, cum = build_nc(None)
        _NC_CACHE["nc"], _ = build_nc(cum)
    nc = _NC_CACHE["nc"]
    shared = _host_shared(inp)
    in_maps = []
    for c in range(NCORE):
        m = dict(shared)
        m.update(_host_core(inp, c))
        in_maps.append(m)
    res = run_bass_kernel_spmd(nc, in_maps, core_ids=list(range(NCORE)))
    R = res.results
    f32 = np.float32
    y_prompt = np.zeros((8, 2048, D), f32)
    y_sample = np.zeros((128, 8, D), f32)
    k_prompt = np.zeros((8, 2048, 16, 64), f32)
    v_prompt = np.zeros((8, 2048, 16, 64), f32)
    k_sample = np.zeros((128, 8, 16, 64), f32)
    v_sample = np.zeros((128, 8, 16, 64), f32)
    srp = np.zeros((1, 8, G, PST), f32)
    sip = np.zeros((1, 8, G, PST), f32)
    srs = np.zeros((1, 128, G, PST), f32)
    sis = np.zeros((1, 128, G, PST), f32)
    for c in range(NCORE):
        r = R[c]
        y_prompt[c] = np.asarray(r["y_p"])
        y_sample[16 * c:16 * c + 16] = np.asarray(r["y_s"]).reshape(16, 8, D)
        k_prompt[c] = np.asarray(r["k_p"]).reshape(2048, 16, 64)
        v_prompt[c] = np.asarray(r["v_p"]).reshape(2048, 16, 64)
        k_sample[16 * c:16 * c + 16] = np.asarray(r["k_s"]).reshape(16, 8, 16, 64)
        v_sample[16 * c:16 * c + 16] = np.asarray(r["v_s"]).reshape(16, 8, 16, 64)
        sp = np.asarray(r["st_p"]).reshape(2, 64, NJ, 2)
        srp[0, c] = sp[..., 0].transpose(2, 0, 1).reshape(G, PST)
        sip[0, c] = sp[..., 1].transpose(2, 0, 1).reshape(G, PST)
        s_ = np.asarray(r["st_s"]).reshape(2, 64, NJ, 2, 16)
        srs[0, 16 * c:16 * c + 16] = s_[:, :, :, 0, :].transpose(3, 2, 0, 1).reshape(16, G, PST)
        sis[0, 16 * c:16 * c + 16] = s_[:, :, :, 1, :].transpose(3, 2, 0, 1).reshape(16, G, PST)
    return (y_prompt, y_sample, k_prompt, v_prompt, k_sample, v_sample, srp, sip, srs, sis)
```

```python
import numpy as np
from contextlib import ExitStack
import concourse.bass as bass
import concourse.mybir as mybir
from concourse.bass_utils import run_bass_kernel_spmd

F32 = mybir.dt.float32
BF16 = mybir.dt.bfloat16
I32 = mybir.dt.int32
ALU = mybir.AluOpType
AF = mybir.ActivationFunctionType
AX = mybir.AxisListType

NCORE = 8
D = 1024
G, PST, CG = 64, 64, 16
NJ = 32
EPS = 1e-6


class Prog:
    def __init__(self, nc):
        self.nc = nc
        self.ops = []
        self.last_w = {}
        self.readers = {}

    def add(self, eng, fn, reads=(), writes=(), dma=None):
        deps = set()
        for r in reads:
            if r in self.last_w:
                deps.add(self.last_w[r])
        for w in writes:
            if w in self.last_w:
                deps.add(self.last_w[w])
            deps |= set(self.readers.get(w, ()))
        i = len(self.ops)
        self.ops.append(dict(eng=eng, fn=fn, deps=deps, dma=dma))
        for r in reads:
            self.readers.setdefault(r, []).append(i)
        for w in writes:
            self.last_w[w] = i
            self.readers[w] = []
        return i

    def _bar_fn(self, e):
        return e.memset(self.bar_ap, 0.0)

    def barrier(self, new_names, dummy_fn=None):
        names = set(self.last_w) | set(self.readers) | set(new_names)
        self.add('dve', self._bar_fn, reads=(), writes=list(names))

    def emit(self, semctx, pool_dummy, counts):
        nc = self.nc
        ops = self.ops
        engs = ['sync', 'act', 'dve', 'pool', 'pe']
        slot_cnt = {}
        dmaval = [None] * len(ops)
        for i, o in enumerate(ops):
            if o['dma'] is not None:
                slot, n = o['dma']
                slot_cnt[slot] = slot_cnt.get(slot, 0) + n
                dmaval[i] = ('dma:' + slot, 16 * slot_cnt[slot])
        semnames = list(engs) + ['dma:' + s for s in slot_cnt]
        sems = {n: semctx(n) for n in semnames}
        per_eng = {e: [] for e in engs}
        for i, o in enumerate(ops):
            per_eng[o['eng']].append(i)
        cum = [0] * len(ops)

        class Proxy:
            def __init__(pself, e, ename):
                pself.e = e
                pself.ename = ename
                pself.count = 0
                pself.selfsync = ename in ('act', 'dve', 'pool')
                pself.nosync = False

            def __getattr__(pself, name):
                f = getattr(pself.e, name)
                if name in ('wait_ge', 'dma_start', 'indirect_dma_start'):
                    return f
                if name in ('nosync',):
                    return object.__getattribute__(pself, name)

                def w(*a, **k):
                    if pself.selfsync and pself.count > 0 and not pself.nosync:
                        pself.e.wait_ge(sems[pself.ename], pself.count)
                    ins = f(*a, **k)
                    ins.then_inc(sems[pself.ename], 1)
                    pself.count += 1
                    return ins
                return w

        def run_engine(ename, e):
            waited = {}
            px = Proxy(e, ename)
            for i in per_eng[ename]:
                o = ops[i]
                nw = 0
                for d in sorted(o['deps']):
                    od = ops[d]
                    if od['dma'] is None and od['eng'] == ename:
                        continue
                    nw += 1
                    if od['dma'] is not None:
                        sn, v = dmaval[d]
                    else:
                        sn, v = od['eng'], (counts[d] if counts is not None else 1)
                    if waited.get(sn, 0) >= v:
                        continue
                    waited[sn] = v
                    e.wait_ge(sems[sn], v)
                if ename == 'pool' and o['dma'] is not None and nw > 0:
                    pool_dummy(px)
                r = o['fn'](px)
                if o['dma'] is not None:
                    sn, v = dmaval[i]
                    assert len(r) == o['dma'][1], (len(r), o['dma'])
                    for ins in r:
                        ins.then_inc(sems[sn], 16)
                cum[i] = px.count
            if ename == 'sync':
                for s, c in slot_cnt.items():
                    e.wait_ge(sems['dma:' + s], 16 * c)

        with nc.Block() as block:
            @block.sync
            def _(e):
                run_engine('sync', e)

            @block.scalar
            def _(e):
                run_engine('act', e)

            @block.vector
            def _(e):
                run_engine('dve', e)

            @block.gpsimd
            def _(e):
                run_engine('pool', e)

            @block.tensor
            def _(e):
                run_engine('pe', e)
        if counts is not None:
            assert counts == cum
        return cum


NPOOL = 2560
STAGE = 5


def build_nc(counts=None):
    nc = bass.Bass("TRN2", target_bir_lowering=False)

    def din(name, shape, dt=F32):
        return nc.dram_tensor(name, list(shape), dt, kind="ExternalInput").ap()

    def dout(name, shape, dt=F32):
        return nc.dram_tensor(name, list(shape), dt, kind="ExternalOutput").ap()

    xp = din("xp", [2048, D])
    xs = din("xs", [128, D])
    wb_d = din("wb", [128, NJ * 2 * 128])
    cx_d = din("cx", [128, NJ * 2 * 128])
    lam_d = din("lam", [128, 3 * NJ])
    h0_d = din("h0", [128, NJ * 2 * 16])
    cols_d = din("cols", [128, 48])
    knb_d = din("knb", [128, 128])
    rope_d = din("rope", [128, 17 * 16])
    wglu_d = din("wglu", [128, 8 * 2048])
    wkv_d = din("wkv", [128, 8 * 2048])
    wq_d = din("wq", [128, 8 * 1024])
    ck_d = din("ck", [NPOOL * 128, 1024])
    cv_d = din("cv", [NPOOL * 128, 1024])
    pt_d = din("pt", [1, 256], I32)
    wo_d = din("wo", [128, 8 * 1024])
    wup_d = [din("wup%d" % l, [128, 8 * 4096]) for l in range(2)]
    wdn_d = [din("wdn%d" % l, [128, 32 * 1024]) for l in range(2)]
    st_p = dout("st_p", [128, NJ * 2])
    st_s = dout("st_s", [128, NJ * 2 * 16])
    k_p = dout("k_p", [2048, D])
    v_p = dout("v_p", [2048, D])
    k_s = dout("k_s", [128, D])
    v_s = dout("v_s", [128, D])
    y_p = dout("y_p", [2048, D])
    y_s = dout("y_s", [128, D])

    with ExitStack() as es:
        def sb(name, shape, dt=F32):
            return es.enter_context(nc.sbuf_tensor("s_" + name, list(shape), dt))

        def ps(name, shape, dt=F32):
            return es.enter_context(nc.psum_tensor("q_" + name, list(shape), dt))

        P = Prog(nc)

        h = sb("h", [128, 17, D])
        arena = sb("arena", [128, 16384], BF16)
        arena2 = sb("arena2", [128, 16896])
        Bk = [ps("bank%d" % i, [128, 512]) for i in range(8)]
        identf = sb("identf", [128, 128])
        ident = sb("ident", [128, 128], BF16)
        dummy = sb("dummy_t", [128, 8])
        P.bar_ap = dummy[:, 1:2]
        lam = sb("lam", [128, 3 * NJ])
        cols = sb("cols", [128, 48])
        knb = sb("knb", [128, 128])
        rope = sb("rope", [128, 17, 16])
        NPW = 9
        apr = sb("apr", [128, NPW, NJ])
        api = sb("api", [128, NPW, NJ])
        apn = sb("apn", [128, NPW, NJ])
        fr = sb("fr", [128, NJ])
        fi = sb("fi", [128, NJ])
        tsm = [sb("tsm%d" % i, [128, NJ]) for i in range(10)]
        xend = sb("xend", [128, NJ, 2])
        scr = sb("scr", [128, 6144])

        def scb(lo, hi):
            return scr[:, lo:hi].bitcast(BF16)
        uT = scb(0, 2048).rearrange("p (k n) -> p k n", k=8)
        xrb = [scb(2048 + 256 * i, 2304 + 256 * i) for i in range(2)]
        xib = [scb(2560 + 256 * i, 2816 + 256 * i) for i in range(2)]
        junk2 = scr[:, 3072:4096]
        tmpc = scr[:, 4096:4608]
        ysb = scr[:, 4608:5120]
        aT = scb(5120, 5632).rearrange("p (f n) -> p f n", f=4)
        rl = scr[:, 5632:5888]
        junk = sb("junk", [128, D])
        xn = sb("xn", [128, D], BF16)
        ss = sb("ss", [128, 4])
        ssh = sb("ssh", [128, 4, 16])
        KTs = scb(5088, 5600).rearrange("p (k n) -> p k n", k=8)
        Vs_bf = scb(5600, 6112)

        def a2b(lo, hi):
            return arena2[:, lo:hi].bitcast(BF16)
        wb = a2b(0, 4096).rearrange("p (a b c) -> p a b c", a=NJ, b=2)
        cx = a2b(4096, 8192).rearrange("p (a b c) -> p a b c", a=NJ, b=2)
        bufs = [[arena2[:, 8192 + (pp * 2 + ri) * 768: 8192 + (pp * 2 + ri + 1) * 768] for ri in range(2)]
                for pp in range(2)]
        sbufS = [[arena2[:, 11264 + (pp * 2 + ri) * 192: 11264 + (pp * 2 + ri + 1) * 192]
                  .rearrange("p (b t) -> p b t", t=12) for ri in range(2)] for pp in range(2)]
        ygT = a2b(12032, 14080).rearrange("p (k n) -> p k n", k=8)
        h0 = arena2[:, 14080:15104].rearrange("p (a b c) -> p a b c", a=NJ, b=2)
        st_s_sb = arena2[:, 15104:16128].rearrange("p (a b c) -> p a b c", a=NJ, b=2)
        uT_all = a2b(0, 8704).rearrange("p (k n) -> p k n", k=8)
        KT = a2b(0, 8192).rearrange("p (k n) -> p k n", k=8)
        Vp = a2b(8192, 16512).rearrange("p (t h d) -> p t h d", t=16, h=16)
        rows_i = arena2[:, 8812:9068].bitcast(I32)
        pt_i = arena2[:, 9068:9324].bitcast(I32)
        S5RES = ['wb', 'cx', 'bufA', 'bufB', 'ygT', 'h0', 'res']
        wglu = arena[:, :].rearrange("p (k n) -> p k n", k=8)
        wkv = wglu

        def wup_v(s):
            return arena[:, s * 8192: s * 8192 + 4096].rearrange("p (k n) -> p k n", k=8)

        def wdn_v(s):
            return arena[:, s * 8192 + 4096: s * 8192 + 8192].rearrange("p (f n) -> p f n", f=4)

        pPr, pPi = Bk[0], Bk[1]
        pT = Bk[2][:, :].bitcast(BF16).rearrange("p (k n) -> p k n", k=8)
        pT2 = Bk[3][:, :].bitcast(BF16).rearrange("p (k n) -> p k n", k=8)
        yT = Bk[3]
        zb = Bk[4:8]

        def mkident(e):
            e.memset(identf[:], 0.0)
            e.memset(dummy[:], 0.0)
            return e.affine_select(identf[:], identf[:], [[-1, 128]], ALU.not_equal, 1.0,
                                   base=0, channel_multiplier=1)
        P.add('pool', mkident, writes=['identf'])
        P.add('dve', lambda e: e.tensor_copy(ident[:], identf[:]), reads=['identf'], writes=['ident'])

        P.add('sync', lambda e: [e.dma_start(out=lam[:], in_=lam_d[:, :]),
                                 e.dma_start(out=cols[:], in_=cols_d[:, :]),
                                 e.dma_start(out=knb[:], in_=knb_d[:, :]),
                                 e.dma_start(out=rope[:].rearrange("p a b -> p (a b)"), in_=rope_d[:, :]),
                                 e.dma_start(out=h0.rearrange("p a b c -> p (a b c)"), in_=h0_d[:, :])],
              writes=['lam', 'cols', 'knb', 'rope', 'h0'], dma=('params', 5))
        P.add('pool', lambda e: [e.dma_start(out=wb[:, 8 * q:8 * q + 8].rearrange("p a b c -> p (a b c)"),
                                             in_=wb_d[:, 2048 * q:2048 * (q + 1)]) for q in range(4)],
              writes=['wb'], dma=('wbl', 4))
        P.add('pool', lambda e: [e.dma_start(out=cx[:, 8 * q:8 * q + 8].rearrange("p a b c -> p (a b c)"),
                                             in_=cx_d[:, 2048 * q:2048 * (q + 1)]) for q in range(4)],
              writes=['cx'], dma=('cx', 4))
        if STAGE >= 2:
            P.add('pool', lambda e: [e.dma_start(out=wglu[:, kc, :], in_=wglu_d[:, 2048 * kc:2048 * (kc + 1)])
                                     for kc in range(8)], writes=['arena'], dma=('arena', 8))
        for t in range(16):
            P.add('sync', (lambda t: lambda e: [e.dma_start(out=h[:, t, :], in_=xp[t * 128:(t + 1) * 128, :])])(t),
                  writes=[('h', t)], dma=('h%d' % t, 1))
        P.add('sync', lambda e: [e.dma_start(out=h[:, 16, :], in_=xs[:, :])], writes=[('h', 16)], dma=('h16', 1))

        lr, li, ld = lam[:, 0:NJ], lam[:, NJ:2 * NJ], lam[:, 2 * NJ:3 * NJ]
        dtv, rho, th, mag, kf, r_, s2, s4, c2, den = [t[:] for t in tsm]
        TWO_PI = float(2 * np.pi)
        MAGIC = 12582912.0
        P.add('act', lambda e: e.activation(dtv, ld, AF.Exp), reads=['lam'], writes=['dtv'])
        P.add('dve', lambda e: e.tensor_tensor(rho, lr, dtv, ALU.mult), reads=['lam', 'dtv'], writes=['rho'])
        P.add('dve', lambda e: e.tensor_tensor(th, li, dtv, ALU.mult), reads=['lam', 'dtv'], writes=['th'])
        P.add('act', lambda e: e.activation(mag, rho, AF.Exp), reads=['rho'], writes=['mag'])

        def rr(e):
            e.tensor_scalar(kf, th, 1.0 / TWO_PI, None, ALU.mult)
            e.tensor_scalar(kf, kf, MAGIC, None, ALU.add)
            e.tensor_scalar(kf, kf, -MAGIC, None, ALU.add)
            return e.scalar_tensor_tensor(r_, kf, -TWO_PI, th, ALU.mult, ALU.add)
        P.add('dve', rr, reads=['th'], writes=['r', 'kf'])
        P.add('act', lambda e: e.activation(s2, r_, AF.Sin, scale=0.5), reads=['r'], writes=['s2'])
        P.add('act', lambda e: e.activation(s4, r_, AF.Sin, scale=0.25), reads=['r'], writes=['s4'])

        def trig(e):
            e.tensor_tensor(c2, s4, s4, ALU.mult)
            e.tensor_scalar(c2, c2, -2.0, 1.0, ALU.mult, ALU.add)
            e.tensor_tensor(den, s2, s2, ALU.mult)
            e.tensor_scalar(den, den, -2.0, 1.0, ALU.mult, ALU.add)
            e.tensor_tensor(apr[:, 0, :], den, mag, ALU.mult)
            e.tensor_tensor(c2, c2, s2, ALU.mult)
            e.scalar_tensor_tensor(api[:, 0, :], c2, 2.0, mag, ALU.mult, ALU.mult)
            nr, tmp1, tmp2 = kf, r_, th
            e.tensor_scalar(nr, apr[:, 0, :], -1.0, None, ALU.add)
            e.tensor_tensor(den, lr, lr, ALU.mult)
            e.tensor_tensor(tmp1, li, li, ALU.mult)
            e.tensor_tensor(den, den, tmp1, ALU.add)
            e.reciprocal(den, den)
            e.tensor_tensor(tmp1, nr, lr, ALU.mult)
            e.tensor_tensor(tmp2, api[:, 0, :], li, ALU.mult)
            e.tensor_tensor(tmp1, tmp1, tmp2, ALU.add)
            e.tensor_tensor(fr[:], tmp1, den, ALU.mult)
            e.tensor_tensor(tmp1, api[:, 0, :], lr, ALU.mult)
            e.tensor_tensor(tmp2, nr, li, ALU.mult)
            e.tensor_tensor(tmp1, tmp1, tmp2, ALU.subtract)
            e.tensor_tensor(fi[:], tmp1, den, ALU.mult)
            for k in range(1, NPW):
                e.tensor_tensor(tmp1, apr[:, k - 1, :], apr[:, k - 1, :], ALU.mult)
                e.tensor_tensor(tmp2, api[:, k - 1, :], api[:, k - 1, :], ALU.mult)
                e.tensor_tensor(apr[:, k, :], tmp1, tmp2, ALU.subtract)
                e.scalar_tensor_tensor(api[:, k, :], apr[:, k - 1, :], 2.0, api[:, k - 1, :], ALU.mult, ALU.mult)
            return e.tensor_scalar(apn[:].rearrange("p a b -> p (a b)"), api[:].rearrange("p a b -> p (a b)"),
                                   -1.0, None, ALU.mult)
        P.add('dve', trig, reads=['s2', 's4', 'mag', 'lam', 'r', 'th', 'kf'], writes=['apw', 'f', 'r', 'th', 'kf'])

        def rms_to_uT(tt, dst, col0, gcol, dst_res, extra_w=()):
            def f1(e):
                e.scalar_tensor_tensor(xn[:], h[:, tt, :], 1.0, h[:, tt, :], ALU.mult, ALU.mult,
                                       accum_out=ss[:, 0:1])
                return e.tensor_scalar(ss[:, 1:2], ss[:, 0:1], 1.0 / D, EPS, ALU.mult, ALU.add)
            P.add('dve', f1, reads=[('h', tt)], writes=['ss', 'xn'])
            P.add('act', lambda e: e.activation(ss[:, 2:3], ss[:, 1:2], AF.Sqrt), reads=['ss'], writes=['ss2'])

            def f2(e):
                e.reciprocal(ss[:, 3:4], ss[:, 2:3])
                return e.tensor_scalar(xn[:], h[:, tt, :], ss[:, 3:4], None, ALU.mult)
            P.add('dve', f2, reads=['ss2', ('h', tt)], writes=['xn', 'ss'])

            def f3(e):
                r = None
                for kc in range(8):
                    r = e.transpose(pT[:, kc, :], xn[:, kc * 128:(kc + 1) * 128], ident[:])
                return r
            P.add('pe', f3, reads=['xn', 'ident'], writes=['B2'])

            def f4(e):
                r = None
                for kc in range(8):
                    r = e.activation(dst[:, kc, col0:col0 + 128], pT[:, kc, :], AF.Copy,
                                     scale=cols[:, gcol * 8 + kc:gcol * 8 + kc + 1])
                return r
            P.add('act', f4, reads=['B2', 'cols'], writes=[dst_res] + list(extra_w))

        LCH = 512
        PAD = 256

        def zero_bufs(e):
            return e.memset(arena2[:, 8192:12032], 0.0)
        P.add('pool', zero_bufs, writes=['bufA', 'bufB'])

        def s5_chunk(ci):
            sample = (ci == 4)
            L = 128 if sample else LCH
            tts = [16] if sample else [ci * 4 + q for q in range(4)]
            for q, tt in enumerate(tts):
                rms_to_uT(tt, uT, q * 128, 0, 'uT')
            for j in range(NJ):
                kc, j4 = j // 4, j % 4

                def bproj(e, j=j, kc=kc):
                    r = None
                    for ri, pp in enumerate((pPr, pPi)):
                        r = e.matmul(pp[:, 0:L], wb[:, j, ri, :], uT[:, kc, 0:L], start=True, stop=True)
                    return r
                P.add('pe', bproj, reads=['uT', 'wb'], writes=['B0', 'B1'])

                if sample:
                    A = [sbufS[0][ri][:, :, 4:12] for ri in range(2)]
                    pv = [pp[:, 0:128].rearrange("p (b t) -> p b t", t=8) for pp in (pPr, pPi)]
                    tv = tmpc[:, 0:128].rearrange("p (b t) -> p b t", t=8)
                else:
                    A = [bufs[0][ri][:, PAD:PAD + L] for ri in range(2)]
                    pv = [pp[:, 0:L] for pp in (pPr, pPi)]
                    tv = tmpc[:, 0:L]

                if sample:
                    tva = junk2[:, 0:128].rearrange("p (b t) -> p b t", t=8)
                    tvb = junk2[:, 512:640].rearrange("p (b t) -> p b t", t=8)
                else:
                    tva, tvb = junk2[:, 0:L], junk2[:, 512:512 + L]

                def evac_a(e, j=j, pv=pv, tva=tva, tvb=tvb):
                    e.activation(tva, pv[1], AF.Copy, scale=fi[:, j:j + 1])
                    return e.activation(tvb, pv[1], AF.Copy, scale=fr[:, j:j + 1])
                P.add('act', evac_a, reads=['B1', 'f'], writes=['tvab'])

                def evac(e, j=j, A=A, pv=pv, tva=tva, tvb=tvb):
                    frj, fij = fr[:, j:j + 1], fi[:, j:j + 1]
                    e.scalar_tensor_tensor(A[0], pv[0], frj, tva, ALU.mult, ALU.subtract)
                    e.nosync = True
                    r = e.scalar_tensor_tensor(A[1], pv[0], fij, tvb, ALU.mult, ALU.add)
                    e.nosync = False
                    a_r, a_i, a_n = apr[:, 0, j:j + 1], api[:, 0, j:j + 1], apn[:, 0, j:j + 1]
                    if sample:
                        c0 = [sbufS[0][ri][:, :, 4:5] for ri in range(2)]
                        er = h0[:, j, 0, :].unsqueeze(2)
                        ei = h0[:, j, 1, :].unsqueeze(2)
                    elif ci > 0:
                        c0 = [bufs[0][ri][:, PAD:PAD + 1] for ri in range(2)]
                        er, ei = xend[:, j, 0:1], xend[:, j, 1:2]
                    else:
                        return r
                    e.scalar_tensor_tensor(c0[0], er, a_r, c0[0], ALU.mult, ALU.add)
                    e.scalar_tensor_tensor(c0[0], ei, a_n, c0[0], ALU.mult, ALU.add)
                    e.scalar_tensor_tensor(c0[1], ei, a_r, c0[1], ALU.mult, ALU.add)
                    r = e.scalar_tensor_tensor(c0[1], er, a_i, c0[1], ALU.mult, ALU.add)
                    return r
                P.add('dve', evac, reads=['B0', 'B1', 'f', 'apw', 'xend', 'h0', 'tvab'], writes=['bufA'])

                nsteps = 3 if sample else 9
                assert nsteps % 2 == 1

                def scan(e, j=j):
                    r = None
                    cur = 0
                    for k in range(nsteps):
                        s = 1 << k
                        a_r, a_i, a_n = apr[:, k, j:j + 1], api[:, k, j:j + 1], apn[:, k, j:j + 1]
                        if sample:
                            X = [sbufS[cur][ri][:, :, 4:12] for ri in range(2)]
                            Xs = [sbufS[cur][ri][:, :, 4 - s:12 - s] for ri in range(2)]
                            Y = [sbufS[1 - cur][ri][:, :, 4:12] for ri in range(2)]
                        else:
                            X = [bufs[cur][ri][:, PAD:PAD + L] for ri in range(2)]
                            Xs = [bufs[cur][ri][:, PAD - s:PAD + L - s] for ri in range(2)]
                            Y = [bufs[1 - cur][ri][:, PAD:PAD + L] for ri in range(2)]
                        e.scalar_tensor_tensor(Y[0], Xs[0], a_r, X[0], ALU.mult, ALU.add)
                        if not sample:
                            e.nosync = True
                        e.scalar_tensor_tensor(Y[0], Xs[1], a_n, Y[0], ALU.mult, ALU.add)
                        e.scalar_tensor_tensor(Y[1], Xs[1], a_r, X[1], ALU.mult, ALU.add)
                        r = e.scalar_tensor_tensor(Y[1], Xs[0], a_i, Y[1], ALU.mult, ALU.add)
                        cur = 1 - cur
                    e.nosync = False
                    assert cur == 1
                    return r
                P.add('dve', scan, reads=['bufA', 'apw'], writes=['bufA', 'bufB'])

                if STAGE < 2:
                    continue
                pb = j % 2
                if sample:
                    R = [sbufS[1][ri][:, :, 4:12] for ri in range(2)]
                    ov = [t[:, 0:128].rearrange("p (b t) -> p b t", t=8) for t in (xrb[pb], xib[pb])]
                else:
                    R = [bufs[1][ri][:, PAD:PAD + L] for ri in range(2)]
                    ov = [xrb[pb][:, 0:L], xib[pb][:, 0:L]]

                def conv(e, R=R, ov=ov, j=j):
                    if sample:
                        for ri in range(2):
                            e.activation(st_s_sb[:, j, ri, :].unsqueeze(2), sbufS[1][ri][:, :, 11:12], AF.Copy)
                    else:
                        for ri in range(2):
                            e.activation(xend[:, j, ri:ri + 1], bufs[1][ri][:, PAD + L - 1:PAD + L], AF.Copy)
                    e.activation(ov[0], R[0], AF.Copy)
                    return e.activation(ov[1], R[1], AF.Copy, scale=-1.0)
                P.add('act', conv, reads=['bufB'], writes=['xb%d' % pb, 'xend', 'res'])

                def cproj(e, j=j, j4=j4, pb=pb):
                    e.matmul(yT[:, 0:L], cx[:, j, 0, :], xrb[pb][:, 0:L], start=(j4 == 0), stop=False)
                    return e.matmul(yT[:, 0:L], cx[:, j, 1, :], xib[pb][:, 0:L], start=False, stop=(j4 == 3))
                P.add('pe', cproj, reads=['xb%d' % pb, 'cx'], writes=['B3'])

                if j4 == 3:
                    def yfin(e, kc=kc):
                        return e.scalar_tensor_tensor(ysb[:, 0:L], uT[:, kc, 0:L], cols[:, 8 + kc:9 + kc],
                                                      yT[:, 0:L], ALU.mult, ALU.add)
                    P.add('dve', yfin, reads=['B3', 'uT', 'cols'], writes=['ysb'])
                    P.add('act', (lambda kc: lambda e: e.activation(ygT[:, kc, 0:L], ysb[:, 0:L],
                                                                    AF.Gelu_apprx_tanh))(kc),
                          reads=['ysb'], writes=['ygT'])

            if STAGE < 2:
                return
            for q, tt in enumerate(tts):
                def glu_mm(e, q=q):
                    r = None
                    for n in range(4):
                        for kc in range(8):
                            r = e.matmul(zb[n][:, :], ygT[:, kc, q * 128:(q + 1) * 128],
                                         wglu[:, kc, n * 512:(n + 1) * 512], start=(kc == 0), stop=(kc == 7))
                    return r
                P.add('pe', glu_mm, reads=['ygT', 'arena'], writes=['B4', 'B5', 'B6', 'B7'])

                def glu_sig(e):
                    e.activation(junk[:, 0:512], zb[2][:, :], AF.Sigmoid)
                    return e.activation(junk[:, 512:1024], zb[3][:, :], AF.Sigmoid)
                P.add('act', glu_sig, reads=['B6', 'B7'], writes=['junk'])

                def glu_fin(e, tt=tt):
                    e.tensor_tensor(junk[:, 0:512], zb[0][:, :], junk[:, 0:512], ALU.mult)
                    e.tensor_tensor(junk[:, 512:1024], zb[1][:, :], junk[:, 512:1024], ALU.mult)
                    return e.tensor_tensor(h[:, tt, :], h[:, tt, :], junk[:], ALU.add)
                P.add('dve', glu_fin, reads=['junk', 'B4', 'B5', ('h', tt)],
                      writes=['junk', ('h', tt), 'B4', 'B5', 'B6', 'B7'])

        for ci in range(5):
            s5_chunk(ci)

        P.add('sync', lambda e: [e.dma_start(out=st_p[:, :], in_=xend[:].rearrange("p a b -> p (a b)")),
                                 e.dma_start(out=st_s[:, :], in_=st_s_sb.rearrange("p a b c -> p (a b c)"))],
              reads=['xend', 'res'], dma=('out', 2))

        def mlp(l, gcol, first_extra):
            for tt in range(17):
                rms_to_uT(tt, uT_all, tt * 128, gcol, 'uT_all', extra_w=(first_extra if tt == 0 else ()))
            groups = [(g * 256, 256, [2 * g, 2 * g + 1]) for g in range(8)] + [(2048, 128, [16])]
            for e8 in range(8):
                s = e8 % 2

                def wload(e, e8=e8, s=s):
                    return [e.dma_start(out=wup_v(s),
                                        in_=wup_d[l].rearrange("p (k n) -> p k n", k=8)[:, :, e8 * 512:(e8 + 1) * 512]),
                            e.dma_start(out=wdn_v(s),
                                        in_=wdn_d[l].rearrange("p (f n) -> p f n", f=32)[:, e8 * 4:(e8 + 1) * 4, :])]
                P.add('pool', wload, writes=['arena%d' % s] + (['arena'] if e8 < 2 else []),
                      dma=('arena%d' % s, 2))
                for (c0, N, tiles) in groups:
                    for f in range(4):
                        ub = Bk[f % 2]

                        def up_mm(e, f=f, ub=ub, c0=c0, N=N, s=s):
                            r = None
                            for kc in range(8):
                                r = e.matmul(ub[:, 0:N], wup_v(s)[:, kc, f * 128:(f + 1) * 128],
                                             uT_all[:, kc, c0:c0 + N], start=(kc == 0), stop=(kc == 7))
                            return r
                        P.add('pe', up_mm, reads=['uT_all', 'arena%d' % s], writes=['B%d' % (f % 2)])

                        def relu2(e, f=f, ub=ub, N=N):
                            e.activation(rl[:, 0:N], ub[:, 0:N], AF.Relu)
                            return e.activation(aT[:, f, 0:N], rl[:, 0:N], AF.Square)
                        P.add('act', relu2, reads=['B%d' % (f % 2)], writes=[('aT', f), 'rl'])
                    for ti, tt in enumerate(tiles):
                        dn = [Bk[4 + 2 * ti], Bk[5 + 2 * ti]]

                        def dn_mm(e, ti=ti, dn=dn, s=s):
                            r = None
                            for hf in range(2):
                                for f in range(4):
                                    r = e.matmul(dn[hf][:, :], aT[:, f, ti * 128:(ti + 1) * 128],
                                                 wdn_v(s)[:, f, hf * 512:(hf + 1) * 512],
                                                 start=(f == 0), stop=(f == 3))
                            return r
                        P.add('pe', dn_mm, reads=[('aT', f) for f in range(4)] + ['arena%d' % s],
                              writes=['B%d' % (4 + 2 * ti), 'B%d' % (5 + 2 * ti)])

                        def dn_add(e, tt=tt, dn=dn):
                            e.tensor_tensor(h[:, tt, 0:512], h[:, tt, 0:512], dn[0][:, :], ALU.add)
                            return e.tensor_tensor(h[:, tt, 512:1024], h[:, tt, 512:1024], dn[1][:, :], ALU.add)
                        P.add('dve', dn_add, reads=['B%d' % (4 + 2 * ti), 'B%d' % (5 + 2 * ti), ('h', tt)],
                              writes=[('h', tt), 'B%d' % (4 + 2 * ti), 'B%d' % (5 + 2 * ti)])

        if STAGE >= 3:
            mlp(0, 2, tuple(S5RES))

            P.add('pool', lambda e: [e.dma_start(out=wkv[:, kc, :], in_=wkv_d[:, 2048 * kc:2048 * (kc + 1)])
                                     for kc in range(8)], writes=['arena', 'arena0', 'arena1'], dma=('arena', 8))

            def ones_col(e):
                return e.memset(Vp[:, :, :, 64:65], 1.0)
            P.add('pool', ones_col, writes=['Vp', 'uT_all'])

            def kv_tile(tt):
                rms_to_uT(tt, uT, 0, 3, 'uT')

                def kv_mm(e):
                    r = None
                    for n in range(4):
                        for kc in range(8):
                            r = e.matmul(zb[n][:, :], uT[:, kc, 0:128], wkv[:, kc, n * 512:(n + 1) * 512],
                                         start=(kc == 0), stop=(kc == 7))
                    return r
                P.add('pe', kv_mm, reads=['uT', 'arena'], writes=['B4', 'B5', 'B6', 'B7'])

                kraw = junk
                vraw = junk2

                def kv_evac(e):
                    e.activation(kraw[:, 0:512], zb[0][:, :], AF.Copy)
                    e.activation(kraw[:, 512:1024], zb[1][:, :], AF.Copy)
                    e.activation(vraw[:, 0:512], zb[2][:, :], AF.Copy)
                    return e.activation(vraw[:, 512:1024], zb[3][:, :], AF.Copy)
                P.add('act', kv_evac, reads=['B4', 'B5', 'B6', 'B7'], writes=['junk', 'junk2', 'B4', 'B5', 'B6', 'B7'])

                def head_norm_rope(e, X, gain_off, tt=tt):
                    X3 = X.rearrange("p (h d) -> p h d", h=16)
                    sq = tmpc[:, :].rearrange("p (h d) -> p h d", h=16)
                    for half in range(2):
                        e.tensor_tensor(sq, X3[:, :, half * 32:(half + 1) * 32], X3[:, :, half * 32:(half + 1) * 32],
                                        ALU.mult)
                        e.tensor_reduce(ssh[:, half, :], sq, AX.X, ALU.add)
                    e.tensor_tensor(ssh[:, 2, :], ssh[:, 0, :], ssh[:, 1, :], ALU.add)
                    return e.tensor_scalar(ssh[:, 2, :], ssh[:, 2, :], 1.0 / 64, EPS, ALU.mult, ALU.add)

                def k_norm1(e):
                    return head_norm_rope(e, kraw[:, :], 0)
                P.add('dve', k_norm1, reads=['junk'], writes=['ssh', 'tmpc'])
                P.add('act', lambda e: e.activation(ssh[:, 3, :], ssh[:, 2, :], AF.Sqrt), reads=['ssh'], writes=['ssh3'])

                def k_norm2(e, tt=tt):
                    X3 = kraw[:, :].rearrange("p (h d) -> p h d", h=16)
                    e.reciprocal(ssh[:, 2, :], ssh[:, 3, :])
                    e.tensor_tensor(X3, X3, ssh[:, 2, :].unsqueeze(2).to_broadcast([128, 16, 64]), ALU.mult)
                    e.tensor_tensor(X3, X3, knb[:, 0:64].unsqueeze(1).to_broadcast([128, 16, 64]), ALU.mult)
                    cs = rope[:, tt, 0:8].unsqueeze(1).to_broadcast([128, 16, 8])
                    sn = rope[:, tt, 8:16].unsqueeze(1).to_broadcast([128, 16, 8])
                    t1 = tmpc[:, 0:128].rearrange("p (h d) -> p h d", h=16)
                    t2 = tmpc[:, 128:256].rearrange("p (h d) -> p h d", h=16)
                    t3 = tmpc[:, 256:384].rearrange("p (h d) -> p h d", h=16)
                    x1, x2 = X3[:, :, 0:8], X3[:, :, 8:16]
                    e.tensor_tensor(t1, x1, sn, ALU.mult)
                    e.tensor_tensor(t2, x2, sn, ALU.mult)
                    e.tensor_tensor(t3, x1, cs, ALU.mult)
                    e.tensor_tensor(x1, t3, t2, ALU.subtract)
                    e.tensor_tensor(t3, x2, cs, ALU.mult)
                    e.tensor_tensor(x2, t3, t1, ALU.add)
                    return e.tensor_copy(xn[:], kraw[:, :])
                P.add('dve', k_norm2, reads=['ssh3', 'junk', 'knb', 'rope'], writes=['junk', 'ssh', 'tmpc', 'xn'])

                if tt < 16:
                    P.add('sync', (lambda tt: lambda e: [
                        e.dma_start(out=k_p[tt * 128:(tt + 1) * 128, :], in_=kraw[:, :]),
                        e.dma_start(out=v_p[tt * 128:(tt + 1) * 128, :], in_=vraw[:, :])])(tt),
                        reads=['junk', 'junk2'], dma=('kvout', 2))

                    def kT_tr(e):
                        r = None
                        for hp in range(8):
                            r = e.transpose(pT2[:, hp, :], xn[:, hp * 128:(hp + 1) * 128], ident[:])
                        return r
                    P.add('pe', kT_tr, reads=['xn', 'ident'], writes=['B3'])
                    P.add('act', (lambda tt: lambda e: e.activation(KT[:, :, tt * 128:(tt + 1) * 128], pT2[:, :, :],
                                                                    AF.Copy))(tt),
                          reads=['B3'], writes=['KT', 'B3'])
                    P.add('pool', (lambda tt: lambda e: e.tensor_copy(
                        Vp[:, tt, :, 0:64], vraw[:, :].rearrange("p (h d) -> p h d", h=16)))(tt),
                        reads=['junk2'], writes=['Vp'])
                else:
                    P.add('sync', lambda e: [e.dma_start(out=k_s[:, :], in_=kraw[:, :]),
                                             e.dma_start(out=v_s[:, :], in_=vraw[:, :])],
                          reads=['junk', 'junk2'], dma=('kvout', 2))

                    def kT_tr_s(e):
                        r = None
                        for hp in range(8):
                            r = e.transpose(pT2[:, hp, :], xn[:, hp * 128:(hp + 1) * 128], ident[:])
                        return r
                    P.add('pe', kT_tr_s, reads=['xn', 'ident'], writes=['B3'])
                    P.add('act', lambda e: e.activation(KTs[:, :, :], pT2[:, :, :], AF.Copy), reads=['B3'],
                          writes=['KTs', 'B3'])
                    P.add('pool', lambda e: e.tensor_copy(Vs_bf[:], vraw[:, :]), reads=['junk2'], writes=['Vs_bf'])
            for tt in range(17):
                kv_tile(tt)

            P.barrier(['arena', 'arena0', 'arena1', 'scr_attn'] + ['S%d' % i for i in range(4)] + ['O%d' % i for i in range(4)] + ['PT%d' % i for i in range(4)])
            wq = arena[:, 0:8192].rearrange("p (k n) -> p k n", k=8)
            wo = arena[:, 8192:16384].rearrange("p (k n) -> p k n", k=8)
            P.add('pool', lambda e: [e.dma_start(out=wq[:, kc, :], in_=wq_d[:, 1024 * kc:1024 * (kc + 1)])
                                     for kc in range(8)] +
                                    [e.dma_start(out=wo[:, kc, :], in_=wo_d[:, 1024 * kc:1024 * (kc + 1)])
                                     for kc in range(8)],
                  writes=['arena'], dma=('arena', 16))
            uTq = scb(0, 512).rearrange("p (k n) -> p k n", k=8)
            qT = scb(512, 1024).rearrange("p (k n) -> p k n", k=8)
            PTb = [scb(1024 + 128 * i, 1152 + 128 * i).rearrange("p (k n) -> p k n", k=2) for i in range(2)] + \
                  [scb(128 * i, 128 * i + 128).rearrange("p (k n) -> p k n", k=2) for i in range(1)]
            Sb = [Bk[0][:, 0:256], Bk[1][:, 0:256], Bk[3][:, 0:256]]
            Sn = ['B0', 'B1', 'B3']
            Ob = [Bk[6][:, 0:65], Bk[7][:, 0:65], Bk[5][:, 0:65]]
            On_ = ['B6', 'B7', 'B5']
            NBUF = 3
            acc = scr[:, 1280:2320].rearrange("p (h d) -> p h d", h=16)
            Gs = scr[:, 2320:2448].rearrange("p (h n) -> p h n", h=16)
            sel = scr[:, 2448:2576].rearrange("p (h n) -> p h n", h=16)
            cmpb = scr[:, 2576:3600]
            rnk = scr[:, 3600:3728]
            kmT = scr[:, 3728:3792]
            kmBD = scb(3792, 3856).rearrange("p (k a n) -> p k a n", k=8, a=2)
            maskT = scb(3856, 3920)
            rden = scr[:, 3920:3936]
            obf = scb(3936, 4448)
            oT = scb(4448, 4960).rearrange("p (k n) -> p k n", k=8)
            maskf = scr[:, 4960:5088]

            def mk_mask(e):
                e.memset(maskf, 1.0)
                e.affine_select(maskf, maskf, [[1, 128]], ALU.is_ge, 0.0, base=0, channel_multiplier=-1)
                return e.tensor_copy(maskT, maskf)
            P.add('pool', mk_mask, writes=['maskT', 'scr_attn'])

            def mk_kmean(e):
                e.tensor_reduce(kmT, KT.rearrange("p k (n s) -> p (k n) s", s=256), AX.X, ALU.add)
                e.memset(kmBD, 0.0)
                km3 = kmT.rearrange("p (k n) -> p k n", k=8)
                e.tensor_scalar(kmBD[0:64, :, 0, :], km3[0:64], 1.0 / 256, None, ALU.mult)
                return e.tensor_scalar(kmBD[64:128, :, 1, :], km3[64:128], 1.0 / 256, None, ALU.mult)
            P.add('dve', mk_kmean, reads=['KT', 'scr_attn'], writes=['kmBD'])

            def head_norm(Xap, gain_lo, tt, post_scale, tag):
                X3 = Xap.rearrange("p (h d) -> p h d", h=16)

                def n1(e):
                    sq = cmpb[:, 0:512].rearrange("p (h d) -> p h d", h=16)
                    for half in range(2):
                        e.tensor_tensor(sq, X3[:, :, half * 32:(half + 1) * 32], X3[:, :, half * 32:(half + 1) * 32],
                                        ALU.mult)
                        e.tensor_reduce(ssh[:, half, :], sq, AX.X, ALU.add)
                    e.tensor_tensor(ssh[:, 2, :], ssh[:, 0, :], ssh[:, 1, :], ALU.add)
                    return e.tensor_scalar(ssh[:, 2, :], ssh[:, 2, :], 1.0 / 64, EPS, ALU.mult, ALU.add)
                P.add('dve', n1, reads=[tag], writes=['ssh', 'cmpb'])
                P.add('act', lambda e: e.activation(ssh[:, 3, :], ssh[:, 2, :], AF.Sqrt), reads=['ssh'], writes=['ssh3'])

                def n2(e):
                    e.reciprocal(ssh[:, 2, :], ssh[:, 3, :])
                    e.tensor_tensor(X3, X3, ssh[:, 2, :].unsqueeze(2).to_broadcast([128, 16, 64]), ALU.mult)
                    e.tensor_tensor(X3, X3, knb[:, gain_lo:gain_lo + 64].unsqueeze(1).to_broadcast([128, 16, 64]),
                                    ALU.mult)
                    cs = rope[:, tt, 0:8].unsqueeze(1).to_broadcast([128, 16, 8])
                    sn = rope[:, tt, 8:16].unsqueeze(1).to_broadcast([128, 16, 8])
                    t1 = cmpb[:, 0:128].rearrange("p (h d) -> p h d", h=16)
                    t2 = cmpb[:, 128:256].rearrange("p (h d) -> p h d", h=16)
                    t3 = cmpb[:, 256:384].rearrange("p (h d) -> p h d", h=16)
                    x1, x2 = X3[:, :, 0:8], X3[:, :, 8:16]
                    e.tensor_tensor(t1, x1, sn, ALU.mult)
                    e.tensor_tensor(t2, x2, sn, ALU.mult)
                    e.tensor_tensor(t3, x1, cs, ALU.mult)
                    e.tensor_tensor(x1, t3, t2, ALU.subtract)
                    e.tensor_tensor(t3, x2, cs, ALU.mult)
                    e.tensor_tensor(x2, t3, t1, ALU.add)
                    return e.tensor_scalar(xn[:], Xap, post_scale, None, ALU.mult)
                P.add('dve', n2, reads=['ssh3', tag, 'knb', 'rope'], writes=[tag, 'ssh', 'cmpb', 'xn'])

            def q_for_tile(tt, dstT, dst_res):
                rms_to_uT(tt, uTq, 0, 4, 'uTq', extra_w=('PT2',))

                def q_mm(e):
                    r = None
                    for n in range(2):
                        for kc in range(8):
                            r = e.matmul(Bk[4 + n][:, :], uTq[:, kc, :], wq[:, kc, n * 512:(n + 1) * 512],
                                         start=(kc == 0), stop=(kc == 7))
                    return r
                P.add('pe', q_mm, reads=['uTq', 'arena', 'PT2'], writes=['B4', 'B5'])

                def q_evac(e):
                    e.activation(junk[:, 0:512], Bk[4][:, :], AF.Copy)
                    return e.activation(junk[:, 512:1024], Bk[5][:, :], AF.Copy)
                P.add('act', q_evac, reads=['B4', 'B5'], writes=['junk', 'B4', 'B5'])
                head_norm(junk[:, :], 64, tt, 0.125, 'junk')

                def q_tr(e):
                    r = None
                    for hp in range(8):
                        r = e.transpose(pT[:, hp, :], xn[:, hp * 128:(hp + 1) * 128], ident[:])
                    return r
                P.add('pe', q_tr, reads=['xn', 'ident'], writes=['B2'])
                P.add('act', lambda e: e.activation(dstT, pT[:, :, :], AF.Copy), reads=['B2'], writes=[dst_res, 'B2'])

            def wo_and_residual(tt, oT_ap, oT_res):
                def wo_mm(e):
                    r = None
                    for n in range(2):
                        for hp in range(8):
                            r = e.matmul(Bk[4 + n][:, :], oT_ap[:, hp, :], wo[:, hp, n * 512:(n + 1) * 512],
                                         start=(hp == 0), stop=(hp == 7))
                    return r
                P.add('pe', wo_mm, reads=[oT_res, 'arena'], writes=['B4', 'B5'])

                def wo_add(e):
                    e.tensor_tensor(h[:, tt, 0:512], h[:, tt, 0:512], Bk[4][:, :], ALU.add)
                    return e.tensor_tensor(h[:, tt, 512:1024], h[:, tt, 512:1024], Bk[5][:, :], ALU.add)
                P.add('dve', wo_add, reads=['B4', 'B5', ('h', tt)], writes=[('h', tt), 'B4', 'B5'])

            def attn_prompt_chunk(j):
                own = j // 2
                q_for_tile(j, qT[:, :, :], 'qT')
                if own > 3:
                    def gate_mm(e):
                        r = None
                        for hp in range(8):
                            r = e.matmul(Bk[3][:, hp * 16:(hp + 1) * 16], qT[:, hp, :],
                                         kmBD[:, hp, :, :].rearrange("p a n -> p (a n)"), start=True, stop=True)
                        return r
                    P.add('pe', gate_mm, reads=['qT', 'kmBD'], writes=['B3'])
                    P.add('act', lambda e: e.activation(Gs.rearrange("p h n -> p (h n)"), Bk[3][:, 0:128], AF.Copy),
                          reads=['B3'], writes=['Gs', 'B3'])

                    def selop(e):
                        g = Gs[:, :, 0:own]
                        c4 = cmpb[:, 0:16 * own * own].rearrange("p (h n m) -> p h n m", h=16, n=own)
                        e.tensor_tensor(c4, g.unsqueeze(2).to_broadcast([128, 16, own, own]),
                                        g.unsqueeze(3).to_broadcast([128, 16, own, own]), ALU.is_gt)
                        r3 = rnk[:, 0:16 * own].rearrange("p (h n) -> p h n", h=16)
                        e.tensor_reduce(r3, c4, AX.X, ALU.add)
                        return e.tensor_scalar(sel[:, :, 0:own], r3, 3.0, None, ALU.is_lt)
                    P.add('dve', selop, reads=['Gs'], writes=['sel', 'cmpb', 'rnk'])
                units = []
                for hh in range(16):
                    units.append((hh, own))
                    for n in range(own):
                        units.append((hh, n))

                def emit_qk(u, idx):
                    hh, n = u
                    hp, h2 = hh // 2, hh % 2
                    pb = idx % NBUF
                    tiles = [2 * n, 2 * n + 1] if n < own else list(range(2 * own, j + 1))

                    def qk(e):
                        r = None
                        for i, kt in enumerate(tiles):
                            r = e.matmul(Sb[pb][:, i * 128:(i + 1) * 128],
                                         KT[64 * h2:64 * h2 + 64, hp, kt * 128:(kt + 1) * 128],
                                         qT[64 * h2:64 * h2 + 64, hp, :], start=True, stop=True)
                        return r
                    P.add('pe', qk, reads=['KT', 'qT'], writes=[Sn[pb]])
                    nt = len(tiles)
                    P.add('act', lambda e: e.activation(PTb[pb][:, 0:nt, :].rearrange("p k n -> p (k n)"),
                                                        Sb[pb][:, 0:nt * 128], AF.Exp),
                          reads=[Sn[pb]], writes=['PT%d' % pb, Sn[pb]])
                    if n == own:
                        P.add('pool', lambda e: e.tensor_tensor(PTb[pb][:, nt - 1, :], PTb[pb][:, nt - 1, :], maskT,
                                                                ALU.mult),
                              reads=['PT%d' % pb, 'maskT'], writes=['PT%d' % pb])
                    return tiles

                def emit_pv(u, idx, tiles):
                    hh, n = u
                    pb = idx % NBUF
                    ob = Ob[pb]

                    def pv(e):
                        r = None
                        for i, kt in enumerate(tiles):
                            r = e.matmul(ob, PTb[pb][:, i, :], Vp[:, kt, hh, :], start=(i == 0),
                                         stop=(i == len(tiles) - 1))
                        return r
                    P.add('pe', pv, reads=['PT%d' % pb, 'Vp'], writes=[On_[pb]])
                    if n == own:
                        P.add('dve', lambda e: e.tensor_copy(acc[:, hh, :], ob), reads=[On_[pb]],
                              writes=['acc', On_[pb]])
                    elif own <= 3:
                        P.add('dve', lambda e: e.tensor_tensor(acc[:, hh, :], acc[:, hh, :], ob, ALU.add),
                              reads=[On_[pb], 'acc'], writes=['acc', On_[pb]])
                    else:
                        P.add('dve', lambda e: e.scalar_tensor_tensor(acc[:, hh, :], ob, sel[:, hh, n:n + 1],
                                                                      acc[:, hh, :], ALU.mult, ALU.add),
                              reads=[On_[pb], 'acc', 'sel'], writes=['acc', On_[pb]])
                pend = []
                for idx, u in enumerate(units):
                    tiles = emit_qk(u, idx)
                    pend.append((u, idx, tiles))
                    if len(pend) > 2:
                        emit_pv(*pend.pop(0))
                while pend:
                    emit_pv(*pend.pop(0))

                def fin(e):
                    e.reciprocal(rden, acc[:, :, 64:65].rearrange("p h o -> p (h o)"))
                    return e.tensor_tensor(obf.rearrange("p (h d) -> p h d", h=16), acc[:, :, 0:64],
                                           rden.unsqueeze(2).to_broadcast([128, 16, 64]), ALU.mult)
                P.add('dve', fin, reads=['acc'], writes=['obf', 'rden'])

                def o_tr(e):
                    r = None
                    for hp in range(8):
                        r = e.transpose(pT[:, hp, :], obf[:, hp * 128:(hp + 1) * 128], ident[:])
                    return r
                P.add('pe', o_tr, reads=['obf', 'ident'], writes=['B2'])
                P.add('act', lambda e: e.activation(oT[:, :, :], pT[:, :, :], AF.Copy), reads=['B2'],
                      writes=['oT', 'B2'])
                wo_and_residual(j, oT, 'oT')

            for j in range(16):
                attn_prompt_chunk(j)

            SN = ['qbd', 'Odl', 'ksum', 'kmTs', 'gTs', 'gate', 'sels', 'Vnb', 'PTo', 'ob2', 'oTs', 'maskh', 'masko',
                  'rows', 'accs', 'kTp0', 'kTp1', 'PTs0', 'PTs1', 'Kpg0', 'Kpg1', 'Kpg2', 'Vpg0', 'Vpg1', 'Vpg2',
                  'pt_i', 'rows_f', 'iota', 'ones', 'par', 'PTof', 'obT', 'B3l', 'B3o', 'B3g', 'B3t']
            P.barrier(SN)
            qbd = a2b(0, 1024).rearrange("p (b k a t) -> p b k a t", b=16, k=8, a=2)
            Kpg = [a2b(1024 + 512 * i, 1536 + 512 * i) for i in range(3)]
            Vpg = [a2b(2560 + 512 * i, 3072 + 512 * i) for i in range(3)]
            kTp = [a2b(4096 + 512 * i, 4608 + 512 * i).rearrange("p (k n) -> p k n", k=8) for i in range(2)]
            PTs = [a2b(5120 + 64 * i, 5184 + 64 * i) for i in range(2)]
            Odl = arena2[:, 5248:5833].rearrange("p (n d) -> p n d", n=9)
            ksum = arena2[:, 5840:5968].rearrange("p (n g k) -> p n g k", n=8, g=2)
            kmTs = a2b(5968, 6000).rearrange("p (k n) -> p k n", k=8)
            gTs = arena2[0:8, 6000:6128]
            gate = arena2[:, 6128:6136]
            cmps = arena2[:, 6136:6200].rearrange("p (n m) -> p n m", n=8)
            ranks = arena2[:, 6200:6208]
            sels = arena2[:, 6208:6216]
            Vnb = a2b(6216, 6728)
            PTof = arena2[0:8, 6728:6856]
            PTo = a2b(6856, 6920)
            ob2 = a2b(6920, 6984)
            oTs = a2b(6984, 7496).rearrange("p (k n) -> p k n", k=8)
            maskh = arena2[:, 7496:7512]
            par = arena2[:, 7512:7514]
            masko = arena2[0:8, 7514:7642]
            rows_f = arena2[:, 7642:7898]
            rm = arena2[:, 7898:7900]
            accs = arena2[:, 7900:7965]
            tmp9 = arena2[:, 7968:8488].rearrange("p (n d) -> p n d", n=8)
            ones_bf = a2b(8488, 8489)
            obT = a2b(8490, 8554)
            iota_p = arena2[:, 8554:8555]
            pt_f = arena2[:, 8556:8812]

            def s_consts(e):
                e.memset(maskh, 1.0)
                e.affine_select(maskh, maskh, [[-8, 16]], ALU.is_ge, 0.0, base=0, channel_multiplier=1)
                e.affine_select(maskh, maskh, [[8, 16]], ALU.is_ge, 0.0, base=7, channel_multiplier=-1)
                e.memset(masko, 1.0)
                e.affine_select(masko, masko, [[0, 16], [1, 8]], ALU.is_ge, 0.0, base=0, channel_multiplier=-1)
                e.memset(ones_bf, 1.0)
                e.memset(qbd.rearrange("p b k a t -> p (b k a t)"), 0.0)
                return e.iota(iota_p, [[0, 1]], base=0, channel_multiplier=1, allow_small_or_imprecise_dtypes=True)
            P.add('pool', s_consts, writes=['maskh', 'masko', 'ones', 'qbd', 'iota'])
            P.add('dve', lambda e: e.tensor_reduce(par, maskh.rearrange("p (k a) -> p a k", a=2), AX.X, ALU.add),
                  reads=['maskh'], writes=['par'])
            P.add('sync', lambda e: [e.dma_start(out=pt_i[:], in_=pt_d[0:1, :].partition_broadcast(128))],
                  writes=['pt_i'], dma=('pt', 1))

            def mk_rows(e):
                e.tensor_copy(pt_f, pt_i[:])
                return e.tensor_scalar(rows_f, pt_f, 128.0, iota_p, ALU.mult, ALU.add)
            P.add('dve', mk_rows, reads=['pt_i', 'iota'], writes=['rows_f'])
            P.add('pool', lambda e: e.tensor_copy(rows_i[:], rows_f), reads=['rows_f'], writes=['rows'])

            q_for_tile(16, qT[:, :, :], 'qT')

            def mk_qbd(e):
                e.tensor_copy(qbd[0:64, :, :, 0, :], qT[0:64].rearrange("p k (b t) -> p b k t", t=8))
                return e.tensor_copy(qbd[64:128, :, :, 1, :], qT[64:128].rearrange("p k (b t) -> p b k t", t=8))
            P.add('dve', mk_qbd, reads=['qT', 'qbd'], writes=['qbd'])

            def diag_extract(n, par_n):
                ba, bb = (Bk[4], Bk[5]) if par_n == 0 else (Bk[6], Bk[7])
                rn = ['B4', 'B5'] if par_n == 0 else ['B6', 'B7']

                def f(e):
                    j3 = junk[:, :].rearrange("p (h d) -> p h d", h=16)
                    e.tensor_tensor(j3[:, 0:8, :], ba[:, :].rearrange("p (h d) -> p h d", h=8),
                                    maskh[:, 0:8].unsqueeze(2).to_broadcast([128, 8, 64]), ALU.mult)
                    e.tensor_tensor(j3[:, 8:16, :], bb[:, :].rearrange("p (h d) -> p h d", h=8),
                                    maskh[:, 8:16].unsqueeze(2).to_broadcast([128, 8, 64]), ALU.mult)
                    e.tensor_reduce(Odl[:, n, 0:64], junk[:, :].rearrange("p (h d) -> p d h", h=16), AX.X, ALU.add)
                    return e.tensor_copy(Odl[:, n, 64:65], Bk[3][:, n:n + 1])
                P.add('dve', f, reads=rn + ['B3l', 'maskh'], writes=['junk', 'Odl'] + rn + ['B3l'])

            def sample_batch(b):
                def vsel(e):
                    e.matmul(Bk[4][0:8, :], ident[:, b * 8:(b + 1) * 8], Vs_bf[:, 0:512], start=True, stop=True)
                    return e.matmul(Bk[5][0:8, :], ident[:, b * 8:(b + 1) * 8], Vs_bf[:, 512:1024], start=True,
                                    stop=True)
                P.add('pe', vsel, reads=['ident', 'Vs_bf'], writes=['B4', 'B5'])

                def vsel_ev(e):
                    e.activation(Vnb[0:8, 0:512], Bk[4][0:8, :], AF.Copy)
                    return e.activation(Vnb[0:8, 512:1024], Bk[5][0:8, :], AF.Copy)
                P.add('act', vsel_ev, reads=['B4', 'B5'], writes=['Vnb', 'B4', 'B5'])

                for n in range(8):
                    par_n = n % 2
                    On = (Bk[4], Bk[5]) if par_n == 0 else (Bk[6], Bk[7])
                    rn = ['B4', 'B5'] if par_n == 0 else ['B6', 'B7']
                    for pg in range(2):
                        c = b * 16 + 2 * n + pg
                        s3, s2 = c % 3, c % 2
                        P.add('pool', (lambda c, s3: lambda e: [e.indirect_dma_start(
                            out=Kpg[s3], out_offset=None, in_=ck_d,
                            in_offset=bass.IndirectOffsetOnAxis(ap=rows_i[:, c:c + 1], axis=0))])(c, s3),
                            reads=['rows'], writes=['Kpg%d' % s3], dma=('kpg%d' % s3, 1))
                        P.add('pool', (lambda c, s3: lambda e: [e.indirect_dma_start(
                            out=Vpg[s3], out_offset=None, in_=cv_d,
                            in_offset=bass.IndirectOffsetOnAxis(ap=rows_i[:, c:c + 1], axis=0))])(c, s3),
                            reads=['rows'], writes=['Vpg%d' % s3], dma=('vpg%d' % s3, 1))

                        def ktr(e, s3=s3):
                            r = None
                            for hp in range(8):
                                r = e.transpose(pT[:, hp, :], Kpg[s3][:, hp * 128:(hp + 1) * 128], ident[:])
                            return r
                        P.add('pe', ktr, reads=['Kpg%d' % s3, 'ident'], writes=['B2'])
                        P.add('act', (lambda s2: lambda e: e.activation(kTp[s2][:, :, :], pT[:, :, :], AF.Copy))(s2),
                              reads=['B2'], writes=['kTp%d' % s2, 'B2'])
                        P.add('dve', (lambda s2, n, pg: lambda e: e.tensor_reduce(ksum[:, n, pg, :], kTp[s2][:, :, :],
                                                                                 AX.X, ALU.add))(s2, n, pg),
                              reads=['kTp%d' % s2], writes=['ksum'])

                        def st_mm(e, s2=s2):
                            r = None
                            for hp in range(8):
                                r = e.matmul(Bk[s2][:, hp * 16:(hp + 1) * 16], kTp[s2][:, hp, :],
                                             qbd[:, b, hp, :, :].rearrange("p a t -> p (a t)"), start=True, stop=True)
                            return r
                        P.add('pe', st_mm, reads=['kTp%d' % s2, 'qbd'], writes=['B%d' % s2])
                        P.add('act', (lambda s2: lambda e: e.activation(PTs[s2], Bk[s2][:, 0:128], AF.Exp))(s2),
                              reads=['B%d' % s2], writes=['PTs%d' % s2, 'B%d' % s2])

                        def pv_mm(e, s2=s2, s3=s3, pg=pg, On=On, n=n):
                            e.matmul(On[0][:, :], PTs[s2], Vpg[s3][:, 0:512], start=(pg == 0), stop=(pg == 1))
                            e.matmul(On[1][:, :], PTs[s2], Vpg[s3][:, 512:1024], start=(pg == 0), stop=(pg == 1))
                            return e.matmul(Bk[3][:, n:n + 1], PTs[s2], ones_bf[:, 0:1], start=(pg == 0),
                                            stop=(pg == 1))
                        P.add('pe', pv_mm, reads=['PTs%d' % s2, 'Vpg%d' % s3, 'ones'], writes=rn + ['B3l'])
                    diag_extract(n, par_n)

                def own_st(e):
                    r = None
                    for hp in range(8):
                        r = e.matmul(Bk[3][0:8, 128 + hp * 16:128 + (hp + 1) * 16], KTs[:, hp, b * 8:(b + 1) * 8],
                                     qbd[:, b, hp, :, :].rearrange("p a t -> p (a t)"), start=True, stop=True)
                    return r
                P.add('pe', own_st, reads=['KTs', 'qbd'], writes=['B3o'])
                P.add('act', lambda e: e.activation(PTof, Bk[3][0:8, 128:256], AF.Exp), reads=['B3o'],
                      writes=['PTof', 'B3o'])
                P.add('dve', lambda e: e.tensor_tensor(PTo[0:8, :], PTof, masko, ALU.mult), reads=['PTof', 'masko'],
                      writes=['PTo'])

                def own_pv(e):
                    e.matmul(Bk[4][:, :], PTo[0:8, :], Vnb[0:8, 0:512], start=True, stop=True)
                    e.matmul(Bk[5][:, :], PTo[0:8, :], Vnb[0:8, 512:1024], start=True, stop=True)
                    return e.matmul(Bk[3][:, 8:9], PTo[0:8, :], ones_bf[0:8, 0:1], start=True, stop=True)
                P.add('pe', own_pv, reads=['PTo', 'Vnb', 'ones'], writes=['B4', 'B5', 'B3l'])
                diag_extract(8, 0)

                def mk_km(e):
                    return e.tensor_tensor(kmTs.rearrange("p k n -> p n k"), ksum[:, :, 0, :], ksum[:, :, 1, :],
                                           ALU.add)
                P.add('dve', mk_km, reads=['ksum'], writes=['kmTs'])

                def g_mm(e):
                    r = None
                    for hp in range(8):
                        r = e.matmul(Bk[3][0:8, 256 + hp * 16:256 + (hp + 1) * 16], kmTs[:, hp, :],
                                     qbd[:, b, hp, :, :].rearrange("p a t -> p (a t)"), start=True, stop=True)
                    return r
                P.add('pe', g_mm, reads=['kmTs', 'qbd'], writes=['B3g'])
                P.add('act', lambda e: e.activation(gTs, Bk[3][0:8, 256:384], AF.Copy), reads=['B3g'],
                      writes=['gTs', 'B3g'])
                P.add('pe', lambda e: e.transpose(Bk[3][:, 400:408], gTs, identf[0:8, 0:8]), reads=['gTs', 'identf'],
                      writes=['B3t'])

                def selop(e):
                    e.tensor_copy(gate, Bk[3][:, 400:408])
                    e.tensor_tensor(cmps, gate.unsqueeze(1).to_broadcast([128, 8, 8]),
                                    gate.unsqueeze(2).to_broadcast([128, 8, 8]), ALU.is_gt)
                    e.tensor_reduce(ranks, cmps, AX.X, ALU.add)
                    return e.tensor_scalar(sels, ranks, 3.0, None, ALU.is_lt)
                P.add('dve', selop, reads=['B3t'], writes=['gate', 'sels', 'B3t'])

                def combine(e):
                    e.tensor_tensor(tmp9, Odl[:, 0:8, :], sels.unsqueeze(2).to_broadcast([128, 8, 65]), ALU.mult)
                    e.tensor_reduce(accs, tmp9.rearrange("p n d -> p d n"), AX.X, ALU.add)
                    e.tensor_tensor(accs, accs, Odl[:, 8, :], ALU.add)
                    e.reciprocal(rm[:, 0:1], accs[:, 64:65])
                    e.tensor_scalar(rm[:, 1:2], par[:, 1:2], rm[:, 0:1], None, ALU.mult)
                    e.tensor_scalar(rm[:, 0:1], par[:, 0:1], rm[:, 0:1], None, ALU.mult)
                    e.tensor_scalar(ob2[:, 0:64], accs[:, 0:64], rm[:, 0:1], None, ALU.mult)
                    return e.tensor_scalar(ob2[:, 64:128], accs[:, 0:64], rm[:, 1:2], None, ALU.mult)
                P.add('dve', combine, reads=['Odl', 'sels', 'par'], writes=['ob2', 'accs'])
                P.add('pe', lambda e: e.transpose(pT[:, 0, :], ob2, ident[:]), reads=['ob2', 'ident'], writes=['B2'])
                P.add('act', lambda e: e.activation(obT, pT[:, 0, :], AF.Copy), reads=['B2'], writes=['obT', 'B2'])

                def o_place(e):
                    T4 = obT.rearrange("p (k a t) -> p k a t", k=8, a=2)
                    return e.tensor_tensor(oTs[:, :, b * 8:(b + 1) * 8], T4[:, :, 0, :], T4[:, :, 1, :], ALU.add)
                P.add('dve', o_place, reads=['obT'], writes=['oTs'])

            for b in range(16):
                sample_batch(b)
            wo_and_residual(16, oTs, 'oTs')

            P.barrier(['arena', 'arena0', 'arena1', 'uT_all', 'rl'] + [('aT', f) for f in range(4)])
            mlp(1, 5, ())

        for tt in range(17):
            if tt < 16:
                P.add('sync', (lambda tt: lambda e: [e.dma_start(out=y_p[tt * 128:(tt + 1) * 128, :], in_=h[:, tt, :])])(tt),
                      reads=[('h', tt)], dma=('yout', 1))
            else:
                P.add('sync', lambda e: [e.dma_start(out=y_s[:, :], in_=h[:, 16, :])], reads=[('h', 16)],
                      dma=('yout', 1))

        cum = P.emit(lambda n: es.enter_context(nc.semaphore(n.replace(':', '_'))),
                     lambda e: e.memset(dummy[:, 0:1], 0.0), counts)
    return nc, cum


_NC_CACHE = {}


def _rope_table():
    half = 8
    inv = (np.float32(500000.0) ** (-np.arange(half, dtype=np.float32) / np.float32(half))).astype(np.float32)
    tab = np.zeros((128, 17, 16), np.float32)
    for tt in range(17):
        if tt < 16:
            pos = (tt * 128 + np.arange(128)).astype(np.float32)
        else:
            pos = (2048 + (np.arange(128) % 8)).astype(np.float32)
        ang = pos[:, None] * inv[None, :]
        tab[:, tt, 0:8] = np.cos(ang)
        tab[:, tt, 8:16] = np.sin(ang)
    return tab.reshape(128, -1)


def _host_shared(inp):
    f32 = np.float32
    m = {}
    wbm = np.zeros((4, 2, 16, NJ, 2, 2, 64), f32)
    for ri, key in enumerate(["ssm_b_re", "ssm_b_im"]):
        B5 = inp[key][0].reshape(NJ, 2, 64, 16)
        for j in range(NJ):
            for g2 in range(2):
                wbm[j % 4, g2, :, j, ri, g2, :] = B5[j, g2].T
    m["wb"] = wbm.reshape(128, -1)
    cxm = np.zeros((2, 64, NJ, 2, 4, 2, 16), f32)
    for ri, key in enumerate(["ssm_c_re", "ssm_c_im"]):
        C5 = inp[key][0].reshape(NJ, 2, 16, 64)
        for j in range(NJ):
            for g2 in range(2):
                cxm[g2, :, j, ri, j % 4, g2, :] = C5[j, g2].T
    m["cx"] = cxm.reshape(128, -1)
    lamm = np.zeros((128, 3 * NJ), f32)
    lamm[:, 0:NJ] = inp["ssm_lambda_re"][0].reshape(NJ, 2, 64).transpose(1, 2, 0).reshape(128, NJ)
    lamm[:, NJ:2 * NJ] = inp["ssm_lambda_im"][0].reshape(NJ, 2, 64).transpose(1, 2, 0).reshape(128, NJ)
    lamm[:, 2 * NJ:] = np.broadcast_to(inp["ssm_log_dt"][0].reshape(NJ, 2).T[:, None, :], (2, 64, NJ)).reshape(128, NJ)
    m["lam"] = lamm
    cols = np.zeros((128, 48), f32)
    for i, v in enumerate([inp["ssm_norm"][0], inp["ssm_d"][0], inp["mlp_norm"][0], inp["kv_norm"],
                           inp["attn_norm"][0], inp["mlp_norm"][1]]):
        cols[:, 8 * i:8 * i + 8] = v.reshape(8, 128).T
    m["cols"] = cols
    knb = np.zeros((128, 128), f32)
    knb[:, 0:64] = inp["k_norm"][None, :]
    knb[:, 64:128] = inp["q_norm"][0][None, :]
    m["knb"] = knb
    m["rope"] = _rope_table()

    def kmaj(w):
        K_, N_ = w.shape
        return np.ascontiguousarray(w.reshape(K_ // 128, 128, N_).transpose(1, 0, 2).reshape(128, -1))
    m["wglu"] = kmaj(inp["ssm_w_glu"][0])
    m["wkv"] = kmaj(inp["w_kv"])
    m["wq"] = kmaj(inp["w_q"][0])
    m["wo"] = kmaj(inp["w_o"][0])
    for l in range(2):
        m["wup%d" % l] = kmaj(inp["w_up"][l])
        m["wdn%d" % l] = kmaj(inp["w_down"][l])
    return m


def _host_core(inp, c):
    f32 = np.float32
    m = {}
    m["xp"] = np.ascontiguousarray(inp["x_prompt"][c])
    m["xs"] = np.ascontiguousarray(inp["x_sample"][16 * c:16 * c + 16].reshape(128, D))
    h0m = np.zeros((2, 64, NJ, 2, 16), f32)
    for ri, key in enumerate(["state_ssm_re", "state_ssm_im"]):
        S = inp[key][0, 16 * c:16 * c + 16].reshape(16, NJ, 2, 64)
        h0m[:, :, :, ri, :] = S.transpose(2, 3, 1, 0)
    m["h0"] = h0m.reshape(128, -1)
    m["pt"] = np.ascontiguousarray(inp["page_table"][16 * c:16 * c + 16].reshape(1, 256).astype(np.int32))
    m["ck"] = inp["cache_k"].reshape(NPOOL * 128, 1024)
    m["cv"] = inp["cache_v"].reshape(NPOOL * 128, 1024)
    return m


def kernel(**inp):
    inp = {k: np.asarray(v) for k, v in inp.items()}
    if "nc" not in _NC_CACHE:
        _, cum = build_nc(None)
        _NC_CACHE["nc"], _ = build_nc(cum)
    nc = _NC_CACHE["nc"]
    shared = _host_shared(inp)
    in_maps = []
    for c in range(NCORE):
        m = dict(shared)
        m.update(_host_core(inp, c))
        in_maps.append(m)
    res = run_bass_kernel_spmd(nc, in_maps, core_ids=list(range(NCORE)))
    R = res.results
    f32 = np.float32
    y_prompt = np.zeros((8, 2048, D), f32)
    y_sample = np.zeros((128, 8, D), f32)
    k_prompt = np.zeros((8, 2048, 16, 64), f32)
    v_prompt = np.zeros((8, 2048, 16, 64), f32)
    k_sample = np.zeros((128, 8, 16, 64), f32)
    v_sample = np.zeros((128, 8, 16, 64), f32)
    srp = np.zeros((1, 8, G, PST), f32)
    sip = np.zeros((1, 8, G, PST), f32)
    srs = np.zeros((1, 128, G, PST), f32)
    sis = np.zeros((1, 128, G, PST), f32)
    for c in range(NCORE):
        r = R[c]
        y_prompt[c] = np.asarray(r["y_p"])
        y_sample[16 * c:16 * c + 16] = np.asarray(r["y_s"]).reshape(16, 8, D)
        k_prompt[c] = np.asarray(r["k_p"]).reshape(2048, 16, 64)
        v_prompt[c] = np.asarray(r["v_p"]).reshape(2048, 16, 64)
        k_sample[16 * c:16 * c + 16] = np.asarray(r["k_s"]).reshape(16, 8, 16, 64)
        v_sample[16 * c:16 * c + 16] = np.asarray(r["v_s"]).reshape(16, 8, 16, 64)
        sp = np.asarray(r["st_p"]).reshape(2, 64, NJ, 2)
        srp[0, c] = sp[..., 0].transpose(2, 0, 1).reshape(G, PST)
        sip[0, c] = sp[..., 1].transpose(2, 0, 1).reshape(G, PST)
        s_ = np.asarray(r["st_s"]).reshape(2, 64, NJ, 2, 16)
        srs[0, 16 * c:16 * c + 16] = s_[:, :, :, 0, :].transpose(3, 2, 0, 1).reshape(16, G, PST)
        sis[0, 16 * c:16 * c + 16] = s_[:, :, :, 1, :].transpose(3, 2, 0, 1).reshape(16, G, PST)
    return (y_prompt, y_sample, k_prompt, v_prompt, k_sample, v_sample, srp, sip, srs, sis)
```
